# Optimizing a Trainium2 kernel written in Bass

```python
import math
import jax, jax.numpy as jnp
from jax import lax
import numpy as np

D_MODEL = 2048
BATCH = 4
SEQ = 2048
DEPTH = 2
DEC_BATCH = 128
DEC_SEQ = 4
PAST_LEN = 16384
PAGE_SIZE = 128

N_BRANCH = 3
D_BR = D_MODEL // 2
D_A = D_BR
H_A = 4
DH_A = D_A // H_A
CHUNK = 128
D_CONV = D_BR
CONV_K = 31
D_POOL = D_BR
POOL_WINDOWS = (2, 4, 8, 16)
N_POOL = len(POOL_WINDOWS)
G_POOL = D_POOL // N_POOL
POOL_PAST = max(POOL_WINDOWS) - 1
MEM_LEN = 256
X_HEADS = 4
X_HEAD_DIM = D_MODEL // X_HEADS
D_FF = 4 * D_MODEL
D_IN = 2 * D_A + 2 * D_CONV + D_POOL + N_BRANCH * D_MODEL
SPLITS = (D_A, 2 * D_A, 2 * D_A + 2 * D_CONV, 2 * D_A + 2 * D_CONV + D_POOL)
RMS_EPS = 1e-6
LN_EPS = 1e-5

kernel_name = "gated_hybrid_chunkmlp_conv_pool_decoder_step"


def rms_norm(x, g):
    xf = x.astype(jnp.float32)
    y = xf * lax.rsqrt(jnp.mean(xf * xf, axis=-1, keepdims=True) + RMS_EPS)
    return (y * g.astype(jnp.float32)).astype(x.dtype)


def layer_norm(x, g, b):
    xf = x.astype(jnp.float32)
    mu = jnp.mean(xf, axis=-1, keepdims=True)
    xc = xf - mu
    var = jnp.mean(xc * xc, axis=-1, keepdims=True)
    y = xc * lax.rsqrt(var + LN_EPS) * g.astype(jnp.float32) + b.astype(jnp.float32)
    return y.astype(x.dtype)


def chunk_mixer(u, v, ln_g, ln_b, w_s, b_s):
    u = jax.nn.gelu(u)
    v = layer_norm(jax.nn.gelu(v), ln_g, ln_b)
    B, S, _ = v.shape
    L = min(S, CHUNK)
    vh = v.reshape(B, S // L, L, H_A, DH_A)
    w = jnp.tril(w_s[:, :L, :L]).astype(v.dtype)
    mixed = jnp.einsum('hts,bcshd->bcthd', w, vh)
    mixed = mixed + jnp.transpose(b_s[:, :L]).astype(v.dtype)[None, None, :, :, None]
    return u * mixed.reshape(B, S, D_A), v


def conv_module(c, past, conv_w, conv_b, ln_g, ln_b):
    a, gt = jnp.split(c, 2, axis=-1)
    glu = a * jax.nn.sigmoid(gt)
    full = jnp.concatenate([past.astype(glu.dtype), glu], axis=1)
    out = lax.conv_general_dilated(
        full, conv_w.astype(full.dtype)[:, None, :], window_strides=(1,), padding='VALID',
        dimension_numbers=('NWC', 'WIO', 'NWC'), feature_group_count=D_CONV)
    out = out + conv_b.astype(out.dtype)
    y = jax.nn.silu(layer_norm(out, ln_g, ln_b))
    return y, full[:, -(CONV_K - 1):]


def pool_mixer(p, past, pos0, pool_w, pool_scale):
    B, S, _ = p.shape
    P = POOL_PAST
    full = jnp.concatenate([past.astype(p.dtype), p], axis=1)
    cs = jnp.cumsum(full.astype(jnp.float32), axis=1)
    cs = jnp.pad(cs, ((0, 0), (1, 0), (0, 0)))
    end = cs[:, P + 1:P + 1 + S]
    pos = pos0 + jnp.arange(S)
    outs = []
    for gi, w in enumerate(POOL_WINDOWS):
        sl = slice(gi * G_POOL, (gi + 1) * G_POOL)
        start = cs[:, P + 1 - w:P + 1 - w + S, sl]
        cnt = jnp.minimum(w, pos + 1).astype(jnp.float32)[None, :, None]
        outs.append((end[..., sl] - start) / cnt)
    pooled = jnp.stack(outs, axis=2)
    mixed = (pooled - p.reshape(B, S, N_POOL, G_POOL).astype(jnp.float32)).astype(p.dtype)
    y = jnp.einsum('bsgc,gcd->bsgd', mixed, pool_w.astype(p.dtype)).reshape(B, S, D_POOL)
    return y * pool_scale.astype(y.dtype), full[:, -P:]


def mem_kv(mem, g_mem, w_k, w_v):
    B, M, _ = mem.shape
    m = rms_norm(mem, g_mem)
    k = (m @ w_k).reshape(B, M, X_HEADS, X_HEAD_DIM)
    v = (m @ w_v).reshape(B, M, X_HEADS, X_HEAD_DIM)
    return k, v


def cross_attn(h, k, v, w_q, w_o):
    B, S, _ = h.shape
    q = (h @ w_q).reshape(B, S, X_HEADS, X_HEAD_DIM)
    s = jnp.einsum('bqhd,bkhd->bhqk', q, k.astype(q.dtype)).astype(jnp.float32) * (X_HEAD_DIM ** -0.5)
    pr = jax.nn.softmax(s, axis=-1).astype(q.dtype)
    o = jnp.einsum('bhqk,bkhd->bqhd', pr, v.astype(q.dtype)).reshape(B, S, D_MODEL)
    return o @ w_o


def decoder_layer(x, mk, mv, conv_past, pool_past, pos0,
                  g_mix, w_in, ln_v_g, ln_v_b, w_s, b_s, conv_w, conv_b, ln_c_g, ln_c_b,
                  pool_w, pool_scale, w_branch, w_out, g_xattn, w_xq, w_xo, g_mlp, w_up, w_down):
    B, S, _ = x.shape
    h = rms_norm(x, g_mix)
    z = h @ w_in
    u, v, c, p, gate = jnp.split(z, SPLITS, axis=-1)
    y_a, v_state = chunk_mixer(u, v, ln_v_g, ln_v_b, w_s, b_s)
    y_b, conv_state = conv_module(c, conv_past, conv_w, conv_b, ln_c_g, ln_c_b)
    y_c, pool_state = pool_mixer(p, pool_past, pos0, pool_w, pool_scale)
    ys = jnp.stack([y_a, y_b, y_c], axis=2)
    br = jnp.einsum('bsnc,ncd->bsnd', ys, w_branch)
    gates = jax.nn.sigmoid(gate.reshape(B, S, N_BRANCH, D_MODEL))
    merged = jnp.sum(gates * br, axis=2)
    x = x + merged @ w_out
    x = x + cross_attn(rms_norm(x, g_xattn), mk, mv, w_xq, w_xo)
    hm = rms_norm(x, g_mlp)
    x = x + jnp.square(jax.nn.relu(hm @ w_up)) @ w_down
    return x, v_state, conv_state, pool_state


def setup_inputs(seed: int = 0) -> dict:
    key = jax.random.key(seed)
    ks = iter(jax.random.split(key, 40))

    def nrm(shape, scale):
        return jax.random.normal(next(ks), shape, jnp.float32) * scale

    def gain(shape):
        return 1.0 + nrm(shape, 0.05)

    return {
        "x_prompt": nrm((BATCH, SEQ, D_MODEL), 1.0),
        "x_sample": nrm((DEC_BATCH, DEC_SEQ, D_MODEL), 1.0),
        "mem_prompt": nrm((BATCH, MEM_LEN, D_MODEL), 1.0),
        "cache_mem_k": nrm((DEPTH, DEC_BATCH, MEM_LEN, X_HEADS, X_HEAD_DIM), 1.0),
        "cache_mem_v": nrm((DEPTH, DEC_BATCH, MEM_LEN, X_HEADS, X_HEAD_DIM), 1.0),
        "state_conv": nrm((DEPTH, DEC_BATCH, CONV_K - 1, D_CONV), 0.5),
        "state_pool": nrm((DEPTH, DEC_BATCH, POOL_PAST, D_POOL), 1.0),
        "g_mix": gain((DEPTH, D_MODEL)),
        "w_in": nrm((DEPTH, D_MODEL, D_IN), D_MODEL ** -0.5),
        "ln_v_g": gain((DEPTH, D_A)),
        "ln_v_b": nrm((DEPTH, D_A), 0.02),
        "w_s": nrm((DEPTH, H_A, CHUNK, CHUNK), CHUNK ** -0.5),
        "b_s": 1.0 + nrm((DEPTH, H_A, CHUNK), 0.1),
        "conv_w": nrm((DEPTH, CONV_K, D_CONV), CONV_K ** -0.5),
        "conv_b": nrm((DEPTH, D_CONV), 0.02),
        "ln_c_g": gain((DEPTH, D_CONV)),
        "ln_c_b": nrm((DEPTH, D_CONV), 0.02),
        "pool_w": nrm((DEPTH, N_POOL, G_POOL, G_POOL), G_POOL ** -0.5),
        "pool_scale": 1.0 + nrm((DEPTH, D_POOL), 0.1),
        "w_branch": nrm((DEPTH, N_BRANCH, D_BR, D_MODEL), D_BR ** -0.5),
        "w_out": nrm((DEPTH, D_MODEL, D_MODEL), D_MODEL ** -0.5),
        "g_xattn": gain((DEPTH, D_MODEL)),
        "g_mem": gain((DEPTH, D_MODEL)),
        "w_xq": nrm((DEPTH, D_MODEL, D_MODEL), D_MODEL ** -0.5),
        "w_xk": nrm((DEPTH, D_MODEL, D_MODEL), D_MODEL ** -0.5),
        "w_xv": nrm((DEPTH, D_MODEL, D_MODEL), D_MODEL ** -0.5),
        "w_xo": nrm((DEPTH, D_MODEL, D_MODEL), D_MODEL ** -0.5),
        "g_mlp": gain((DEPTH, D_MODEL)),
        "w_up": nrm((DEPTH, D_MODEL, D_FF), D_MODEL ** -0.5),
        "w_down": nrm((DEPTH, D_FF, D_MODEL), D_FF ** -0.5),
        "g_final": gain((D_MODEL,)),
    }


def reference(x_prompt, x_sample, mem_prompt, cache_mem_k, cache_mem_v, state_conv, state_pool,
              g_mix, w_in, ln_v_g, ln_v_b, w_s, b_s, conv_w, conv_b, ln_c_g, ln_c_b,
              pool_w, pool_scale, w_branch, w_out, g_xattn, g_mem, w_xq, w_xk, w_xv, w_xo,
              g_mlp, w_up, w_down, g_final):
    xp, xs = x_prompt, x_sample
    Bp = xp.shape[0]
    mk_p, mv_p, conv_p, pool_p = [], [], [], []
    conv_s, pool_s, chunk_s = [], [], []
    for l in range(DEPTH):
        lw = (g_mix[l], w_in[l], ln_v_g[l], ln_v_b[l], w_s[l], b_s[l], conv_w[l], conv_b[l],
              ln_c_g[l], ln_c_b[l], pool_w[l], pool_scale[l], w_branch[l], w_out[l],
              g_xattn[l], w_xq[l], w_xo[l], g_mlp[l], w_up[l], w_down[l])
        mk, mv = mem_kv(mem_prompt, g_mem[l], w_xk[l], w_xv[l])
        zc = jnp.zeros((Bp, CONV_K - 1, D_CONV), xp.dtype)
        zp = jnp.zeros((Bp, POOL_PAST, D_POOL), xp.dtype)
        xp, _, cp, pp = decoder_layer(xp, mk, mv, zc, zp, 0, *lw)
        mk_p.append(mk); mv_p.append(mv); conv_p.append(cp); pool_p.append(pp)
        xs, vs, cs, ps = decoder_layer(xs, cache_mem_k[l], cache_mem_v[l], state_conv[l],
                                       state_pool[l], PAST_LEN, *lw)
        conv_s.append(cs); pool_s.append(ps); chunk_s.append(vs)
    y_prompt = rms_norm(xp, g_final)
    y_sample = rms_norm(xs, g_final)
    new_mem_k_prompt = jnp.stack(mk_p)
    new_mem_v_prompt = jnp.stack(mv_p)
    new_conv_prompt = jnp.stack(conv_p)
    new_pool_prompt = jnp.stack(pool_p)
    new_conv_sample = jnp.stack(conv_s)
    new_pool_sample = jnp.stack(pool_s)
    new_chunk_v_sample = jnp.stack(chunk_s)
    return (y_prompt, y_sample, new_mem_k_prompt, new_mem_v_prompt, new_conv_prompt,
            new_pool_prompt, new_conv_sample, new_pool_sample, new_chunk_v_sample)
```

```python
import numpy as np
from contextlib import ExitStack
import concourse.bass as bass
import concourse.mybir as mybir
from concourse.bass_utils import run_bass_kernel_spmd

F32 = mybir.dt.float32
BF16 = mybir.dt.bfloat16
AF = mybir.ActivationFunctionType
ALU = mybir.AluOpType
AX = mybir.AxisListType

D = 2048
KC = 16
NL = 2
NTM = 640
MAIN = 512
RMS_EPS = 1e-6
LN_EPS = 1e-5
SLOT = 8192
NSLOT = 3
WTOT = 655360
GELU_C = 0.044715
GELU_S = 1.5957691216057308
ATT_SCALE = 512 ** -0.5

WIN1_CHUNKS = (
    [("u", c) for c in range(8)] + [("v", c) for c in range(8)]
    + [x for c in range(8) for x in (("a", c), ("g", c))]
    + [("p", c) for c in range(8)]
)


def win_col(kind, c):
    base = {"u": 0, "v": 1024, "a": 2048, "g": 3072, "p": 4096}[kind]
    return base + c * 128


def stream_offsets():
    off = {}
    o = 0
    for b in range(10):
        off[("win1", b)] = o; o += 8192
    for d in range(16):
        off[("mg", d)] = o; o += 6144
        off[("mb", d)] = o; o += 3072
    for nm in ("wo", "xq", "xk", "xv", "xo"):
        for b in range(4):
            off[(nm, b)] = o; o += 8192
    for fb in range(4):
        for b in range(4):
            off[("up", fb, b)] = o; o += 8192
        for b in range(4):
            off[("dn", fb, b)] = o; o += 8192
    assert o == WTOT, o
    return off


WOFF = stream_offsets()


def pack_weights(w_in, w_branch, w_out, w_xq, w_xk, w_xv, w_xo, w_up, w_down):
    out = np.empty((128, WTOT), np.float32)

    def put(off, W, c0, ncols, width=None, col_off=0):
        K = W.shape[0]
        kc = K // 128
        width = width or ncols
        dst = out[:, off:off + kc * width].reshape(128, kc, width)[:, :, col_off:col_off + ncols]
        dst[...] = W[:, c0:c0 + ncols].reshape(kc, 128, ncols).transpose(1, 0, 2)

    for b in range(10):
        for j, (k, c) in enumerate(WIN1_CHUNKS[4 * b:4 * b + 4]):
            put(WOFF[("win1", b)], w_in, win_col(k, c), 128, width=512, col_off=j * 128)
    for d in range(16):
        for n in range(3):
            put(WOFF[("mg", d)] + n * 2048, w_in, 5120 + n * 2048 + d * 128, 128)
            put(WOFF[("mb", d)] + n * 1024, w_branch[n], d * 128, 128)
    for nm, W in (("wo", w_out), ("xq", w_xq), ("xk", w_xk), ("xv", w_xv), ("xo", w_xo)):
        for b in range(4):
            put(WOFF[(nm, b)], W, b * 512, 512)
    for fb in range(4):
        for b in range(4):
            put(WOFF[("up", fb, b)], w_up, fb * 2048 + b * 512, 512)
            put(WOFF[("dn", fb, b)], w_down[fb * 2048:(fb + 1) * 2048], b * 512, 512)
    return out


PV = {}
_o = 0
for _nm in ("g_mix", "g_xattn", "g_mem", "g_mlp"):
    for _l in range(NL):
        PV[(_nm, _l)] = _o; _o += 16
PV["g_final"] = _o; _o += 16
for _l in range(NL):
    PV[("conv_w", _l)] = _o; _o += 8 * 31
for _nm in ("conv_b", "ln_c_g", "ln_c_b", "pool_scale"):
    for _l in range(NL):
        PV[(_nm, _l)] = _o; _o += 8
PV["flag"] = _o; _o += 1
PV["icnt"] = _o; _o += 64
NPV = _o


class Op:
    __slots__ = ("eng", "fn", "dma", "deps", "signal", "pos", "sigval", "dsem", "dval", "name")


class Sched:
    ENGS = ("pe", "act", "dve", "pool", "sp")
    SELF_SYNC = {"pe": False, "act": True, "dve": True, "pool": True, "sp": False}
    NDSEM = 8

    def __init__(self):
        self.streams = {e: [] for e in self.ENGS}
        self.lastw = {}
        self.readers = {}
        self.dma_slot_prev = {}
        self.dma_count = {e: 0 for e in self.ENGS}
        self.out_dmas = []

    def add(self, eng, fn, reads=(), writes=(), dma=False, name=""):
        op = Op()
        op.eng, op.fn, op.dma, op.signal, op.name = eng, fn, dma, False, name
        op.pos = len(self.streams[eng])
        deps = set()
        for r in reads:
            w = self.lastw.get(r)
            if w is not None:
                deps.add(w)
        for w_ in writes:
            w = self.lastw.get(w_)
            if w is not None:
                deps.add(w)
            for rd in self.readers.get(w_, ()):
                deps.add(rd)
        if dma:
            k = self.dma_count[eng]
            self.dma_count[eng] += 1
            slot = (eng, k % self.NDSEM)
            op.dsem = slot
            op.dval = 16 * (k // self.NDSEM + 1)
            prev = self.dma_slot_prev.get(slot)
            if prev is not None:
                deps.add(prev)
            self.dma_slot_prev[slot] = op
        deps.discard(op)
        op.deps = deps
        for d in deps:
            if not d.dma:
                d.signal = True
        for r in reads:
            self.readers.setdefault(r, []).append(op)
        for w_ in writes:
            self.lastw[w_] = op
            self.readers[w_] = []
        self.streams[eng].append(op)
        return op

    def emit(self, nc, es):
        for e in self.ENGS:
            c = 0
            for op in self.streams[e]:
                if op.signal:
                    c += 1
                    op.sigval = c
        sems = {e: es.enter_context(nc.semaphore("s_" + e)) for e in self.ENGS}
        dsems = {}
        for e in self.ENGS:
            if self.dma_count[e]:
                for i in range(self.NDSEM):
                    dsems[(e, i)] = es.enter_context(nc.semaphore("d_%s%d" % (e, i)))
        block = es.enter_context(nc.Block())
        streams = self.streams
        SELF = self.SELF_SYNC

        def run(ename, eng):
            seen = {e: 0 for e in self.ENGS}
            seen_d = {}
            for op in streams[ename]:
                need = {}
                for d in op.deps:
                    if d.dma:
                        if seen_d.get(d.dsem, 0) < d.dval:
                            seen_d[d.dsem] = d.dval
                            eng.wait_ge(dsems[d.dsem], d.dval)
                    else:
                        if d.eng == ename and not SELF[ename]:
                            continue
                        if d.sigval > need.get(d.eng, 0):
                            need[d.eng] = d.sigval
                for e2, v in need.items():
                    if v > seen[e2]:
                        seen[e2] = v
                        eng.wait_ge(sems[e2], v)
                ins = op.fn(eng)
                if op.dma:
                    ins.then_inc(dsems[op.dsem], 16)
                elif op.signal:
                    ins.then_inc(sems[ename], 1)
            if ename == "sp":
                for d in self.out_dmas:
                    eng.wait_ge(dsems[d.dsem], d.dval)

        @block.tensor
        def _(e):
            run("pe", e)

        @block.scalar
        def _(e):
            run("act", e)

        @block.vector
        def _(e):
            run("dve", e)

        @block.gpsimd
        def _(e):
            run("pool", e)

        @block.sync
        def _(e):
            run("sp", e)


def build(debug=()):
    nc = bass.Bass("TRN2", target_bir_lowering=False)
    dram_in = lambda n, s: nc.dram_tensor(n, list(s), F32, kind="ExternalInput").ap()
    dram_out = lambda n, s, dt=F32: nc.dram_tensor(n, list(s), dt, kind="ExternalOutput").ap()
    wst = dram_in("wst", WST_SHAPE if WST_SHAPE else ((NL, 128, WTOT) if NLAYERS_RUN else (1, 128, 8)))
    xin = dram_in("xin", (1216, D))
    memin = dram_in("memin", (256, D))
    kcin = dram_in("kcin", (NL, 16, 256, D))
    vcin = dram_in("vcin", (NL, 16, 256, D))
    scin = dram_in("scin", (NL, 16, 30, 1024))
    spin = dram_in("spin", (NL, 16, 15, 1024))
    pvin = dram_in("pvin", (128, NPV))
    lnvin = dram_in("lnvin", (NL, 2, 1024))
    wstin = dram_in("wstin", (NL, 128, 4, 128))
    wssin = dram_in("wssin", (NL, 64, 4, 64))
    maskin = dram_in("maskin", (128, 192))
    bsin = dram_in("bsin", (1, NL * 4 * 192))
    pwin = dram_in("pwin", (NL, 128, 2048))
    identin = dram_in("identin", (128, 128))

    y_o = dram_out("y_o", (1088, D))
    mk_o = dram_out("mk_o", (NL, 256, D))
    mv_o = dram_out("mv_o", (NL, 256, D))
    cp_o = dram_out("cp_o", (NL, 30, 1024))
    pp_o = dram_out("pp_o", (NL, 15, 1024))
    cs_o = dram_out("cs_o", (NL, 16, 30, 1024))
    ps_o = dram_out("ps_o", (NL, 16, 15, 1024))
    cv_o = dram_out("cv_o", (NL, 64, 1024))
    dbg_o = {}
    for (nm, shp, dts) in debug:
        dbg_o[nm] = dram_out("dbg_" + nm, shp, BF16 if dts == "bf16" else F32)

    S = Sched()
    es = ExitStack()
    sb = lambda n, s, dt: es.enter_context(nc.sbuf_tensor(n, list(s), dt))
    xT = sb("xT", (128, KC, NTM), F32)
    B1 = sb("B1", (128, KC, NTM), BF16)
    B2 = sb("B2", (128, KC, NTM), BF16)
    Yb = sb("Yb", (128, 24, NTM), BF16)
    Wr = sb("Wr", (128, NSLOT, SLOT), BF16)
    Tt = sb("Tt", (128, 4, NTM), F32)
    PTb = sb("PTb", (128, 2, NTM), BF16)
    pv = sb("pv", (128, NPV), F32)
    ident = sb("ident", (128, 128), F32)
    identb = sb("identb", (128, 128), BF16)
    onesb = sb("onesb", (128, 128), BF16)
    ones1 = sb("ones1", (1, 128), F32)
    bsrow = sb("bsrow", (1, NL * 4 * 192), F32)
    maskt = sb("maskt", (128, 192), F32)
    wsm = sb("wsm", (128, 4, 192), BF16)
    wtmp = sb("wtmp", (128, 4, 192), F32)
    pwb = sb("pwb", (128, 2048), BF16)
    hconv = sb("hconv", (128, NL, 8, 30), F32)
    hpool = sb("hpool", (128, NL, 8, 15), F32)
    kst = sb("kst", (128, 2, 512), F32)
    ostg = sb("ostg", (128, 1024), F32)
    stt = sb("stt", (128, 16), F32)
    t3b = sb("t3b", (128, 64), F32)
    rss = sb("rss", (128, 64), F32)
    pts = sb("pts", (128, 32), BF16)
    epst = sb("epst", (128, 2), F32)
    ps = es.enter_context(nc.psum_tensor("ps", [128, 6, 512], F32))
    psb = es.enter_context(nc.psum_tensor("psb", [128, 2, 1024], BF16))

    B2f = B2[:].rearrange("p k n -> p (k n)")
    B2w = B2f.bitcast(F32)
    STG = B2w[:, 0:2048]
    VT0 = B2w[:, 0:1024]; VT1 = B2w[:, 1024:2048]; LNVG = B2w[:, 2048:3072]; LNVB = B2w[:, 3072:4096]
    VB = B2w[:, 4096:4608].bitcast(BF16)
    VSET = ["VT0", "VT1", "LNV", "VB"]
    GLm = B2w[:, 0:542]; GLs = B2w[:, 544:1088]; Pm = B2w[:, 1088:1615]; Psb_ = B2w[:, 1616:1920]
    PMc = B2w[:, 1920:2560].bitcast(BF16).rearrange("p (k n) -> p k n", n=NTM)
    ACC = B2w[:, 2560:3200]
    ST1 = B2w[:, 3200:3712]
    CSET = ["GLm", "GLs", "Pm", "Ps", ("PMc", 0), ("PMc", 1), "ACC", "ST1"]
    GLs3 = GLs.rearrange("p (s n) -> p s n", n=34)
    Ps3 = Psb_.rearrange("p (s n) -> p s n", n=19)
    B1f = B1[:].rearrange("p k n -> p (k n)")
    Yf = Yb[:].rearrange("p k n -> p (k n)")
    MTB = Yf[:, 0:4096].rearrange("p (k n) -> p k n", n=256)
    KT = Yf[:, 4096:8192].rearrange("p (k n) -> p k n", n=256)
    VP = Yf[:, 8192:12288].rearrange("p (k n) -> p k n", n=2048)
    MEMSTG = B1f[:, 0:4096].bitcast(F32)
    KRAW = B1f[:, 0:2048]
    VS = B1f[:, 2048:6144].rearrange("p (k n) -> p k n", n=2048)
    KTS = B1f[:, 6144:10240].rearrange("p (k n) -> p k n", n=256)
    ASET = ["MTB", "KT", "VP", "B1a", ("KTS", 0), ("KTS", 1)]

    XK = [("x", k) for k in range(KC)]
    B1K = [("B1", k) for k in range(KC)]
    B2K = [("B2", k) for k in range(KC)]
    YK = [("Y", k) for k in range(24)]

    def A(eng, meth, reads=(), writes=(), **kw):
        return S.add(eng, lambda e: getattr(e, meth)(**kw), reads=reads, writes=writes, name=meth)

    def AM(eng, calls, reads=(), writes=()):
        def fn(e):
            ins = None
            for (meth, kw) in calls:
                ins = getattr(e, meth)(**kw)
            return ins
        return S.add(eng, fn, reads=reads, writes=writes, name="multi")

    def CHAIN(eng, calls, reads=(), writes=()):
        grp_ = []
        for (meth, kw, short) in calls:
            if short:
                if grp_:
                    AM(eng, grp_, reads, writes)
                    grp_ = []
                AM(eng, [(meth, kw)], reads, writes)
            else:
                grp_.append((meth, kw))
        if grp_:
            AM(eng, grp_, reads, writes)

    def DMA(eng, out, in_, reads=(), writes=(), is_out=False):
        op = S.add(eng, lambda e: e.dma_start(out=out, in_=in_), reads=reads, writes=writes, dma=True, name="dma")
        if is_out:
            S.out_dmas.append(op)
        return op

    def MM(out, lhsT, rhs, start, stop):
        return ("matmul", dict(out=out, lhsT=lhsT, rhs=rhs, start=start, stop=stop))

    def TP(out, in_, idn):
        return ("transpose", dict(out=out, in_=in_, identity=idn))

    def ACT(out, in_, func, reads, writes, **kw):
        return A("act", "activation", reads, writes, out=out, in_=in_, func=func, **kw)

    def TT(out, in0, in1, op, reads, writes, eng="dve"):
        return A(eng, "tensor_tensor", reads, writes, out=out, in0=in0, in1=in1, op=op)

    def TS(out, in0, s1, s2, op0, op1, reads, writes, eng="dve"):
        if s2 is None:
            return A(eng, "tensor_scalar", reads, writes, out=out, in0=in0, scalar1=s1, scalar2=None, op0=op0)
        return A(eng, "tensor_scalar", reads, writes, out=out, in0=in0, scalar1=s1, scalar2=s2, op0=op0, op1=op1)

    def STT(out, in0, scalar, in1, op0, op1, reads, writes, eng="dve"):
        return A(eng, "scalar_tensor_tensor", reads, writes, out=out, in0=in0, scalar=scalar, in1=in1, op0=op0, op1=op1)

    def retag(old, new):
        S.add("dve", lambda e: e.engine_nop(), writes=list(old) + list(new), name="retag")

    psrr = [0]

    def psbank(n=1):
        r = []
        for _ in range(n):
            r.append(psrr[0] % 6)
            psrr[0] += 1
        return r

    wcount = [0]

    def wload(layer, key, size):
        slot = wcount[0] % NSLOT
        wcount[0] += 1
        off = WOFF[key]
        DMA("pool", Wr[:, slot, 0:size], wst[layer, :, off:off + size], writes=[("W", slot)])
        return slot

    def dbg(nm, src_ap, reads):
        if nm in dbg_o:
            DMA("sp", dbg_o[nm], src_ap, reads=reads, is_out=True)

    def pvc(key, i=0, n=1):
        return pv[:, PV[key] + i:PV[key] + i + n]

    DMA("sp", pv[:], pvin[:, :], writes=["pv"])
    DMA("sp", ident[:], identin[:, :], writes=["ident"])
    DMA("sp", bsrow[:], bsin[:, :], writes=["bsrow"])
    DMA("sp", maskt[:], maskin[:, :], writes=["maskt"])
    A("dve", "tensor_copy", ["ident"], ["identb"], out=identb[:], in_=ident[:])
    A("dve", "memset", [], ["onesb"], ap=onesb[:], constant=1.0)
    A("dve", "memset", [], ["ones1"], ap=ones1[:], constant=1.0)
    A("dve", "memset", [], ["epst"], ap=epst[:, 0:1], constant=RMS_EPS)
    A("dve", "memset", [], ["epst"], ap=epst[:, 1:2], constant=LN_EPS)

    def rsqrt_(out, in_, scale, eps_col, reads, writes, npart=128):
        ACT(out, in_, AF.Sqrt, list(reads) + ["epst"], writes, scale=scale, bias=epst[0:npart, eps_col:eps_col + 1])
        A("dve", "reciprocal", writes, writes, out=out, in_=out)

    def rms_stats(sqbuf, sqkeys, NT, grp):
        for k in range(KC):
            ACT(sqbuf[:, k, 0:NT], xT[:, k, 0:NT], AF.Square, [("x", k)], [sqkeys[k]])
        for (c0, n) in grp:
            (b,) = psbank(1)
            AM("pe", [MM(ps[:, b, 0:n], onesb[:], sqbuf[:, k, c0:c0 + n], k == 0, k == KC - 1) for k in range(KC)],
               reads=list(sqkeys) + ["onesb"], writes=[("ps", b)])
            rsqrt_(Tt[:, 0, c0:c0 + n], ps[:, b, 0:n], 1.0 / D, 0, [("ps", b)], [("T", 0, c0)])

    def rmsnorm_to(dst, dstkeys, gkey, NT, grp):
        rms_stats(dst, dstkeys, NT, grp)
        tk = [("T", 0, c0) for (c0, n) in grp]
        for k in range(KC):
            STT(dst[:, k, 0:NT], xT[:, k, 0:NT], pvc(gkey, k), Tt[:, 0, 0:NT], ALU.mult, ALU.mult,
                [("x", k), "pv"] + tk, [dstkeys[k]])

    def mm_chunk(lhs_fn, nkc, rhs_fn, rhskeys, wkeys, grp):
        banks = psbank(len(grp))
        res = []
        for (c0, n), b in zip(grp, banks):
            AM("pe", [MM(ps[:, b, 0:n], lhs_fn(k), rhs_fn(k, c0, n), k == 0, k == nkc - 1) for k in range(nkc)],
               reads=list(rhskeys) + list(wkeys), writes=[("ps", b)])
            res.append((b, c0, n))
        return res

    def gelu_ps(src, srckey, t, tk, out_ap, out_keys):
        ACT(t, src, AF.Square, [srckey], [tk])
        TS(t, t, GELU_C, 1.0, ALU.mult, ALU.add, [tk], [tk])
        TT(t, src, t, ALU.mult, [tk, srckey], [tk])
        ACT(t, t, AF.Sigmoid, [tk], [tk], scale=GELU_S)
        TT(out_ap, src, t, ALU.mult, [tk, srckey], out_keys)

    def wview(slot):
        return Wr[:, slot, :].rearrange("p (k n) -> p k n", n=512)

    def do_layer(pas, l, NT, ns, grp, tiles):
        DMA("pool", pwb[:], pwin[l, :, :], writes=["pwb"])
        DMA("sp", wtmp[:, :, 0:128], wstin[l, :, :, :], writes=["wtmp"])
        DMA("sp", wtmp[0:64, :, 128:192], wssin[l, :, :, :], writes=["wtmp"])
        for h in range(4):
            TT(wsm[:, h, 0:128], wtmp[:, h, 0:128], maskt[:, 0:128], ALU.mult, ["wtmp", "maskt"], ["wsm"])
            TT(wsm[0:64, h, 128:192], wtmp[0:64, h, 128:192], maskt[0:64, 128:192], ALU.mult, ["wtmp", "maskt"], ["wsm"])

        rmsnorm_to(B1, B1K, ("g_mix", l), NT, grp)
        dbg("h_%d_%d" % (pas, l), B1[:], B1K)

        retag(B2K, VSET)
        DMA("sp", LNVG, lnvin[l, 0:1, :].partition_broadcast(128), writes=["LNV"])
        DMA("sp", LNVB, lnvin[l, 1:2, :].partition_broadcast(128), writes=["LNV"])

        for blk in range(2):
            slot = wload(l, ("win1", blk), 8192)
            wv = wview(slot)
            for j in range(4):
                c = blk * 4 + j
                res = mm_chunk(lambda k: wv[:, k, j * 128:(j + 1) * 128], KC,
                               lambda k, c0, n: B1[:, k, c0:c0 + n], B1K, [("W", slot)], grp)
                for gi, (b, c0, n) in enumerate(res):
                    gelu_ps(ps[:, b, 0:n], ("ps", b), Tt[:, 1 + gi, 0:n], ("T", 1 + gi, 0), Yb[:, c, c0:c0 + n], [("Y", c)])

        slots_v = [wload(l, ("win1", 2), 8192), wload(l, ("win1", 3), 8192)]
        for ti, (c0, nt, r0) in enumerate(tiles):
            banks = psbank(2)
            for hf in range(2):
                wv = wview(slots_v[hf])
                AM("pe", [MM(ps[0:nt, banks[hf], :], B1[:, k, c0:c0 + nt], wv[:, k, :], k == 0, k == KC - 1) for k in range(KC)],
                   reads=B1K + [("W", slots_v[hf])], writes=[("ps", banks[hf])])
            for hf in range(2):
                b = banks[hf]
                sl = slice(hf * 512, (hf + 1) * 512)
                gelu_ps(ps[0:nt, b, :], ("ps", b), VT1[0:nt, sl], "VT1", VT0[0:nt, sl], ["VT0"])
            A("dve", "reduce_sum", ["VT0"], ["stt"], out=stt[0:nt, 0:1], in_=VT0[0:nt, :], axis=AX.X)
            TS(stt[0:nt, 1:2], stt[0:nt, 0:1], -1.0 / 1024, None, ALU.mult, None, ["stt"], ["stt"])
            ACT(VT1[0:nt, :], VT0[0:nt, :], AF.Square, ["VT0", "stt"], ["VT1", "stt"], bias=stt[0:nt, 1:2], accum_out=stt[0:nt, 2:3])
            rsqrt_(stt[0:nt, 3:4], stt[0:nt, 2:3], 1.0 / 1024, 1, ["stt"], ["stt"], npart=nt)
            TS(VT0[0:nt, :], VT0[0:nt, :], stt[0:nt, 1:2], stt[0:nt, 3:4], ALU.add, ALU.mult, ["VT0", "stt"], ["VT0"])
            TT(VT0[0:nt, :], VT0[0:nt, :], LNVG[0:nt, :], ALU.mult, ["VT0", "LNV"], ["VT0"])
            TT(VT0[0:nt, :], VT0[0:nt, :], LNVB[0:nt, :], ALU.add, ["VT0", "LNV"], ["VT0"])
            ACT(VB[0:nt, :], VT0[0:nt, :], AF.Copy, ["VT0"], ["VB"])
            sample_tile = (pas == 1 and ti == 4)
            if sample_tile:
                DMA("sp", cv_o[l, :, :], VT0[0:64, :], reads=["VT0"], is_out=True)
            banks2 = psbank(2)
            for half in range(2):
                b = banks2[half]
                calls = []
                for j in range(4):
                    fc = half * 4 + j
                    hh = fc // 2
                    w_ap = wsm[0:nt, hh, 128:128 + nt] if sample_tile else wsm[:, hh, 0:128]
                    bo = ((l * 4 + hh) * 192) + (128 if sample_tile else 0)
                    calls.append(MM(ps[:, b, j * 128:j * 128 + nt], VB[0:nt, fc * 128:(fc + 1) * 128], w_ap, True, False))
                    calls.append(MM(ps[:, b, j * 128:j * 128 + nt], ones1[0:1, :], bsrow[0:1, bo:bo + nt], False, True))
                AM("pe", calls, reads=["VB", "wsm", "bsrow", "ones1"], writes=[("ps", b)])
                yk = [("Y", half * 4 + j) for j in range(4)]
                TT(Yb[:, half * 4:half * 4 + 4, c0:c0 + nt], ps[:, b, :].rearrange("p (j n) -> p j n", n=128)[:, :, 0:nt],
                   Yb[:, half * 4:half * 4 + 4, c0:c0 + nt], ALU.mult, [("ps", b)] + yk, yk)
        dbg("ya_%d_%d" % (pas, l), Yb[:, 0:8, :], YK[0:8])
        if STOP == "v":
            retag(VSET, B2K)
            return

        retag(VSET, CSET)
        if pas == 0:
            A("dve", "memset", [], ["GLs"], ap=GLs[:, 0:30], constant=0.0)
            A("dve", "memset", [], ["Ps"], ap=Psb_[:, 0:15], constant=0.0)
        for blk in range(4):
            slot = wload(l, ("win1", 4 + blk), 8192)
            wv = wview(slot)
            for pr in range(2):
                c = blk * 2 + pr
                ra = mm_chunk(lambda k: wv[:, k, (2 * pr) * 128:(2 * pr + 1) * 128], KC,
                              lambda k, c0, n: B1[:, k, c0:c0 + n], B1K, [("W", slot)], grp)
                rg = mm_chunk(lambda k: wv[:, k, (2 * pr + 1) * 128:(2 * pr + 2) * 128], KC,
                              lambda k, c0, n: B1[:, k, c0:c0 + n], B1K, [("W", slot)], grp)
                if pas == 1:
                    DMA("sp", ST1[0:120, 0:512].rearrange("p (g n) -> p g n", n=128),
                        scin[l, :, :, c * 128:(c + 1) * 128].rearrange("(g s) r n -> (s r) g n", g=4), writes=["ST1"])
                    (b,) = psbank(1)
                    AM("pe", [TP(ps[:, b, g * 120:(g + 1) * 120], ST1[0:120, g * 128:(g + 1) * 128], ident[0:120, 0:120]) for g in range(4)],
                       reads=["ST1", "ident"], writes=[("ps", b)])
                    ACT(GLs3[:, :, 0:30], ps[:, b, 0:480].rearrange("p (s r) -> p s r", r=30), AF.Copy, [("ps", b)], ["GLs"])
                for gi in range(2):
                    (ba, c0, n) = ra[gi]
                    (bg, _, _) = rg[gi]
                    tk = ("T", 1 + gi, 0)
                    t = Tt[:, 1 + gi, 0:n]
                    ACT(t, ps[:, bg, 0:n], AF.Sigmoid, [("ps", bg)], [tk])
                    in0 = ps[:, ba, 0:n]
                    if gi == 0:
                        o_ap = GLm[:, 30:542]; ok = "GLm"
                    elif pas == 0:
                        o_ap = GLs[:, 30:158]; ok = "GLs"
                    else:
                        o_ap = GLs3[:, :, 30:34]; ok = "GLs"
                        t = t.rearrange("p (s r) -> p s r", r=4)
                        in0 = in0.rearrange("p (s r) -> p s r", r=4)
                    TT(o_ap, in0, t, ALU.mult, [tk, ("ps", ba)], [ok])
                if pas == 0:
                    TS(GLm[:, 0:30], GLs[:, 128:158], pvc("flag"), None, ALU.mult, None, ["GLs", "pv"], ["GLm"])
                    A("dve", "tensor_copy", ["GLm"], ["hconv"], out=hconv[:, l, c, :], in_=GLm[:, 512:542])
                else:
                    A("dve", "tensor_copy", ["hconv"], ["GLm"], out=GLm[:, 0:30], in_=hconv[:, l, c, :])
                    A("dve", "tensor_copy", ["GLm"], ["hconv"], out=hconv[:, l, c, :], in_=GLm[:, 512:542])
                cw = PV[("conv_w", l)] + c * 31
                cb = pvc(("conv_b", l), c)
                specs = [(ACC[:, 0:512], lambda k: GLm[:, k:k + 512])]
                if pas == 0:
                    specs.append((ACC[:, 512:640], lambda k: GLs[:, k:k + 128]))
                else:
                    specs.append((ACC[:, 512:576].rearrange("p (s r) -> p s r", r=4), lambda k: GLs3[:, :, k:k + 4]))
                calls = []
                for si, (o, src) in enumerate(specs):
                    short = (si == 1 and pas == 1)
                    calls.append(("tensor_scalar", dict(out=o, in0=src(0), scalar1=pv[:, cw:cw + 1], scalar2=cb, op0=ALU.mult, op1=ALU.add), short))
                    for k in range(1, 31):
                        calls.append(("scalar_tensor_tensor", dict(out=o, in0=src(k), scalar=pv[:, cw + k:cw + k + 1], in1=o, op0=ALU.mult, op1=ALU.add), short))
                CHAIN("dve", calls, reads=["GLm", "GLs", "pv"], writes=["ACC"])
                ACT(Yb[:, 8 + c, 0:NT], ACC[:, 0:NT], AF.Copy, ["ACC"], [("Y", 8 + c)])
                if pas == 1:
                    (b,) = psbank(1)
                    A("dve", "tensor_copy", ["GLs"], ["t3b"], out=t3b[:].rearrange("p (s r) -> p s r", r=4), in_=GLs3[:, :, 30:34])
                    AM("pe", [TP(ps[0:64, b, 0:128], t3b[:], ident[:, :])], reads=["t3b", "ident"], writes=[("ps", b)])
                    A("dve", "tensor_copy", [("ps", b)], ["ostg"], out=ostg[0:64, c * 128:(c + 1) * 128], in_=ps[0:64, b, 0:128])
        if pas == 1:
            for s in range(16):
                DMA("sp", cs_o[l, s, 26:30, :], ostg[s * 4:(s + 1) * 4, :], reads=["ostg"], is_out=True)
            DMA("sp", cs_o[l, :, 0:26, :], scin[l, :, 4:30, :], is_out=True)
            banks = psbank(2)
            for half in range(2):
                b = banks[half]
                AM("pe", [TP(ps[0:30, b, j * 128:(j + 1) * 128], hconv[:, l, half * 4 + j, :], ident[:, :]) for j in range(4)],
                   reads=["hconv", "ident"], writes=[("ps", b)])
                A("dve", "tensor_copy", [("ps", b)], [("kst", 0), ("kst", 1)], out=kst[0:30, half, :], in_=ps[0:30, b, :])
            DMA("sp", cp_o[l, :, :], kst[0:30, :, :].rearrange("p a n -> p (a n)"), reads=[("kst", 0), ("kst", 1)], is_out=True)

        for (c0, n) in grp:
            b1, b2 = psbank(2)
            AM("pe", [MM(ps[:, b1, 0:n], onesb[:], Yb[:, 8 + c, c0:c0 + n], c == 0, c == 7) for c in range(8)],
               reads=YK[8:16] + ["onesb"], writes=[("ps", b1)])
            for c in range(8):
                ACT(PTb[:, c % 2, 0:n], Yb[:, 8 + c, c0:c0 + n], AF.Square, [("Y", 8 + c)], [("PT", c % 2)])
                AM("pe", [MM(ps[:, b2, 0:n], onesb[:], PTb[:, c % 2, 0:n], c == 0, c == 7)],
                   reads=[("PT", c % 2), "onesb"], writes=[("ps", b2)])
            TS(Tt[:, 1, c0:c0 + n], ps[:, b1, 0:n], 1.0 / 1024, None, ALU.mult, None, [("ps", b1)], [("T", 1, c0)])
            TT(Tt[:, 2, c0:c0 + n], Tt[:, 1, c0:c0 + n], Tt[:, 1, c0:c0 + n], ALU.mult, [("T", 1, c0)], [("T", 2, c0)])
            STT(Tt[:, 2, c0:c0 + n], ps[:, b2, 0:n], 1.0 / 1024, Tt[:, 2, c0:c0 + n], ALU.mult, ALU.subtract,
                [("ps", b2), ("T", 2, c0)], [("T", 2, c0)])
            rsqrt_(Tt[:, 2, c0:c0 + n], Tt[:, 2, c0:c0 + n], 1.0, 1, [("T", 2, c0)], [("T", 2, c0)])
        tks = [("T", 1, 0), ("T", 1, MAIN), ("T", 2, 0), ("T", 2, MAIN)]
        for c in range(8):
            TT(Tt[:, 3, 0:NT], Yb[:, 8 + c, 0:NT], Tt[:, 1, 0:NT], ALU.subtract, [("Y", 8 + c)] + tks, ["T3"])
            TT(Tt[:, 3, 0:NT], Tt[:, 3, 0:NT], Tt[:, 2, 0:NT], ALU.mult, ["T3"] + tks, ["T3"])
            ACT(Yb[:, 8 + c, 0:NT], Tt[:, 3, 0:NT], AF.Silu, ["T3", "pv"], [("Y", 8 + c)],
                scale=pvc(("ln_c_g", l), c), bias=pvc(("ln_c_b", l), c))
        dbg("yb_%d_%d" % (pas, l), Yb[:, 8:16, :], YK[8:16])
        if STOP == "conv":
            retag(CSET, B2K)
            return

        pw4 = pwb[:].rearrange("p (g k n) -> p g k n", g=4, k=2)
        for blk in range(2):
            slot = wload(l, ("win1", 8 + blk), 8192)
            wv = wview(slot)
            for j in range(4):
                c = blk * 4 + j
                gi_ = c // 2
                w = 2 << gi_
                res = mm_chunk(lambda k: wv[:, k, j * 128:(j + 1) * 128], KC,
                               lambda k, c0, n: B1[:, k, c0:c0 + n], B1K, [("W", slot)], grp)
                if pas == 1:
                    DMA("sp", ST1[0:120, 0:256].rearrange("p (g n) -> p g n", n=128),
                        spin[l, :, :, c * 128:(c + 1) * 128].rearrange("(g s) r n -> (s r) g n", g=2), writes=["ST1"])
                    (b,) = psbank(1)
                    AM("pe", [TP(ps[:, b, g * 120:(g + 1) * 120], ST1[0:120, g * 128:(g + 1) * 128], ident[0:120, 0:120]) for g in range(2)],
                       reads=["ST1", "ident"], writes=[("ps", b)])
                    ACT(Ps3[:, :, 0:15], ps[:, b, 0:240].rearrange("p (s r) -> p s r", r=15), AF.Copy, [("ps", b)], ["Ps"])
                (bm, _, _), (bs_, _, nsn) = res
                ACT(Pm[:, 15:527], ps[:, bm, 0:512], AF.Copy, [("ps", bm)], ["Pm"])
                if pas == 0:
                    ACT(Psb_[:, 15:143], ps[:, bs_, 0:128], AF.Copy, [("ps", bs_)], ["Ps"])
                    TS(Pm[:, 0:15], Psb_[:, 128:143], pvc("flag"), None, ALU.mult, None, ["Ps", "pv"], ["Pm"])
                    A("dve", "tensor_copy", ["Pm"], ["hpool"], out=hpool[:, l, c, :], in_=Pm[:, 512:527])
                else:
                    ACT(Ps3[:, :, 15:19], ps[:, bs_, 0:64].rearrange("p (s r) -> p s r", r=4), AF.Copy, [("ps", bs_)], ["Ps"])
                    A("dve", "tensor_copy", ["hpool"], ["Pm"], out=Pm[:, 0:15], in_=hpool[:, l, c, :])
                    A("dve", "tensor_copy", ["Pm"], ["hpool"], out=hpool[:, l, c, :], in_=Pm[:, 512:527])
                    (b,) = psbank(1)
                    A("dve", "tensor_copy", ["Ps"], ["t3b"], out=t3b[:].rearrange("p (s r) -> p s r", r=4), in_=Ps3[:, :, 15:19])
                    AM("pe", [TP(ps[0:64, b, 0:128], t3b[:], ident[:, :])], reads=["t3b", "ident"], writes=[("ps", b)])
                    A("dve", "tensor_copy", [("ps", b)], ["ostg"], out=ostg[0:64, c * 128:(c + 1) * 128], in_=ps[0:64, b, 0:128])
                calls = []
                for which in range(2):
                    if which == 0:
                        L = 527
                        src = lambda a, b_: Pm[:, a:b_]
                        t1 = lambda a, b_: Tt[:, 1, a:b_]
                        t2 = lambda a, b_: Tt[:, 2, a:b_]
                    elif pas == 0:
                        L = 143
                        src = lambda a, b_: Psb_[:, a:b_]
                        t1 = lambda a, b_: ACC[:, a:b_]
                        t2 = lambda a, b_: ACC[:, 160 + a:160 + b_]
                    else:
                        L = 19
                        src = lambda a, b_: Ps3[:, :, a:b_]
                        A3 = ACC[:, 0:304].rearrange("p (s n) -> p s n", n=19)
                        B3 = ACC[:, 320:624].rearrange("p (s n) -> p s n", n=19)
                        t1 = lambda a, b_, A3=A3: A3[:, :, a:b_]
                        t2 = lambda a, b_, B3=B3: B3[:, :, a:b_]
                    cur = src
                    sh = 1
                    bufs = [t1, t2]
                    bi = 0
                    lo = 0
                    while sh < w:
                        lo2 = lo + sh
                        nxt = bufs[bi]
                        calls.append(("tensor_tensor", dict(out=nxt(lo2, L), in0=cur(lo2, L), in1=cur(lo2 - sh, L - sh), op=ALU.add), which == 1 and pas == 1))
                        cur = nxt
                        bi ^= 1
                        lo = lo2
                        sh *= 2
                    hist = 15
                    if which == 0:
                        calls.append(("scalar_tensor_tensor", dict(out=PMc[:, c % 2, 0:512], in0=cur(hist, L), scalar=1.0 / w, in1=src(hist, L),
                                                                   op0=ALU.mult, op1=ALU.subtract), False))
                        if pas == 0:
                            ic = PV["icnt"] + (c // 2) * 16
                            calls.append(("tensor_tensor", dict(out=Tt[:, 3, 0:16], in0=cur(hist, hist + 16), in1=pv[:, ic:ic + 16], op=ALU.mult), True))
                            calls.append(("tensor_tensor", dict(out=PMc[:, c % 2, 0:16], in0=Tt[:, 3, 0:16], in1=src(hist, hist + 16), op=ALU.subtract), True))
                    else:
                        if pas == 0:
                            o = PMc[:, c % 2, 512:640]
                        else:
                            o = PMc[:, c % 2, 512:576].rearrange("p (s r) -> p s r", r=4)
                        calls.append(("scalar_tensor_tensor", dict(out=o, in0=cur(hist, L), scalar=1.0 / w, in1=src(hist, L), op0=ALU.mult, op1=ALU.subtract), pas == 1))
                CHAIN("dve", calls, reads=["Pm", "Ps", "pv"],
                   writes=[("PMc", c % 2), ("T", 1, 0), ("T", 2, 0), ("T", 1, MAIN), ("T", 2, MAIN), "ACC", "T3"])
                if c % 2 == 1:
                    for oc in range(2):
                        res2 = mm_chunk(lambda k: pw4[:, gi_, k, oc * 128:(oc + 1) * 128], 2,
                                        lambda k, c0, n: PMc[:, k, c0:c0 + n], [("PMc", 0), ("PMc", 1)], ["pwb"], grp)
                        for (b, c0, n) in res2:
                            ACT(Yb[:, 16 + 2 * gi_ + oc, c0:c0 + n], ps[:, b, 0:n], AF.Copy, [("ps", b), "pv"], [("Y", 16 + 2 * gi_ + oc)],
                                scale=pvc(("pool_scale", l), 2 * gi_ + oc))
        if pas == 1:
            for s in range(16):
                DMA("sp", ps_o[l, s, 11:15, :], ostg[s * 4:(s + 1) * 4, :], reads=["ostg"], is_out=True)
            DMA("sp", ps_o[l, :, 0:11, :], spin[l, :, 4:15, :], is_out=True)
            banks = psbank(2)
            for half in range(2):
                b = banks[half]
                AM("pe", [TP(ps[0:15, b, j * 128:(j + 1) * 128], hpool[:, l, half * 4 + j, :], ident[:, :]) for j in range(4)],
                   reads=["hpool", "ident"], writes=[("ps", b)])
                A("dve", "tensor_copy", [("ps", b)], [("kst", 0), ("kst", 1)], out=kst[0:15, half, :], in_=ps[0:15, b, :])
            DMA("sp", pp_o[l, :, :], kst[0:15, :, :].rearrange("p a n -> p (a n)"), reads=[("kst", 0), ("kst", 1)], is_out=True)
        dbg("yc_%d_%d" % (pas, l), Yb[:, 16:24, :], YK[16:24])
        retag(CSET, B2K)
        if STOP == "pool":
            return

        for d in range(16):
            sg = wload(l, ("mg", d), 6144)
            sbr = wload(l, ("mb", d), 3072)
            gv = Wr[:, sg, 0:6144].rearrange("p (n k c) -> p n k c", n=3, k=16)
            bv = Wr[:, sbr, 0:3072].rearrange("p (n k c) -> p n k c", n=3, k=8)
            for n_ in range(3):
                rg = mm_chunk(lambda k: gv[:, n_, k, :], KC, lambda k, c0, n: B1[:, k, c0:c0 + n], B1K, [("W", sg)], grp)
                rb = mm_chunk(lambda k: bv[:, n_, k, :], 8, lambda k, c0, n: Yb[:, n_ * 8 + k, c0:c0 + n],
                              YK[n_ * 8:n_ * 8 + 8], [("W", sbr)], grp)
                for gi in range(2):
                    (bg, c0, n) = rg[gi]
                    (bb, _, _) = rb[gi]
                    tk = ("T", 1, c0)
                    ak = ("T", 2, c0)
                    ACT(Tt[:, 1, c0:c0 + n], ps[:, bg, 0:n], AF.Sigmoid, [("ps", bg)], [tk])
                    if n_ == 0:
                        TT(Tt[:, 2, c0:c0 + n], ps[:, bb, 0:n], Tt[:, 1, c0:c0 + n], ALU.mult, [("ps", bb), tk], [ak])
                    else:
                        TT(Tt[:, 1, c0:c0 + n], ps[:, bb, 0:n], Tt[:, 1, c0:c0 + n], ALU.mult, [("ps", bb), tk], [tk])
                        if n_ == 1:
                            TT(Tt[:, 2, c0:c0 + n], Tt[:, 2, c0:c0 + n], Tt[:, 1, c0:c0 + n], ALU.add, [tk, ak], [ak])
                        else:
                            TT(B2[:, d, c0:c0 + n], Tt[:, 2, c0:c0 + n], Tt[:, 1, c0:c0 + n], ALU.add, [tk, ak], [("B2", d)])
        dbg("mg_%d_%d" % (pas, l), B2[:], B2K)
        if STOP == "merge":
            return

        def proj_residual(key, src, srckeys):
            for blk in range(4):
                slot = wload(l, (key, blk), 8192)
                wv = wview(slot)
                for j in range(4):
                    e_ = blk * 4 + j
                    res = mm_chunk(lambda k: wv[:, k, j * 128:(j + 1) * 128], KC,
                                   lambda k, c0, n: src[:, k, c0:c0 + n], srckeys, [("W", slot)], grp)
                    for (b, c0, n) in res:
                        TT(xT[:, e_, c0:c0 + n], ps[:, b, 0:n], xT[:, e_, c0:c0 + n], ALU.add, [("ps", b), ("x", e_)], [("x", e_)])

        proj_residual("wo", B2, B2K)
        dbg("x1_%d_%d" % (pas, l), xT[:], XK)
        if STOP == "wo":
            return

        rmsnorm_to(B1, B1K, ("g_xattn", l), NT, grp)
        for blk in range(4):
            slot = wload(l, ("xq", blk), 8192)
            wv = wview(slot)
            for j in range(4):
                e_ = blk * 4 + j
                res = mm_chunk(lambda k: wv[:, k, j * 128:(j + 1) * 128], KC,
                               lambda k, c0, n: B1[:, k, c0:c0 + n], B1K, [("W", slot)], grp)
                for (b, c0, n) in res:
                    ACT(B2[:, e_, c0:c0 + n], ps[:, b, 0:n], AF.Copy, [("ps", b)], [("B2", e_)])
        if STOP == "attn_q":
            return
        retag(YK + B1K, ASET)
        for tkc in range(2):
            DMA("sp", MEMSTG, memin[tkc * 128:(tkc + 1) * 128, :], writes=["B1a"])
            for q in range(4):
                ACT(Tt[:, 1, 0:512], MEMSTG[:, q * 512:(q + 1) * 512], AF.Square, ["B1a"], [("T", 1, 0), "stt"], accum_out=stt[:, 8 + q:9 + q])
            A("dve", "reduce_sum", ["stt"], ["stt"], out=stt[:, 12:13], in_=stt[:, 8:12], axis=AX.X)
            rsqrt_(stt[:, 13:14], stt[:, 12:13], 1.0 / D, 0, ["stt"], ["stt"])
            TS(MEMSTG, MEMSTG, stt[:, 13:14], None, ALU.mult, None, ["stt", "B1a"], ["B1a"])
            for q in range(4):
                (b,) = psbank(1)
                AM("pe", [TP(ps[:, b, j * 128:(j + 1) * 128], MEMSTG[:, (q * 4 + j) * 128:(q * 4 + j + 1) * 128], ident[:, :]) for j in range(4)],
                   reads=["B1a", "ident"], writes=[("ps", b)])
                for j in range(4):
                    k = q * 4 + j
                    ACT(MTB[:, k, tkc * 128:(tkc + 1) * 128], ps[:, b, j * 128:(j + 1) * 128], AF.Copy, [("ps", b), "pv"], ["MTB"],
                        scale=pvc(("g_mem", l), k))
        if STOP == "attn_mem":
            retag(ASET, YK + B1K)
            return
        for which, key, out_d in ((0, "xk", mk_o), (1, "xv", mv_o)):
            for blk in range(4):
                slot = wload(l, (key, blk), 8192)
                wv = wview(slot)
                for tkc in range(2):
                    (b,) = psbank(1)
                    AM("pe", [MM(ps[:, b, :], MTB[:, k, tkc * 128:(tkc + 1) * 128], wv[:, k, :], k == 0, k == KC - 1) for k in range(KC)],
                       reads=["MTB", ("W", slot)], writes=[("ps", b)])
                    if pas == 0 and "kvout" not in SKIP:
                        if True:
                            A("dve", "tensor_copy", [("ps", b)], [("kst", tkc)], out=kst[:, tkc, :], in_=ps[:, b, :])
                        else:
                            ACT(kst[:, tkc, :], ps[:, b, :], AF.Copy, [("ps", b)], [("kst", tkc)])
                        DMA("sp", out_d[l, tkc * 128:(tkc + 1) * 128, blk * 512:(blk + 1) * 512], kst[:, tkc, :], reads=[("kst", tkc)], is_out=True)
                    if which == 1:
                        A("dve", "tensor_copy", [("ps", b)], ["VP"], out=VP[:, tkc, blk * 512:(blk + 1) * 512], in_=ps[:, b, :])
                if which == 0 and "kt" not in SKIP:
                    for j in range(4):
                        (b,) = psbank(1)
                        AM("pe", [MM(ps[:, b, 0:256], wv[:, k, j * 128:(j + 1) * 128], MTB[:, k, :], k == 0, k == KC - 1) for k in range(KC)],
                           reads=["MTB", ("W", slot)], writes=[("ps", b)])
                        ACT(KT[:, blk * 4 + j, :], ps[:, b, 0:256], AF.Copy, [("ps", b)], ["KT"])
        if STOP == "attn_kv":
            retag(ASET, YK + B1K)
            return
        pgrp = [(0, MAIN)] + ([(MAIN, 128)] if pas == 0 else [])
        for hd in range(4):
            qk = [("B2", hd * 4 + dc) for dc in range(4)]
            for (c0, n) in pgrp:
                for kcx in range(2):
                    (b,) = psbank(1)
                    AM("pe", [MM(ps[:, b, 0:n], KT[:, hd * 4 + dc, kcx * 128:(kcx + 1) * 128], B2[:, hd * 4 + dc, c0:c0 + n], dc == 0, dc == 3) for dc in range(4)],
                       reads=["KT"] + qk, writes=[("ps", b)])
                    ACT(PTb[:, kcx, c0:c0 + n], ps[:, b, 0:n], AF.Exp, [("ps", b)], [("PT", kcx)], scale=ATT_SCALE)
                (b,) = psbank(1)
                AM("pe", [MM(ps[:, b, 0:n], onesb[:], PTb[:, 0, c0:c0 + n], True, False), MM(ps[:, b, 0:n], onesb[:], PTb[:, 1, c0:c0 + n], False, True)],
                   reads=[("PT", 0), ("PT", 1), "onesb"], writes=[("ps", b)])
                A("dve", "reciprocal", [("ps", b)], [("T", 1, c0)], out=Tt[:, 1, c0:c0 + n], in_=ps[:, b, 0:n])
                for dc in range(4):
                    (b,) = psbank(1)
                    ch = hd * 4 + dc
                    AM("pe", [MM(ps[:, b, 0:n], VP[:, 0, ch * 128:(ch + 1) * 128], PTb[:, 0, c0:c0 + n], True, False),
                              MM(ps[:, b, 0:n], VP[:, 1, ch * 128:(ch + 1) * 128], PTb[:, 1, c0:c0 + n], False, True)],
                       reads=["VP", ("PT", 0), ("PT", 1)], writes=[("ps", b)])
                    TT(B2[:, ch, c0:c0 + n], ps[:, b, 0:n], Tt[:, 1, c0:c0 + n], ALU.mult, [("ps", b), ("T", 1, c0)], [("B2", ch)])
        if pas == 1:
            pt4 = pts[:].rearrange("p (k h t) -> p k h t", k=2, h=4)
            for s in range(16):
                DMA("pool", VS, vcin[l, s, :, :].rearrange("(k p) n -> p k n", p=128), writes=["B1a"])
                for kcx in range(2):
                    DMA("pool", KRAW, kcin[l, s, kcx * 128:(kcx + 1) * 128, :], writes=["B1a"])
                    for q in range(2):
                        AM("pe", [TP(psb[:, q, j * 128:(j + 1) * 128], KRAW[:, (q * 8 + j) * 128:(q * 8 + j + 1) * 128], identb[:, :]) for j in range(8)],
                           reads=["B1a", "identb"], writes=[("psb", q)])
                        o_ = KTS[:, q * 8:(q + 1) * 8, kcx * 128:(kcx + 1) * 128]
                        i_ = psb[:, q, :].rearrange("p (j n) -> p j n", n=128)
                        if q == 0:
                            ACT(o_, i_, AF.Copy, [("psb", q)], [("KTS", kcx)])
                        else:
                            A("dve", "tensor_copy", [("psb", q)], [("KTS", kcx)], out=o_, in_=i_)
                (b,) = psbank(1)
                cs0 = MAIN + s * 4
                calls = []
                for kcx in range(2):
                    for hd in range(4):
                        for dc in range(4):
                            calls.append(MM(ps[:, b, (kcx * 4 + hd) * 4:(kcx * 4 + hd) * 4 + 4], KTS[:, hd * 4 + dc, kcx * 128:(kcx + 1) * 128],
                                            B2[:, hd * 4 + dc, cs0:cs0 + 4], dc == 0, dc == 3))
                AM("pe", calls, reads=[("KTS", 0), ("KTS", 1)] + B2K, writes=[("ps", b)])
                ACT(pts[:], ps[:, b, 0:32], AF.Exp, [("ps", b)], ["pts"], scale=ATT_SCALE)
                (b2_,) = psbank(1)
                calls = []
                for hd in range(4):
                    for r in range(4):
                        for kcx in range(2):
                            calls.append(MM(ps[:, b2_, 64 + hd * 16 + r * 4:64 + hd * 16 + r * 4 + 4], onesb[:], pt4[:, kcx, hd, :], kcx == 0, kcx == 1))
                for dc16 in range(16):
                    hd = dc16 // 4
                    for kcx in range(2):
                        calls.append(MM(ps[:, b2_, dc16 * 4:dc16 * 4 + 4], VS[:, kcx, dc16 * 128:(dc16 + 1) * 128], pt4[:, kcx, hd, :], kcx == 0, kcx == 1))
                AM("pe", calls, reads=["pts", "B1a", "onesb"], writes=[("ps", b2_)])
                A("dve", "reciprocal", [("ps", b2_)], ["rss"], out=rss[:], in_=ps[:, b2_, 64:128])
                TT(B2[:, :, cs0:cs0 + 4], ps[:, b2_, 0:64].rearrange("p (d t) -> p d t", t=4), rss[:].rearrange("p (d t) -> p d t", t=4), ALU.mult,
                   [("ps", b2_), "rss"], B2K)
        retag(ASET, YK + B1K)
        dbg("ot_%d_%d" % (pas, l), B2[:], B2K)
        proj_residual("xo", B2, B2K)
        dbg("x2_%d_%d" % (pas, l), xT[:], XK)
        if STOP == "attn":
            return

        rmsnorm_to(B1, B1K, ("g_mlp", l), NT, grp)
        for fb in range(4):
            for blk in range(4):
                slot = wload(l, ("up", fb, blk), 8192)
                wv = wview(slot)
                for j in range(4):
                    f_ = blk * 4 + j
                    res = mm_chunk(lambda k: wv[:, k, j * 128:(j + 1) * 128], KC,
                                   lambda k, c0, n: B1[:, k, c0:c0 + n], B1K, [("W", slot)], grp)
                    for gi, (b, c0, n) in enumerate(res):
                        tk = ("T", 1 + gi, 0)
                        ACT(Tt[:, 1 + gi, 0:n], ps[:, b, 0:n], AF.Relu, [("ps", b)], [tk])
                        TT(B2[:, f_, c0:c0 + n], Tt[:, 1 + gi, 0:n], Tt[:, 1 + gi, 0:n], ALU.mult, [tk], [("B2", f_)])
            for blk in range(4):
                slot = wload(l, ("dn", fb, blk), 8192)
                wv = wview(slot)
                for j in range(4):
                    e_ = blk * 4 + j
                    res = mm_chunk(lambda k: wv[:, k, j * 128:(j + 1) * 128], KC,
                                   lambda k, c0, n: B2[:, k, c0:c0 + n], B2K, [("W", slot)], grp)
                    for (b, c0, n) in res:
                        TT(xT[:, e_, c0:c0 + n], ps[:, b, 0:n], xT[:, e_, c0:c0 + n], ALU.add, [("ps", b), ("x", e_)], [("x", e_)])
        dbg("x3_%d_%d" % (pas, l), xT[:], XK)

    def do_pass(pas):
        ns = 128 if pas == 0 else 64
        NT = MAIN + ns
        grp = [(0, MAIN), (MAIN, ns)]
        if pas == 0:
            tiles = [(i * 128, 128, 128 + i * 128) for i in range(4)] + [(MAIN, 128, 0)]
        else:
            tiles = [(i * 128, 128, 128 + 512 + i * 128) for i in range(4)] + [(MAIN, 64, 1152)]
        retag(B2K, ["STG"])
        for (c0, nt, r0) in tiles:
            DMA("sp", STG[0:nt, :], xin[r0:r0 + nt, :], writes=["STG"])
            for q in range(4):
                (b,) = psbank(1)
                AM("pe", [TP(ps[:, b, j * 128:j * 128 + nt], STG[0:nt, (q * 4 + j) * 128:(q * 4 + j + 1) * 128], ident[0:nt, 0:nt]) for j in range(4)],
                   reads=["STG", "ident"], writes=[("ps", b)])
                ACT(xT[:, q * 4:q * 4 + 4, c0:c0 + nt], ps[:, b, :].rearrange("p (j n) -> p j n", n=128)[:, :, 0:nt], AF.Copy,
                    [("ps", b)], [("x", q * 4 + j) for j in range(4)])
        retag(["STG"], B2K)
        for l in range(NL):
            if l < NLAYERS_RUN:
                do_layer(pas, l, NT, ns, grp, tiles)
        rms_stats(B1, B1K, NT, grp)
        retag(B2K, ["STG"])
        out_tiles = [(i * 128, 128, (0 if pas == 0 else 512) + i * 128) for i in range(4)]
        if pas == 1:
            out_tiles.append((MAIN, 64, 1024))
        for (c0, nt, orow) in out_tiles:
            for k in range(KC):
                STT(xT[:, k, c0:c0 + nt], xT[:, k, c0:c0 + nt], pvc("g_final", k), Tt[:, 0, c0:c0 + nt], ALU.mult, ALU.mult,
                    [("x", k), "pv", ("T", 0, 0), ("T", 0, MAIN)], [("x", k)])
            for q in range(4):
                (b,) = psbank(1)
                AM("pe", [TP(ps[0:nt, b, j * 128:(j + 1) * 128], xT[:, q * 4 + j, c0:c0 + nt], ident[:, :]) for j in range(4)],
                   reads=XK[q * 4:q * 4 + 4] + ["ident"], writes=[("ps", b)])
                ACT(STG[0:nt, q * 512:(q + 1) * 512], ps[0:nt, b, :], AF.Copy, [("ps", b)], ["STG"])
            DMA("sp", y_o[orow:orow + nt, :], STG[0:nt, :], reads=["STG"], is_out=True)
        retag(["STG"], B2K)

    for pas in PASSES_RUN:
        do_pass(pas)

    S.emit(nc, es)
    es.close()
    return nc


NLAYERS_RUN = NL
PASSES_RUN = (0, 1)
STOP = None
WST_SHAPE = None
SKIP = ()


def make_in_maps(inp):
    f = lambda a: np.ascontiguousarray(np.asarray(a, dtype=np.float32))
    I = {k: np.asarray(v) for k, v in inp.items()}
    wstream = np.stack([
        pack_weights(I["w_in"][l], I["w_branch"][l], I["w_out"][l], I["w_xq"][l], I["w_xk"][l], I["w_xv"][l],
                     I["w_xo"][l], I["w_up"][l], I["w_down"][l]) for l in range(NL)])
    lnv = f(np.stack([np.stack([I["ln_v_g"][l], I["ln_v_b"][l]]) for l in range(NL)]))
    wstT = f(np.transpose(I["w_s"], (0, 3, 1, 2)))
    wss = f(np.tile(np.transpose(I["w_s"][:, :, :4, :4], (0, 3, 1, 2)), (1, 16, 1, 16)))
    mask = np.zeros((128, 192), np.float32)
    mask[:, :128] = np.triu(np.ones((128, 128), np.float32))
    blk = np.kron(np.eye(16, dtype=np.float32), np.triu(np.ones((4, 4), np.float32)))
    mask[:64, 128:192] = blk
    bs = np.zeros((NL, 4, 192), np.float32)
    bs[:, :, :128] = I["b_s"]
    bs[:, :, 128:192] = np.tile(I["b_s"][:, :, :4], (1, 1, 16))
    bs = f(bs.reshape(1, -1))
    pw = f(np.stack([np.transpose(I["pool_w"][l].reshape(4, 2, 128, 256), (2, 0, 1, 3)).reshape(128, 2048) for l in range(NL)]))
    ident = np.eye(128, dtype=np.float32)

    def pvec(v):
        return np.asarray(v, np.float32).reshape(-1, 128).T

    maps = []
    for c in range(8):
        b, half = c // 2, c % 2
        pvh = np.zeros((128, NPV), np.float32)
        for nm in ("g_mix", "g_xattn", "g_mem", "g_mlp"):
            for l in range(NL):
                pvh[:, PV[(nm, l)]:PV[(nm, l)] + 16] = pvec(I[nm][l])
        pvh[:, PV["g_final"]:PV["g_final"] + 16] = pvec(I["g_final"])
        for l in range(NL):
            cw = np.asarray(I["conv_w"][l], np.float32)
            pvh[:, PV[("conv_w", l)]:PV[("conv_w", l)] + 248] = cw.T.reshape(8, 128, 31).transpose(1, 0, 2).reshape(128, 248)
            for nm in ("conv_b", "ln_c_g", "ln_c_b", "pool_scale"):
                pvh[:, PV[(nm, l)]:PV[(nm, l)] + 8] = pvec(I[nm][l])
        pvh[:, PV["flag"]] = float(half)
        ic = np.zeros((4, 16), np.float32)
        for gi, w in enumerate((2, 4, 8, 16)):
            for t in range(16):
                ic[gi, t] = 1.0 / (min(w, t + 1) if half == 0 else w)
        pvh[:, PV["icnt"]:PV["icnt"] + 64] = ic.reshape(1, 64)
        xp = I["x_prompt"][b]
        own = xp[half * 1024:(half + 1) * 1024]
        halo = xp[896:1024] if half == 1 else np.zeros((128, D), np.float32)
        xs = I["x_sample"][c * 16:(c + 1) * 16].reshape(64, D)
        xin = f(np.concatenate([halo, own, xs], axis=0))
        maps.append({
            "wst": wstream, "xin": xin, "memin": f(I["mem_prompt"][b]),
            "kcin": f(I["cache_mem_k"][:, c * 16:(c + 1) * 16].reshape(NL, 16, 256, D)),
            "vcin": f(I["cache_mem_v"][:, c * 16:(c + 1) * 16].reshape(NL, 16, 256, D)),
            "scin": f(I["state_conv"][:, c * 16:(c + 1) * 16]), "spin": f(I["state_pool"][:, c * 16:(c + 1) * 16]),
            "pvin": pvh, "lnvin": lnv, "wstin": wstT, "wssin": wss, "maskin": mask, "bsin": bs, "pwin": pw, "identin": ident,
        })
    return maps


def assemble(results):
    y_prompt = np.zeros((4, 2048, D), np.float32)
    y_sample = np.zeros((128, 4, D), np.float32)
    mk = np.zeros((NL, 4, 256, 4, 512), np.float32)
    mv = np.zeros((NL, 4, 256, 4, 512), np.float32)
    cp = np.zeros((NL, 4, 30, 1024), np.float32)
    pp = np.zeros((NL, 4, 15, 1024), np.float32)
    cs = np.zeros((NL, 128, 30, 1024), np.float32)
    pps = np.zeros((NL, 128, 15, 1024), np.float32)
    cv = np.zeros((NL, 128, 4, 1024), np.float32)
    for c, r in enumerate(results):
        b, half = c // 2, c % 2
        y_prompt[b, half * 1024:(half + 1) * 1024] = r["y_o"][:1024]
        y_sample[c * 16:(c + 1) * 16] = r["y_o"][1024:1088].reshape(16, 4, D)
        if half == 0:
            mk[:, b] = r["mk_o"].reshape(NL, 256, 4, 512)
            mv[:, b] = r["mv_o"].reshape(NL, 256, 4, 512)
        else:
            cp[:, b] = r["cp_o"]
            pp[:, b] = r["pp_o"]
        cs[:, c * 16:(c + 1) * 16] = r["cs_o"]
        pps[:, c * 16:(c + 1) * 16] = r["ps_o"]
        cv[:, c * 16:(c + 1) * 16] = r["cv_o"].reshape(NL, 16, 4, 1024)
    return (y_prompt, y_sample, mk, mv, cp, pp, cs, pps, cv)


def kernel(**inputs):
    maps = make_in_maps(inputs)
    nc = build()
    res = run_bass_kernel_spmd(nc, maps, core_ids=list(range(8)))
    return assemble(res.results)
```

```python
import numpy as np
from contextlib import ExitStack
import concourse.bass as bass
import concourse.mybir as mybir
from concourse.bass_utils import run_bass_kernel_spmd

F32 = mybir.dt.float32
BF16 = mybir.dt.bfloat16
AF = mybir.ActivationFunctionType
ALU = mybir.AluOpType
AX = mybir.AxisListType

D = 2048
KC = 16
NL = 2
NTM = 640
MAIN = 512
RMS_EPS = 1e-6
LN_EPS = 1e-5
SLOT = 8192
NSLOT = 3
WTOT = 655360
GELU_C = 0.044715
GELU_S = 1.5957691216057308
ATT_SCALE = 512 ** -0.5

WIN1_CHUNKS = (
    [("u", c) for c in range(8)] + [("v", c) for c in range(8)]
    + [x for c in range(8) for x in (("a", c), ("g", c))]
    + [("p", c) for c in range(8)]
)


def win_col(kind, c):
    base = {"u": 0, "v": 1024, "a": 2048, "g": 3072, "p": 4096}[kind]
    return base + c * 128


def stream_offsets():
    off = {}
    o = 0
    for b in range(10):
        off[("win1", b)] = o; o += 8192
    for d in range(16):
        off[("mg", d)] = o; o += 6144
        off[("mb", d)] = o; o += 3072
    for nm in ("wo", "xq", "xk", "xv", "xo"):
        for b in range(4):
            off[(nm, b)] = o; o += 8192
    for fb in range(4):
        for b in range(4):
            off[("up", fb, b)] = o; o += 8192
        for b in range(4):
            off[("dn", fb, b)] = o; o += 8192
    assert o == WTOT, o
    return off


WOFF = stream_offsets()


def pack_weights(w_in, w_branch, w_out, w_xq, w_xk, w_xv, w_xo, w_up, w_down):
    out = np.empty((128, WTOT), np.float32)

    def put(off, W, c0, ncols, width=None, col_off=0):
        K = W.shape[0]
        kc = K // 128
        width = width or ncols
        dst = out[:, off:off + kc * width].reshape(128, kc, width)[:, :, col_off:col_off + ncols]
        dst[...] = W[:, c0:c0 + ncols].reshape(kc, 128, ncols).transpose(1, 0, 2)

    for b in range(10):
        for j, (k, c) in enumerate(WIN1_CHUNKS[4 * b:4 * b + 4]):
            put(WOFF[("win1", b)], w_in, win_col(k, c), 128, width=512, col_off=j * 128)
    for d in range(16):
        for n in range(3):
            put(WOFF[("mg", d)] + n * 2048, w_in, 5120 + n * 2048 + d * 128, 128)
            put(WOFF[("mb", d)] + n * 1024, w_branch[n], d * 128, 128)
    for nm, W in (("wo", w_out), ("xq", w_xq), ("xk", w_xk), ("xv", w_xv), ("xo", w_xo)):
        for b in range(4):
            put(WOFF[(nm, b)], W, b * 512, 512)
    for fb in range(4):
        for b in range(4):
            put(WOFF[("up", fb, b)], w_up, fb * 2048 + b * 512, 512)
            put(WOFF[("dn", fb, b)], w_down[fb * 2048:(fb + 1) * 2048], b * 512, 512)
    return out


PV = {}
_o = 0
for _nm in ("g_mix", "g_xattn", "g_mem", "g_mlp"):
    for _l in range(NL):
        PV[(_nm, _l)] = _o; _o += 16
PV["g_final"] = _o; _o += 16
for _l in range(NL):
    PV[("conv_w", _l)] = _o; _o += 8 * 31
for _nm in ("conv_b", "ln_c_g", "ln_c_b", "pool_scale"):
    for _l in range(NL):
        PV[(_nm, _l)] = _o; _o += 8
PV["flag"] = _o; _o += 1
PV["icnt"] = _o; _o += 64
NPV = _o


class Op:
    __slots__ = ("eng", "fn", "dma", "deps", "signal", "pos", "sigval", "dsem", "dval", "name")


class Sched:
    ENGS = ("pe", "act", "dve", "pool", "sp")
    SELF_SYNC = {"pe": False, "act": True, "dve": True, "pool": True, "sp": False}
    NDSEM = 8

    def __init__(self):
        self.streams = {e: [] for e in self.ENGS}
        self.lastw = {}
        self.readers = {}
        self.dma_slot_prev = {}
        self.dma_count = {e: 0 for e in self.ENGS}
        self.out_dmas = []

    def add(self, eng, fn, reads=(), writes=(), dma=False, name=""):
        op = Op()
        op.eng, op.fn, op.dma, op.signal, op.name = eng, fn, dma, False, name
        op.pos = len(self.streams[eng])
        deps = set()
        for r in reads:
            w = self.lastw.get(r)
            if w is not None:
                deps.add(w)
        for w_ in writes:
            w = self.lastw.get(w_)
            if w is not None:
                deps.add(w)
            for rd in self.readers.get(w_, ()):
                deps.add(rd)
        if dma:
            k = self.dma_count[eng]
            self.dma_count[eng] += 1
            slot = (eng, k % self.NDSEM)
            op.dsem = slot
            op.dval = 16 * (k // self.NDSEM + 1)
            prev = self.dma_slot_prev.get(slot)
            if prev is not None:
                deps.add(prev)
            self.dma_slot_prev[slot] = op
        deps.discard(op)
        op.deps = deps
        for d in deps:
            if not d.dma:
                d.signal = True
        for r in reads:
            self.readers.setdefault(r, []).append(op)
        for w_ in writes:
            self.lastw[w_] = op
            self.readers[w_] = []
        self.streams[eng].append(op)
        return op

    def emit(self, nc, es):
        for e in self.ENGS:
            c = 0
            for op in self.streams[e]:
                if op.signal:
                    c += 1
                    op.sigval = c
        sems = {e: es.enter_context(nc.semaphore("s_" + e)) for e in self.ENGS}
        dsems = {}
        for e in self.ENGS:
            if self.dma_count[e]:
                for i in range(self.NDSEM):
                    dsems[(e, i)] = es.enter_context(nc.semaphore("d_%s%d" % (e, i)))
        block = es.enter_context(nc.Block())
        streams = self.streams
        SELF = self.SELF_SYNC

        def run(ename, eng):
            seen = {e: 0 for e in self.ENGS}
            seen_d = {}
            for op in streams[ename]:
                need = {}
                for d in op.deps:
                    if d.dma:
                        if seen_d.get(d.dsem, 0) < d.dval:
                            seen_d[d.dsem] = d.dval
                            eng.wait_ge(dsems[d.dsem], d.dval)
                    else:
                        if d.eng == ename and not SELF[ename]:
                            continue
                        if d.sigval > need.get(d.eng, 0):
                            need[d.eng] = d.sigval
                for e2, v in need.items():
                    if v > seen[e2]:
                        seen[e2] = v
                        eng.wait_ge(sems[e2], v)
                ins = op.fn(eng)
                if op.dma:
                    ins.then_inc(dsems[op.dsem], 16)
                elif op.signal:
                    ins.then_inc(sems[ename], 1)
            if ename == "sp":
                for d in self.out_dmas:
                    eng.wait_ge(dsems[d.dsem], d.dval)

        @block.tensor
        def _(e):
            run("pe", e)

        @block.scalar
        def _(e):
            run("act", e)

        @block.vector
        def _(e):
            run("dve", e)

        @block.gpsimd
        def _(e):
            run("pool", e)

        @block.sync
        def _(e):
            run("sp", e)


def build(debug=()):
    nc = bass.Bass("TRN2", target_bir_lowering=False)
    dram_in = lambda n, s: nc.dram_tensor(n, list(s), F32, kind="ExternalInput").ap()
    dram_out = lambda n, s, dt=F32: nc.dram_tensor(n, list(s), dt, kind="ExternalOutput").ap()
    wst = dram_in("wst", WST_SHAPE if WST_SHAPE else ((NL, 128, WTOT) if NLAYERS_RUN else (1, 128, 8)))
    xin = dram_in("xin", (1216, D))
    memin = dram_in("memin", (256, D))
    kcin = dram_in("kcin", (NL, 16, 256, D))
    vcin = dram_in("vcin", (NL, 16, 256, D))
    scin = dram_in("scin", (NL, 16, 30, 1024))
    spin = dram_in("spin", (NL, 16, 15, 1024))
    pvin = dram_in("pvin", (128, NPV))
    lnvin = dram_in("lnvin", (NL, 2, 1024))
    wstin = dram_in("wstin", (NL, 128, 4, 128))
    wssin = dram_in("wssin", (NL, 64, 4, 64))
    maskin = dram_in("maskin", (128, 192))
    bsin = dram_in("bsin", (1, NL * 4 * 192))
    pwin = dram_in("pwin", (NL, 128, 2048))
    identin = dram_in("identin", (128, 128))

    y_o = dram_out("y_o", (1088, D))
    mk_o = dram_out("mk_o", (NL, 256, D))
    mv_o = dram_out("mv_o", (NL, 256, D))
    cp_o = dram_out("cp_o", (NL, 30, 1024))
    pp_o = dram_out("pp_o", (NL, 15, 1024))
    cs_o = dram_out("cs_o", (NL, 16, 30, 1024))
    ps_o = dram_out("ps_o", (NL, 16, 15, 1024))
    cv_o = dram_out("cv_o", (NL, 64, 1024))
    dbg_o = {}
    for (nm, shp, dts) in debug:
        dbg_o[nm] = dram_out("dbg_" + nm, shp, BF16 if dts == "bf16" else F32)

    S = Sched()
    es = ExitStack()
    sb = lambda n, s, dt: es.enter_context(nc.sbuf_tensor(n, list(s), dt))
    xT = sb("xT", (128, KC, NTM), F32)
    B1 = sb("B1", (128, KC, NTM), BF16)
    B2 = sb("B2", (128, KC, NTM), BF16)
    Yb = sb("Yb", (128, 24, NTM), BF16)
    Wr = sb("Wr", (128, NSLOT, SLOT), BF16)
    Tt = sb("Tt", (128, 4, NTM), F32)
    PTb = sb("PTb", (128, 2, NTM), BF16)
    pv = sb("pv", (128, NPV), F32)
    ident = sb("ident", (128, 128), F32)
    identb = sb("identb", (128, 128), BF16)
    onesb = sb("onesb", (128, 128), BF16)
    ones1 = sb("ones1", (1, 128), F32)
    bsrow = sb("bsrow", (1, NL * 4 * 192), F32)
    maskt = sb("maskt", (128, 192), F32)
    wsm = sb("wsm", (128, 4, 192), BF16)
    wtmp = sb("wtmp", (128, 4, 192), F32)
    pwb = sb("pwb", (128, 2048), BF16)
    hconv = sb("hconv", (128, NL, 8, 30), F32)
    hpool = sb("hpool", (128, NL, 8, 15), F32)
    kst = sb("kst", (128, 2, 512), F32)
    ostg = sb("ostg", (128, 1024), F32)
    stt = sb("stt", (128, 16), F32)
    t3b = sb("t3b", (128, 64), F32)
    rss = sb("rss", (128, 2, 64), F32)
    pts = sb("pts", (128, 2, 32), BF16)
    ots = sb("ots", (128, 16, 64), BF16)
    epst = sb("epst", (128, 2), F32)
    ps = es.enter_context(nc.psum_tensor("ps", [128, 6, 512], F32))
    psb = es.enter_context(nc.psum_tensor("psb", [128, 2, 1024], BF16))

    B2f = B2[:].rearrange("p k n -> p (k n)")
    B2w = B2f.bitcast(F32)
    STG = B2w[:, 0:2048]
    VT0 = B2w[:, 0:1024]; VT1 = B2w[:, 1024:2048]; LNVG = B2w[:, 2048:3072]; LNVB = B2w[:, 3072:4096]
    VB = B2w[:, 4096:4608].bitcast(BF16)
    VSET = ["VT0", "VT1", "LNV", "VB"]
    GLm = B2w[:, 0:542]; GLs = B2w[:, 544:1088]; Pm = B2w[:, 1088:1615]; Psb_ = B2w[:, 1616:1920]
    PMc = B2w[:, 1920:2560].bitcast(BF16).rearrange("p (k n) -> p k n", n=NTM)
    ACC = B2w[:, 2560:3200]
    ST1 = B2w[:, 3200:3712]
    CSET = ["GLm", "GLs", "Pm", "Ps", ("PMc", 0), ("PMc", 1), "ACC", "ST1"]
    GLs3 = GLs.rearrange("p (s n) -> p s n", n=34)
    Ps3 = Psb_.rearrange("p (s n) -> p s n", n=19)
    B1f = B1[:].rearrange("p k n -> p (k n)")
    Yf = Yb[:].rearrange("p k n -> p (k n)")
    MTB = Yf[:, 0:4096].rearrange("p (k n) -> p k n", n=256)
    KT = Yf[:, 4096:8192].rearrange("p (k n) -> p k n", n=256)
    VP = Yf[:, 8192:12288].rearrange("p (k n) -> p k n", n=2048)
    MEMSTG = B1f[:, 0:4096].bitcast(F32)
    KRAW0 = B1f[:, 0:2048]
    VS0 = B1f[:, 2048:6144].rearrange("p (k n) -> p k n", n=2048)
    KTS = B1f[:, 6144:10240].rearrange("p (k n) -> p k n", n=256)
    VS1 = Yf[:, 0:4096].rearrange("p (k n) -> p k n", n=2048)
    KRAW1 = Yf[:, 12288:14336]
    MEMK = ["KR0", "VS0"]
    ASET = ["MTB", "KT", "VP", "KR0", "VS0", "KR1", ("KTS", 0), ("KTS", 1)]

    XK = [("x", k) for k in range(KC)]
    B1K = [("B1", k) for k in range(KC)]
    B2K = [("B2", k) for k in range(KC)]
    YK = [("Y", k) for k in range(24)]

    def A(eng, meth, reads=(), writes=(), **kw):
        return S.add(eng, lambda e: getattr(e, meth)(**kw), reads=reads, writes=writes, name=meth)

    def AM(eng, calls, reads=(), writes=()):
        def fn(e):
            ins = None
            for (meth, kw) in calls:
                ins = getattr(e, meth)(**kw)
            return ins
        return S.add(eng, fn, reads=reads, writes=writes, name="multi")

    def CHAIN(eng, calls, reads=(), writes=()):
        grp_ = []
        for (meth, kw, short) in calls:
            if short:
                if grp_:
                    AM(eng, grp_, reads, writes)
                    grp_ = []
                AM(eng, [(meth, kw)], reads, writes)
            else:
                grp_.append((meth, kw))
        if grp_:
            AM(eng, grp_, reads, writes)

    def DMA(eng, out, in_, reads=(), writes=(), is_out=False):
        op = S.add(eng, lambda e: e.dma_start(out=out, in_=in_), reads=reads, writes=writes, dma=True, name="dma")
        if is_out:
            S.out_dmas.append(op)
        return op

    def MM(out, lhsT, rhs, start, stop):
        return ("matmul", dict(out=out, lhsT=lhsT, rhs=rhs, start=start, stop=stop))

    def TP(out, in_, idn):
        return ("transpose", dict(out=out, in_=in_, identity=idn))

    def ACT(out, in_, func, reads, writes, **kw):
        return A("act", "activation", reads, writes, out=out, in_=in_, func=func, **kw)

    def TT(out, in0, in1, op, reads, writes, eng="dve"):
        return A(eng, "tensor_tensor", reads, writes, out=out, in0=in0, in1=in1, op=op)

    def TS(out, in0, s1, s2, op0, op1, reads, writes, eng="dve"):
        if s2 is None:
            return A(eng, "tensor_scalar", reads, writes, out=out, in0=in0, scalar1=s1, scalar2=None, op0=op0)
        return A(eng, "tensor_scalar", reads, writes, out=out, in0=in0, scalar1=s1, scalar2=s2, op0=op0, op1=op1)

    def STT(out, in0, scalar, in1, op0, op1, reads, writes, eng="dve"):
        return A(eng, "scalar_tensor_tensor", reads, writes, out=out, in0=in0, scalar=scalar, in1=in1, op0=op0, op1=op1)

    def retag(old, new):
        S.add("dve", lambda e: e.engine_nop(), writes=list(old) + list(new), name="retag")

    psrr = [0]

    def psbank(n=1):
        r = []
        for _ in range(n):
            r.append(psrr[0] % 6)
            psrr[0] += 1
        return r

    wcount = [0]

    preloaded = {}

    def wprefetch(layer, key, size):
        preloaded[(layer, key)] = wload(layer, key, size)

    def wload(layer, key, size):
        if (layer, key) in preloaded:
            return preloaded.pop((layer, key))
        slot = wcount[0] % NSLOT
        wcount[0] += 1
        off = WOFF[key]
        DMA("pool", Wr[:, slot, 0:size], wst[layer, :, off:off + size], writes=[("W", slot)])
        return slot

    def dbg(nm, src_ap, reads):
        if nm in dbg_o:
            DMA("sp", dbg_o[nm], src_ap, reads=reads, is_out=True)

    def pvc(key, i=0, n=1):
        return pv[:, PV[key] + i:PV[key] + i + n]

    DMA("sp", pv[:], pvin[:, :], writes=["pv"])
    DMA("sp", ident[:], identin[:, :], writes=["ident"])
    DMA("sp", bsrow[:], bsin[:, :], writes=["bsrow"])
    DMA("sp", maskt[:], maskin[:, :], writes=["maskt"])
    A("dve", "tensor_copy", ["ident"], ["identb"], out=identb[:], in_=ident[:])
    A("dve", "memset", [], ["onesb"], ap=onesb[:], constant=1.0)
    A("dve", "memset", [], ["ones1"], ap=ones1[:], constant=1.0)
    A("dve", "memset", [], ["epst"], ap=epst[:, 0:1], constant=RMS_EPS)
    A("dve", "memset", [], ["epst"], ap=epst[:, 1:2], constant=LN_EPS)

    def rsqrt_(out, in_, scale, eps_col, reads, writes, npart=128):
        ACT(out, in_, AF.Sqrt, list(reads) + ["epst"], writes, scale=scale, bias=epst[0:npart, eps_col:eps_col + 1])
        A("dve", "reciprocal", writes, writes, out=out, in_=out)

    def rms_stats(sqbuf, sqkeys, NT, grp):
        for k in range(KC):
            ACT(sqbuf[:, k, 0:NT], xT[:, k, 0:NT], AF.Square, [("x", k)], [sqkeys[k]])
        for (c0, n) in grp:
            (b,) = psbank(1)
            AM("pe", [MM(ps[:, b, 0:n], onesb[:], sqbuf[:, k, c0:c0 + n], k == 0, k == KC - 1) for k in range(KC)],
               reads=list(sqkeys) + ["onesb"], writes=[("ps", b)])
            rsqrt_(Tt[:, 0, c0:c0 + n], ps[:, b, 0:n], 1.0 / D, 0, [("ps", b)], [("T", 0, c0)])

    def rmsnorm_to(dst, dstkeys, gkey, NT, grp):
        rms_stats(dst, dstkeys, NT, grp)
        tk = [("T", 0, c0) for (c0, n) in grp]
        for k in range(KC):
            STT(dst[:, k, 0:NT], xT[:, k, 0:NT], pvc(gkey, k), Tt[:, 0, 0:NT], ALU.mult, ALU.mult,
                [("x", k), "pv"] + tk, [dstkeys[k]])

    def mm_chunk(lhs_fn, nkc, rhs_fn, rhskeys, wkeys, grp):
        banks = psbank(len(grp))
        res = []
        for (c0, n), b in zip(grp, banks):
            AM("pe", [MM(ps[:, b, 0:n], lhs_fn(k), rhs_fn(k, c0, n), k == 0, k == nkc - 1) for k in range(nkc)],
               reads=list(rhskeys) + list(wkeys), writes=[("ps", b)])
            res.append((b, c0, n))
        return res

    def gelu_ps(src, srckey, t, tk, out_ap, out_keys):
        ACT(t, src, AF.Square, [srckey], [tk])
        TS(t, t, GELU_C, 1.0, ALU.mult, ALU.add, [tk], [tk])
        TT(t, src, t, ALU.mult, [tk, srckey], [tk])
        ACT(t, t, AF.Sigmoid, [tk], [tk], scale=GELU_S)
        TT(out_ap, src, t, ALU.mult, [tk, srckey], out_keys)

    def wview(slot):
        return Wr[:, slot, :].rearrange("p (k n) -> p k n", n=512)

    def do_layer(pas, l, NT, ns, grp, tiles):
        DMA("pool", pwb[:], pwin[l, :, :], writes=["pwb"])
        DMA("sp", wtmp[:, :, 0:128], wstin[l, :, :, :], writes=["wtmp"])
        DMA("sp", wtmp[0:64, :, 128:192], wssin[l, :, :, :], writes=["wtmp"])
        for h in range(4):
            TT(wsm[:, h, 0:128], wtmp[:, h, 0:128], maskt[:, 0:128], ALU.mult, ["wtmp", "maskt"], ["wsm"])
            TT(wsm[0:64, h, 128:192], wtmp[0:64, h, 128:192], maskt[0:64, 128:192], ALU.mult, ["wtmp", "maskt"], ["wsm"])

        rmsnorm_to(B1, B1K, ("g_mix", l), NT, grp)
        dbg("h_%d_%d" % (pas, l), B1[:], B1K)

        retag(B2K, VSET)
        DMA("sp", LNVG, lnvin[l, 0:1, :].partition_broadcast(128), writes=["LNV"])
        DMA("sp", LNVB, lnvin[l, 1:2, :].partition_broadcast(128), writes=["LNV"])

        for blk in range(2):
            slot = wload(l, ("win1", blk), 8192)
            wv = wview(slot)
            for j in range(4):
                c = blk * 4 + j
                res = mm_chunk(lambda k: wv[:, k, j * 128:(j + 1) * 128], KC,
                               lambda k, c0, n: B1[:, k, c0:c0 + n], B1K, [("W", slot)], grp)
                for gi, (b, c0, n) in enumerate(res):
                    gelu_ps(ps[:, b, 0:n], ("ps", b), Tt[:, 1 + gi, 0:n], ("T", 1 + gi, 0), Yb[:, c, c0:c0 + n], [("Y", c)])

        slots_v = [wload(l, ("win1", 2), 8192), wload(l, ("win1", 3), 8192)]
        for ti, (c0, nt, r0) in enumerate(tiles):
            banks = psbank(2)
            for hf in range(2):
                wv = wview(slots_v[hf])
                AM("pe", [MM(ps[0:nt, banks[hf], :], B1[:, k, c0:c0 + nt], wv[:, k, :], k == 0, k == KC - 1) for k in range(KC)],
                   reads=B1K + [("W", slots_v[hf])], writes=[("ps", banks[hf])])
            for hf in range(2):
                b = banks[hf]
                sl = slice(hf * 512, (hf + 1) * 512)
                gelu_ps(ps[0:nt, b, :], ("ps", b), VT1[0:nt, sl], "VT1", VT0[0:nt, sl], ["VT0"])
            A("dve", "reduce_sum", ["VT0"], ["stt"], out=stt[0:nt, 0:1], in_=VT0[0:nt, :], axis=AX.X)
            TS(stt[0:nt, 1:2], stt[0:nt, 0:1], -1.0 / 1024, None, ALU.mult, None, ["stt"], ["stt"])
            ACT(VT1[0:nt, :], VT0[0:nt, :], AF.Square, ["VT0", "stt"], ["VT1", "stt"], bias=stt[0:nt, 1:2], accum_out=stt[0:nt, 2:3])
            rsqrt_(stt[0:nt, 3:4], stt[0:nt, 2:3], 1.0 / 1024, 1, ["stt"], ["stt"], npart=nt)
            TS(VT0[0:nt, :], VT0[0:nt, :], stt[0:nt, 1:2], stt[0:nt, 3:4], ALU.add, ALU.mult, ["VT0", "stt"], ["VT0"])
            TT(VT0[0:nt, :], VT0[0:nt, :], LNVG[0:nt, :], ALU.mult, ["VT0", "LNV"], ["VT0"])
            TT(VT0[0:nt, :], VT0[0:nt, :], LNVB[0:nt, :], ALU.add, ["VT0", "LNV"], ["VT0"])
            ACT(VB[0:nt, :], VT0[0:nt, :], AF.Copy, ["VT0"], ["VB"])
            sample_tile = (pas == 1 and ti == 4)
            if sample_tile:
                DMA("sp", cv_o[l, :, :], VT0[0:64, :], reads=["VT0"], is_out=True)
            banks2 = psbank(2)
            for half in range(2):
                b = banks2[half]
                calls = []
                for j in range(4):
                    fc = half * 4 + j
                    hh = fc // 2
                    w_ap = wsm[0:nt, hh, 128:128 + nt] if sample_tile else wsm[:, hh, 0:128]
                    bo = ((l * 4 + hh) * 192) + (128 if sample_tile else 0)
                    calls.append(MM(ps[:, b, j * 128:j * 128 + nt], VB[0:nt, fc * 128:(fc + 1) * 128], w_ap, True, False))
                    calls.append(MM(ps[:, b, j * 128:j * 128 + nt], ones1[0:1, :], bsrow[0:1, bo:bo + nt], False, True))
                AM("pe", calls, reads=["VB", "wsm", "bsrow", "ones1"], writes=[("ps", b)])
                yk = [("Y", half * 4 + j) for j in range(4)]
                TT(Yb[:, half * 4:half * 4 + 4, c0:c0 + nt], ps[:, b, :].rearrange("p (j n) -> p j n", n=128)[:, :, 0:nt],
                   Yb[:, half * 4:half * 4 + 4, c0:c0 + nt], ALU.mult, [("ps", b)] + yk, yk)
        dbg("ya_%d_%d" % (pas, l), Yb[:, 0:8, :], YK[0:8])
        if STOP == "v":
            retag(VSET, B2K)
            return

        retag(VSET, CSET)
        if pas == 0:
            A("dve", "memset", [], ["GLs"], ap=GLs[:, 0:30], constant=0.0)
            A("dve", "memset", [], ["Ps"], ap=Psb_[:, 0:15], constant=0.0)
        for blk in range(4):
            slot = wload(l, ("win1", 4 + blk), 8192)
            wv = wview(slot)
            for pr in range(2):
                c = blk * 2 + pr
                ra = mm_chunk(lambda k: wv[:, k, (2 * pr) * 128:(2 * pr + 1) * 128], KC,
                              lambda k, c0, n: B1[:, k, c0:c0 + n], B1K, [("W", slot)], grp)
                rg = mm_chunk(lambda k: wv[:, k, (2 * pr + 1) * 128:(2 * pr + 2) * 128], KC,
                              lambda k, c0, n: B1[:, k, c0:c0 + n], B1K, [("W", slot)], grp)
                if pas == 1:
                    DMA("sp", ST1[0:120, 0:512].rearrange("p (g n) -> p g n", n=128),
                        scin[l, :, :, c * 128:(c + 1) * 128].rearrange("(g s) r n -> (s r) g n", g=4), writes=["ST1"])
                    (b,) = psbank(1)
                    AM("pe", [TP(ps[:, b, g * 120:(g + 1) * 120], ST1[0:120, g * 128:(g + 1) * 128], ident[0:120, 0:120]) for g in range(4)],
                       reads=["ST1", "ident"], writes=[("ps", b)])
                    ACT(GLs3[:, :, 0:30], ps[:, b, 0:480].rearrange("p (s r) -> p s r", r=30), AF.Copy, [("ps", b)], ["GLs"])
                for gi in range(2):
                    (ba, c0, n) = ra[gi]
                    (bg, _, _) = rg[gi]
                    tk = ("T", 1 + gi, 0)
                    t = Tt[:, 1 + gi, 0:n]
                    ACT(t, ps[:, bg, 0:n], AF.Sigmoid, [("ps", bg)], [tk])
                    in0 = ps[:, ba, 0:n]
                    if gi == 0:
                        o_ap = GLm[:, 30:542]; ok = "GLm"
                    elif pas == 0:
                        o_ap = GLs[:, 30:158]; ok = "GLs"
                    else:
                        o_ap = GLs3[:, :, 30:34]; ok = "GLs"
                        t = t.rearrange("p (s r) -> p s r", r=4)
                        in0 = in0.rearrange("p (s r) -> p s r", r=4)
                    TT(o_ap, in0, t, ALU.mult, [tk, ("ps", ba)], [ok])
                if pas == 0:
                    TS(GLm[:, 0:30], GLs[:, 128:158], pvc("flag"), None, ALU.mult, None, ["GLs", "pv"], ["GLm"])
                    A("dve", "tensor_copy", ["GLm"], ["hconv"], out=hconv[:, l, c, :], in_=GLm[:, 512:542])
                else:
                    A("dve", "tensor_copy", ["hconv"], ["GLm"], out=GLm[:, 0:30], in_=hconv[:, l, c, :])
                    A("dve", "tensor_copy", ["GLm"], ["hconv"], out=hconv[:, l, c, :], in_=GLm[:, 512:542])
                cw = PV[("conv_w", l)] + c * 31
                cb = pvc(("conv_b", l), c)
                specs = [(ACC[:, 0:512], lambda k: GLm[:, k:k + 512])]
                if pas == 0:
                    specs.append((ACC[:, 512:640], lambda k: GLs[:, k:k + 128]))
                else:
                    specs.append((ACC[:, 512:576].rearrange("p (s r) -> p s r", r=4), lambda k: GLs3[:, :, k:k + 4]))
                calls = []
                for si, (o, src) in enumerate(specs):
                    short = (si == 1 and pas == 1)
                    calls.append(("tensor_scalar", dict(out=o, in0=src(0), scalar1=pv[:, cw:cw + 1], scalar2=cb, op0=ALU.mult, op1=ALU.add), short))
                    for k in range(1, 31):
                        calls.append(("scalar_tensor_tensor", dict(out=o, in0=src(k), scalar=pv[:, cw + k:cw + k + 1], in1=o, op0=ALU.mult, op1=ALU.add), short))
                CHAIN("dve", calls, reads=["GLm", "GLs", "pv"], writes=["ACC"])
                ACT(Yb[:, 8 + c, 0:NT], ACC[:, 0:NT], AF.Copy, ["ACC"], [("Y", 8 + c)])
                if pas == 1:
                    (b,) = psbank(1)
                    A("dve", "tensor_copy", ["GLs"], ["t3b"], out=t3b[:].rearrange("p (s r) -> p s r", r=4), in_=GLs3[:, :, 30:34])
                    AM("pe", [TP(ps[0:64, b, 0:128], t3b[:], ident[:, :])], reads=["t3b", "ident"], writes=[("ps", b)])
                    A("dve", "tensor_copy", [("ps", b)], ["ostg"], out=ostg[0:64, c * 128:(c + 1) * 128], in_=ps[0:64, b, 0:128])
        if pas == 1:
            for s in range(16):
                DMA("sp", cs_o[l, s, 26:30, :], ostg[s * 4:(s + 1) * 4, :], reads=["ostg"], is_out=True)
            DMA("sp", cs_o[l, :, 0:26, :], scin[l, :, 4:30, :], is_out=True)
            banks = psbank(2)
            for half in range(2):
                b = banks[half]
                AM("pe", [TP(ps[0:30, b, j * 128:(j + 1) * 128], hconv[:, l, half * 4 + j, :], ident[:, :]) for j in range(4)],
                   reads=["hconv", "ident"], writes=[("ps", b)])
                A("dve", "tensor_copy", [("ps", b)], [("kst", 0), ("kst", 1)], out=kst[0:30, half, :], in_=ps[0:30, b, :])
            DMA("sp", cp_o[l, :, :], kst[0:30, :, :].rearrange("p a n -> p (a n)"), reads=[("kst", 0), ("kst", 1)], is_out=True)

        for (c0, n) in grp:
            b1, b2 = psbank(2)
            AM("pe", [MM(ps[:, b1, 0:n], onesb[:], Yb[:, 8 + c, c0:c0 + n], c == 0, c == 7) for c in range(8)],
               reads=YK[8:16] + ["onesb"], writes=[("ps", b1)])
            for c in range(8):
                ACT(PTb[:, c % 2, 0:n], Yb[:, 8 + c, c0:c0 + n], AF.Square, [("Y", 8 + c)], [("PT", c % 2)])
                AM("pe", [MM(ps[:, b2, 0:n], onesb[:], PTb[:, c % 2, 0:n], c == 0, c == 7)],
                   reads=[("PT", c % 2), "onesb"], writes=[("ps", b2)])
            TS(Tt[:, 1, c0:c0 + n], ps[:, b1, 0:n], 1.0 / 1024, None, ALU.mult, None, [("ps", b1)], [("T", 1, c0)])
            TT(Tt[:, 2, c0:c0 + n], Tt[:, 1, c0:c0 + n], Tt[:, 1, c0:c0 + n], ALU.mult, [("T", 1, c0)], [("T", 2, c0)])
            STT(Tt[:, 2, c0:c0 + n], ps[:, b2, 0:n], 1.0 / 1024, Tt[:, 2, c0:c0 + n], ALU.mult, ALU.subtract,
                [("ps", b2), ("T", 2, c0)], [("T", 2, c0)])
            rsqrt_(Tt[:, 2, c0:c0 + n], Tt[:, 2, c0:c0 + n], 1.0, 1, [("T", 2, c0)], [("T", 2, c0)])
        tks = [("T", 1, 0), ("T", 1, MAIN), ("T", 2, 0), ("T", 2, MAIN)]
        for c in range(8):
            TT(Tt[:, 3, 0:NT], Yb[:, 8 + c, 0:NT], Tt[:, 1, 0:NT], ALU.subtract, [("Y", 8 + c)] + tks, ["T3"])
            TT(Tt[:, 3, 0:NT], Tt[:, 3, 0:NT], Tt[:, 2, 0:NT], ALU.mult, ["T3"] + tks, ["T3"])
            ACT(Yb[:, 8 + c, 0:NT], Tt[:, 3, 0:NT], AF.Silu, ["T3", "pv"], [("Y", 8 + c)],
                scale=pvc(("ln_c_g", l), c), bias=pvc(("ln_c_b", l), c))
        dbg("yb_%d_%d" % (pas, l), Yb[:, 8:16, :], YK[8:16])
        if STOP == "conv":
            retag(CSET, B2K)
            return

        pw4 = pwb[:].rearrange("p (g k n) -> p g k n", g=4, k=2)
        for blk in range(2):
            slot = wload(l, ("win1", 8 + blk), 8192)
            wv = wview(slot)
            for j in range(4):
                c = blk * 4 + j
                gi_ = c // 2
                w = 2 << gi_
                res = mm_chunk(lambda k: wv[:, k, j * 128:(j + 1) * 128], KC,
                               lambda k, c0, n: B1[:, k, c0:c0 + n], B1K, [("W", slot)], grp)
                if pas == 1:
                    DMA("sp", ST1[0:120, 0:256].rearrange("p (g n) -> p g n", n=128),
                        spin[l, :, :, c * 128:(c + 1) * 128].rearrange("(g s) r n -> (s r) g n", g=2), writes=["ST1"])
                    (b,) = psbank(1)
                    AM("pe", [TP(ps[:, b, g * 120:(g + 1) * 120], ST1[0:120, g * 128:(g + 1) * 128], ident[0:120, 0:120]) for g in range(2)],
                       reads=["ST1", "ident"], writes=[("ps", b)])
                    ACT(Ps3[:, :, 0:15], ps[:, b, 0:240].rearrange("p (s r) -> p s r", r=15), AF.Copy, [("ps", b)], ["Ps"])
                (bm, _, _), (bs_, _, nsn) = res
                ACT(Pm[:, 15:527], ps[:, bm, 0:512], AF.Copy, [("ps", bm)], ["Pm"])
                if pas == 0:
                    ACT(Psb_[:, 15:143], ps[:, bs_, 0:128], AF.Copy, [("ps", bs_)], ["Ps"])
                    TS(Pm[:, 0:15], Psb_[:, 128:143], pvc("flag"), None, ALU.mult, None, ["Ps", "pv"], ["Pm"])
                    A("dve", "tensor_copy", ["Pm"], ["hpool"], out=hpool[:, l, c, :], in_=Pm[:, 512:527])
                else:
                    ACT(Ps3[:, :, 15:19], ps[:, bs_, 0:64].rearrange("p (s r) -> p s r", r=4), AF.Copy, [("ps", bs_)], ["Ps"])
                    A("dve", "tensor_copy", ["hpool"], ["Pm"], out=Pm[:, 0:15], in_=hpool[:, l, c, :])
                    A("dve", "tensor_copy", ["Pm"], ["hpool"], out=hpool[:, l, c, :], in_=Pm[:, 512:527])
                    (b,) = psbank(1)
                    A("dve", "tensor_copy", ["Ps"], ["t3b"], out=t3b[:].rearrange("p (s r) -> p s r", r=4), in_=Ps3[:, :, 15:19])
                    AM("pe", [TP(ps[0:64, b, 0:128], t3b[:], ident[:, :])], reads=["t3b", "ident"], writes=[("ps", b)])
                    A("dve", "tensor_copy", [("ps", b)], ["ostg"], out=ostg[0:64, c * 128:(c + 1) * 128], in_=ps[0:64, b, 0:128])
                calls = []
                for which in range(2):
                    if which == 0:
                        L = 527
                        src = lambda a, b_: Pm[:, a:b_]
                        t1 = lambda a, b_: Tt[:, 1, a:b_]
                        t2 = lambda a, b_: Tt[:, 2, a:b_]
                    elif pas == 0:
                        L = 143
                        src = lambda a, b_: Psb_[:, a:b_]
                        t1 = lambda a, b_: ACC[:, a:b_]
                        t2 = lambda a, b_: ACC[:, 160 + a:160 + b_]
                    else:
                        L = 19
                        src = lambda a, b_: Ps3[:, :, a:b_]
                        A3 = ACC[:, 0:304].rearrange("p (s n) -> p s n", n=19)
                        B3 = ACC[:, 320:624].rearrange("p (s n) -> p s n", n=19)
                        t1 = lambda a, b_, A3=A3: A3[:, :, a:b_]
                        t2 = lambda a, b_, B3=B3: B3[:, :, a:b_]
                    cur = src
                    sh = 1
                    bufs = [t1, t2]
                    bi = 0
                    lo = 0
                    while sh < w:
                        lo2 = lo + sh
                        nxt = bufs[bi]
                        calls.append(("tensor_tensor", dict(out=nxt(lo2, L), in0=cur(lo2, L), in1=cur(lo2 - sh, L - sh), op=ALU.add), which == 1 and pas == 1))
                        cur = nxt
                        bi ^= 1
                        lo = lo2
                        sh *= 2
                    hist = 15
                    if which == 0:
                        calls.append(("scalar_tensor_tensor", dict(out=PMc[:, c % 2, 0:512], in0=cur(hist, L), scalar=1.0 / w, in1=src(hist, L),
                                                                   op0=ALU.mult, op1=ALU.subtract), False))
                        if pas == 0:
                            ic = PV["icnt"] + (c // 2) * 16
                            calls.append(("tensor_tensor", dict(out=Tt[:, 3, 0:16], in0=cur(hist, hist + 16), in1=pv[:, ic:ic + 16], op=ALU.mult), True))
                            calls.append(("tensor_tensor", dict(out=PMc[:, c % 2, 0:16], in0=Tt[:, 3, 0:16], in1=src(hist, hist + 16), op=ALU.subtract), True))
                    else:
                        if pas == 0:
                            o = PMc[:, c % 2, 512:640]
                        else:
                            o = PMc[:, c % 2, 512:576].rearrange("p (s r) -> p s r", r=4)
                        calls.append(("scalar_tensor_tensor", dict(out=o, in0=cur(hist, L), scalar=1.0 / w, in1=src(hist, L), op0=ALU.mult, op1=ALU.subtract), pas == 1))
                CHAIN("dve", calls, reads=["Pm", "Ps", "pv"],
                   writes=[("PMc", c % 2), ("T", 1, 0), ("T", 2, 0), ("T", 1, MAIN), ("T", 2, MAIN), "ACC", "T3"])
                if c % 2 == 1:
                    for oc in range(2):
                        res2 = mm_chunk(lambda k: pw4[:, gi_, k, oc * 128:(oc + 1) * 128], 2,
                                        lambda k, c0, n: PMc[:, k, c0:c0 + n], [("PMc", 0), ("PMc", 1)], ["pwb"], grp)
                        for (b, c0, n) in res2:
                            ACT(Yb[:, 16 + 2 * gi_ + oc, c0:c0 + n], ps[:, b, 0:n], AF.Copy, [("ps", b), "pv"], [("Y", 16 + 2 * gi_ + oc)],
                                scale=pvc(("pool_scale", l), 2 * gi_ + oc))
        if pas == 1:
            for s in range(16):
                DMA("sp", ps_o[l, s, 11:15, :], ostg[s * 4:(s + 1) * 4, :], reads=["ostg"], is_out=True)
            DMA("sp", ps_o[l, :, 0:11, :], spin[l, :, 4:15, :], is_out=True)
            banks = psbank(2)
            for half in range(2):
                b = banks[half]
                AM("pe", [TP(ps[0:15, b, j * 128:(j + 1) * 128], hpool[:, l, half * 4 + j, :], ident[:, :]) for j in range(4)],
                   reads=["hpool", "ident"], writes=[("ps", b)])
                A("dve", "tensor_copy", [("ps", b)], [("kst", 0), ("kst", 1)], out=kst[0:15, half, :], in_=ps[0:15, b, :])
            DMA("sp", pp_o[l, :, :], kst[0:15, :, :].rearrange("p a n -> p (a n)"), reads=[("kst", 0), ("kst", 1)], is_out=True)
        dbg("yc_%d_%d" % (pas, l), Yb[:, 16:24, :], YK[16:24])
        retag(CSET, B2K)
        if STOP == "pool":
            return

        if pas == 0 and l == NL - 1:
            grp = [(0, MAIN)]
            NT = MAIN
        for d in range(16):
            sg = wload(l, ("mg", d), 6144)
            sbr = wload(l, ("mb", d), 3072)
            gv = Wr[:, sg, 0:6144].rearrange("p (n k c) -> p n k c", n=3, k=16)
            bv = Wr[:, sbr, 0:3072].rearrange("p (n k c) -> p n k c", n=3, k=8)
            for n_ in range(3):
                rg = mm_chunk(lambda k: gv[:, n_, k, :], KC, lambda k, c0, n: B1[:, k, c0:c0 + n], B1K, [("W", sg)], grp)
                rb = mm_chunk(lambda k: bv[:, n_, k, :], 8, lambda k, c0, n: Yb[:, n_ * 8 + k, c0:c0 + n],
                              YK[n_ * 8:n_ * 8 + 8], [("W", sbr)], grp)
                for gi in range(len(grp)):
                    (bg, c0, n) = rg[gi]
                    (bb, _, _) = rb[gi]
                    tk = ("T", 1, c0)
                    ak = ("T", 2, c0)
                    ACT(Tt[:, 1, c0:c0 + n], ps[:, bg, 0:n], AF.Sigmoid, [("ps", bg)], [tk])
                    if n_ == 0:
                        TT(Tt[:, 2, c0:c0 + n], ps[:, bb, 0:n], Tt[:, 1, c0:c0 + n], ALU.mult, [("ps", bb), tk], [ak])
                    else:
                        TT(Tt[:, 1, c0:c0 + n], ps[:, bb, 0:n], Tt[:, 1, c0:c0 + n], ALU.mult, [("ps", bb), tk], [tk])
                        if n_ == 1:
                            TT(Tt[:, 2, c0:c0 + n], Tt[:, 2, c0:c0 + n], Tt[:, 1, c0:c0 + n], ALU.add, [tk, ak], [ak])
                        else:
                            TT(B2[:, d, c0:c0 + n], Tt[:, 2, c0:c0 + n], Tt[:, 1, c0:c0 + n], ALU.add, [tk, ak], [("B2", d)])
        dbg("mg_%d_%d" % (pas, l), B2[:], B2K)
        if STOP == "merge":
            return

        def proj_residual(key, src, srckeys):
            for blk in range(4):
                slot = wload(l, (key, blk), 8192)
                wv = wview(slot)
                for j in range(4):
                    e_ = blk * 4 + j
                    res = mm_chunk(lambda k: wv[:, k, j * 128:(j + 1) * 128], KC,
                                   lambda k, c0, n: src[:, k, c0:c0 + n], srckeys, [("W", slot)], grp)
                    for (b, c0, n) in res:
                        TT(xT[:, e_, c0:c0 + n], ps[:, b, 0:n], xT[:, e_, c0:c0 + n], ALU.add, [("ps", b), ("x", e_)], [("x", e_)])

        proj_residual("wo", B2, B2K)
        dbg("x1_%d_%d" % (pas, l), xT[:], XK)
        if STOP == "wo":
            return

        rmsnorm_to(B1, B1K, ("g_xattn", l), NT, grp)
        for blk in range(4):
            slot = wload(l, ("xq", blk), 8192)
            wv = wview(slot)
            for j in range(4):
                e_ = blk * 4 + j
                res = mm_chunk(lambda k: wv[:, k, j * 128:(j + 1) * 128], KC,
                               lambda k, c0, n: B1[:, k, c0:c0 + n], B1K, [("W", slot)], grp)
                for (b, c0, n) in res:
                    ACT(B2[:, e_, c0:c0 + n], ps[:, b, 0:n], AF.Copy, [("ps", b)], [("B2", e_)])
        if STOP == "attn_q":
            return
        retag(YK + B1K, ASET)
        for tkc in range(2):
            DMA("sp", MEMSTG, memin[tkc * 128:(tkc + 1) * 128, :], writes=MEMK)
            for q in range(4):
                ACT(Tt[:, 1, 0:512], MEMSTG[:, q * 512:(q + 1) * 512], AF.Square, MEMK, [("T", 1, 0), "stt"], accum_out=stt[:, 8 + q:9 + q])
            A("dve", "reduce_sum", ["stt"], ["stt"], out=stt[:, 12:13], in_=stt[:, 8:12], axis=AX.X)
            rsqrt_(stt[:, 13:14], stt[:, 12:13], 1.0 / D, 0, ["stt"], ["stt"])
            TS(MEMSTG, MEMSTG, stt[:, 13:14], None, ALU.mult, None, ["stt"] + MEMK, MEMK)
            for q in range(4):
                (b,) = psbank(1)
                AM("pe", [TP(ps[:, b, j * 128:(j + 1) * 128], MEMSTG[:, (q * 4 + j) * 128:(q * 4 + j + 1) * 128], ident[:, :]) for j in range(4)],
                   reads=MEMK + ["ident"], writes=[("ps", b)])
                for j in range(4):
                    k = q * 4 + j
                    ACT(MTB[:, k, tkc * 128:(tkc + 1) * 128], ps[:, b, j * 128:(j + 1) * 128], AF.Copy, [("ps", b), "pv"], ["MTB"],
                        scale=pvc(("g_mem", l), k))
        if STOP == "attn_mem":
            retag(ASET, YK + B1K)
            return
        for which, key, out_d in ((0, "xk", mk_o), (1, "xv", mv_o)):
            for blk in range(4):
                slot = wload(l, (key, blk), 8192)
                wv = wview(slot)
                for tkc in range(2):
                    (b,) = psbank(1)
                    AM("pe", [MM(ps[:, b, :], MTB[:, k, tkc * 128:(tkc + 1) * 128], wv[:, k, :], k == 0, k == KC - 1) for k in range(KC)],
                       reads=["MTB", ("W", slot)], writes=[("ps", b)])
                    if pas == 0 and "kvout" not in SKIP:
                        if True:
                            A("dve", "tensor_copy", [("ps", b)], [("kst", tkc)], out=kst[:, tkc, :], in_=ps[:, b, :])
                        else:
                            ACT(kst[:, tkc, :], ps[:, b, :], AF.Copy, [("ps", b)], [("kst", tkc)])
                        DMA("sp", out_d[l, tkc * 128:(tkc + 1) * 128, blk * 512:(blk + 1) * 512], kst[:, tkc, :], reads=[("kst", tkc)], is_out=True)
                    if which == 1:
                        A("dve", "tensor_copy", [("ps", b)], ["VP"], out=VP[:, tkc, blk * 512:(blk + 1) * 512], in_=ps[:, b, :])
                if which == 0 and "kt" not in SKIP:
                    for j in range(4):
                        (b,) = psbank(1)
                        AM("pe", [MM(ps[:, b, 0:256], wv[:, k, j * 128:(j + 1) * 128], MTB[:, k, :], k == 0, k == KC - 1) for k in range(KC)],
                           reads=["MTB", ("W", slot)], writes=[("ps", b)])
                        ACT(KT[:, blk * 4 + j, :], ps[:, b, 0:256], AF.Copy, [("ps", b)], ["KT"])
        if STOP == "attn_kv":
            retag(ASET, YK + B1K)
            return
        for blk in range(3):
            wprefetch(l, ("xo", blk), 8192)
        pgrp = [(0, MAIN)] + ([(MAIN, 128)] if (pas == 0 and len(grp) > 1) else [])
        for hd in range(4):
            qk = [("B2", hd * 4 + dc) for dc in range(4)]
            for (c0, n) in pgrp:
                for kcx in range(2):
                    (b,) = psbank(1)
                    AM("pe", [MM(ps[:, b, 0:n], KT[:, hd * 4 + dc, kcx * 128:(kcx + 1) * 128], B2[:, hd * 4 + dc, c0:c0 + n], dc == 0, dc == 3) for dc in range(4)],
                       reads=["KT"] + qk, writes=[("ps", b)])
                    ACT(PTb[:, kcx, c0:c0 + n], ps[:, b, 0:n], AF.Exp, [("ps", b)], [("PT", kcx)], scale=ATT_SCALE)
                (b,) = psbank(1)
                AM("pe", [MM(ps[:, b, 0:n], onesb[:], PTb[:, 0, c0:c0 + n], True, False), MM(ps[:, b, 0:n], onesb[:], PTb[:, 1, c0:c0 + n], False, True)],
                   reads=[("PT", 0), ("PT", 1), "onesb"], writes=[("ps", b)])
                A("dve", "reciprocal", [("ps", b)], [("T", 1, c0)], out=Tt[:, 1, c0:c0 + n], in_=ps[:, b, 0:n])
                for dc in range(4):
                    (b,) = psbank(1)
                    ch = hd * 4 + dc
                    AM("pe", [MM(ps[:, b, 0:n], VP[:, 0, ch * 128:(ch + 1) * 128], PTb[:, 0, c0:c0 + n], True, False),
                              MM(ps[:, b, 0:n], VP[:, 1, ch * 128:(ch + 1) * 128], PTb[:, 1, c0:c0 + n], False, True)],
                       reads=["VP", ("PT", 0), ("PT", 1)], writes=[("ps", b)])
                    TT(B2[:, ch, c0:c0 + n], ps[:, b, 0:n], Tt[:, 1, c0:c0 + n], ALU.mult, [("ps", b), ("T", 1, c0)], [("B2", ch)])
        if pas == 1:
            KR = [KRAW0, KRAW1]; KRK = ["KR0", "KR1"]
            VSB = [VS0, VS1]; VSK = ["VS0", "MTB"]
            for s in range(16):
                vb = s % 2
                pt4 = pts[:, vb, :].rearrange("p (k h t) -> p k h t", k=2, h=4)
                DMA("pool", VSB[vb], vcin[l, s, :, :].rearrange("(k p) n -> p k n", p=128), writes=[VSK[vb]])
                for kcx in range(2):
                    DMA("pool", KR[kcx], kcin[l, s, kcx * 128:(kcx + 1) * 128, :], writes=[KRK[kcx]])
                    for q in range(2):
                        AM("pe", [TP(psb[:, q, j * 128:(j + 1) * 128], KR[kcx][:, (q * 8 + j) * 128:(q * 8 + j + 1) * 128], identb[:, :]) for j in range(8)],
                           reads=[KRK[kcx], "identb"], writes=[("psb", q)])
                        o_ = KTS[:, q * 8:(q + 1) * 8, kcx * 128:(kcx + 1) * 128]
                        i_ = psb[:, q, :].rearrange("p (j n) -> p j n", n=128)
                        if q == 0:
                            ACT(o_, i_, AF.Copy, [("psb", q)], [("KTS", kcx)])
                        else:
                            A("dve", "tensor_copy", [("psb", q)], [("KTS", kcx)], out=o_, in_=i_)
                (b,) = psbank(1)
                cs0 = MAIN + s * 4
                calls = []
                for kcx in range(2):
                    for hd in range(4):
                        for dc in range(4):
                            calls.append(MM(ps[:, b, (kcx * 4 + hd) * 4:(kcx * 4 + hd) * 4 + 4], KTS[:, hd * 4 + dc, kcx * 128:(kcx + 1) * 128],
                                            B2[:, hd * 4 + dc, cs0:cs0 + 4], dc == 0, dc == 3))
                AM("pe", calls, reads=[("KTS", 0), ("KTS", 1)] + B2K, writes=[("ps", b)])
                ACT(pts[:, vb, :], ps[:, b, 0:32], AF.Exp, [("ps", b)], [("pts", vb)], scale=ATT_SCALE)
                (b2_,) = psbank(1)
                calls = []
                for hd in range(4):
                    for r in range(4):
                        for kcx in range(2):
                            calls.append(MM(ps[:, b2_, 64 + hd * 16 + r * 4:64 + hd * 16 + r * 4 + 4], onesb[:], pt4[:, kcx, hd, :], kcx == 0, kcx == 1))
                for dc16 in range(16):
                    hd = dc16 // 4
                    for kcx in range(2):
                        calls.append(MM(ps[:, b2_, dc16 * 4:dc16 * 4 + 4], VSB[vb][:, kcx, dc16 * 128:(dc16 + 1) * 128], pt4[:, kcx, hd, :], kcx == 0, kcx == 1))
                AM("pe", calls, reads=[("pts", vb), VSK[vb], "onesb"], writes=[("ps", b2_)])
                A("dve", "reciprocal", [("ps", b2_)], [("rss", vb)], out=rss[:, vb, :], in_=ps[:, b2_, 64:128])
                TT(ots[:, :, s * 4:(s + 1) * 4], ps[:, b2_, 0:64].rearrange("p (d t) -> p d t", t=4), rss[:, vb, :].rearrange("p (d t) -> p d t", t=4), ALU.mult,
                   [("ps", b2_), ("rss", vb)], ["ots"])
            A("dve", "tensor_copy", ["ots"] + B2K, B2K, out=B2[:, :, MAIN:MAIN + 64], in_=ots[:])
        retag(ASET, YK + B1K)
        dbg("ot_%d_%d" % (pas, l), B2[:], B2K)
        proj_residual("xo", B2, B2K)
        dbg("x2_%d_%d" % (pas, l), xT[:], XK)
        if STOP == "attn":
            return

        rmsnorm_to(B1, B1K, ("g_mlp", l), NT, grp)
        for fb in range(4):
            for blk in range(4):
                slot = wload(l, ("up", fb, blk), 8192)
                wv = wview(slot)
                for j in range(4):
                    f_ = blk * 4 + j
                    res = mm_chunk(lambda k: wv[:, k, j * 128:(j + 1) * 128], KC,
                                   lambda k, c0, n: B1[:, k, c0:c0 + n], B1K, [("W", slot)], grp)
                    for gi, (b, c0, n) in enumerate(res):
                        tk = ("T", 1 + gi, 0)
                        ACT(Tt[:, 1 + gi, 0:n], ps[:, b, 0:n], AF.Relu, [("ps", b)], [tk])
                        TT(B2[:, f_, c0:c0 + n], Tt[:, 1 + gi, 0:n], Tt[:, 1 + gi, 0:n], ALU.mult, [tk], [("B2", f_)])
            for blk in range(4):
                slot = wload(l, ("dn", fb, blk), 8192)
                wv = wview(slot)
                for j in range(4):
                    e_ = blk * 4 + j
                    res = mm_chunk(lambda k: wv[:, k, j * 128:(j + 1) * 128], KC,
                                   lambda k, c0, n: B2[:, k, c0:c0 + n], B2K, [("W", slot)], grp)
                    for (b, c0, n) in res:
                        TT(xT[:, e_, c0:c0 + n], ps[:, b, 0:n], xT[:, e_, c0:c0 + n], ALU.add, [("ps", b), ("x", e_)], [("x", e_)])
        dbg("x3_%d_%d" % (pas, l), xT[:], XK)

    def do_pass(pas):
        ns = 128 if pas == 0 else 64
        NT = MAIN + ns
        grp = [(0, MAIN), (MAIN, ns)]
        if pas == 0:
            tiles = [(i * 128, 128, 128 + i * 128) for i in range(4)] + [(MAIN, 128, 0)]
        else:
            tiles = [(i * 128, 128, 128 + 512 + i * 128) for i in range(4)] + [(MAIN, 64, 1152)]
        retag(B2K, ["STG"])
        for (c0, nt, r0) in tiles:
            DMA("sp", STG[0:nt, :], xin[r0:r0 + nt, :], writes=["STG"])
            for q in range(4):
                (b,) = psbank(1)
                AM("pe", [TP(ps[:, b, j * 128:j * 128 + nt], STG[0:nt, (q * 4 + j) * 128:(q * 4 + j + 1) * 128], ident[0:nt, 0:nt]) for j in range(4)],
                   reads=["STG", "ident"], writes=[("ps", b)])
                ACT(xT[:, q * 4:q * 4 + 4, c0:c0 + nt], ps[:, b, :].rearrange("p (j n) -> p j n", n=128)[:, :, 0:nt], AF.Copy,
                    [("ps", b)], [("x", q * 4 + j) for j in range(4)])
        retag(["STG"], B2K)
        for l in range(NL):
            if l < NLAYERS_RUN:
                do_layer(pas, l, NT, ns, grp, tiles)
        rms_stats(B1, B1K, NT, grp)
        retag(B2K, ["STG"])
        out_tiles = [(i * 128, 128, (0 if pas == 0 else 512) + i * 128) for i in range(4)]
        if pas == 1:
            out_tiles.append((MAIN, 64, 1024))
        for (c0, nt, orow) in out_tiles:
            for k in range(KC):
                STT(xT[:, k, c0:c0 + nt], xT[:, k, c0:c0 + nt], pvc("g_final", k), Tt[:, 0, c0:c0 + nt], ALU.mult, ALU.mult,
                    [("x", k), "pv", ("T", 0, 0), ("T", 0, MAIN)], [("x", k)])
            for q in range(4):
                (b,) = psbank(1)
                AM("pe", [TP(ps[0:nt, b, j * 128:(j + 1) * 128], xT[:, q * 4 + j, c0:c0 + nt], ident[:, :]) for j in range(4)],
                   reads=XK[q * 4:q * 4 + 4] + ["ident"], writes=[("ps", b)])
                ACT(STG[0:nt, q * 512:(q + 1) * 512], ps[0:nt, b, :], AF.Copy, [("ps", b)], ["STG"])
            DMA("sp", y_o[orow:orow + nt, :], STG[0:nt, :], reads=["STG"], is_out=True)
        retag(["STG"], B2K)

    for pas in PASSES_RUN:
        do_pass(pas)

    S.emit(nc, es)
    es.close()
    return nc


NLAYERS_RUN = NL
PASSES_RUN = (0, 1)
STOP = None
WST_SHAPE = None
SKIP = ()


def make_in_maps(inp):
    f = lambda a: np.ascontiguousarray(np.asarray(a, dtype=np.float32))
    I = {k: np.asarray(v) for k, v in inp.items()}
    wstream = np.stack([
        pack_weights(I["w_in"][l], I["w_branch"][l], I["w_out"][l], I["w_xq"][l], I["w_xk"][l], I["w_xv"][l],
                     I["w_xo"][l], I["w_up"][l], I["w_down"][l]) for l in range(NL)])
    lnv = f(np.stack([np.stack([I["ln_v_g"][l], I["ln_v_b"][l]]) for l in range(NL)]))
    wstT = f(np.transpose(I["w_s"], (0, 3, 1, 2)))
    wss = f(np.tile(np.transpose(I["w_s"][:, :, :4, :4], (0, 3, 1, 2)), (1, 16, 1, 16)))
    mask = np.zeros((128, 192), np.float32)
    mask[:, :128] = np.triu(np.ones((128, 128), np.float32))
    blk = np.kron(np.eye(16, dtype=np.float32), np.triu(np.ones((4, 4), np.float32)))
    mask[:64, 128:192] = blk
    bs = np.zeros((NL, 4, 192), np.float32)
    bs[:, :, :128] = I["b_s"]
    bs[:, :, 128:192] = np.tile(I["b_s"][:, :, :4], (1, 1, 16))
    bs = f(bs.reshape(1, -1))
    pw = f(np.stack([np.transpose(I["pool_w"][l].reshape(4, 2, 128, 256), (2, 0, 1, 3)).reshape(128, 2048) for l in range(NL)]))
    ident = np.eye(128, dtype=np.float32)

    def pvec(v):
        return np.asarray(v, np.float32).reshape(-1, 128).T

    maps = []
    for c in range(8):
        b, half = c // 2, c % 2
        pvh = np.zeros((128, NPV), np.float32)
        for nm in ("g_mix", "g_xattn", "g_mem", "g_mlp"):
            for l in range(NL):
                pvh[:, PV[(nm, l)]:PV[(nm, l)] + 16] = pvec(I[nm][l])
        pvh[:, PV["g_final"]:PV["g_final"] + 16] = pvec(I["g_final"])
        for l in range(NL):
            cw = np.asarray(I["conv_w"][l], np.float32)
            pvh[:, PV[("conv_w", l)]:PV[("conv_w", l)] + 248] = cw.T.reshape(8, 128, 31).transpose(1, 0, 2).reshape(128, 248)
            for nm in ("conv_b", "ln_c_g", "ln_c_b", "pool_scale"):
                pvh[:, PV[(nm, l)]:PV[(nm, l)] + 8] = pvec(I[nm][l])
        pvh[:, PV["flag"]] = float(half)
        ic = np.zeros((4, 16), np.float32)
        for gi, w in enumerate((2, 4, 8, 16)):
            for t in range(16):
                ic[gi, t] = 1.0 / (min(w, t + 1) if half == 0 else w)
        pvh[:, PV["icnt"]:PV["icnt"] + 64] = ic.reshape(1, 64)
        xp = I["x_prompt"][b]
        own = xp[half * 1024:(half + 1) * 1024]
        halo = xp[896:1024] if half == 1 else np.zeros((128, D), np.float32)
        xs = I["x_sample"][c * 16:(c + 1) * 16].reshape(64, D)
        xin = f(np.concatenate([halo, own, xs], axis=0))
        maps.append({
            "wst": wstream, "xin": xin, "memin": f(I["mem_prompt"][b]),
            "kcin": f(I["cache_mem_k"][:, c * 16:(c + 1) * 16].reshape(NL, 16, 256, D)),
            "vcin": f(I["cache_mem_v"][:, c * 16:(c + 1) * 16].reshape(NL, 16, 256, D)),
            "scin": f(I["state_conv"][:, c * 16:(c + 1) * 16]), "spin": f(I["state_pool"][:, c * 16:(c + 1) * 16]),
            "pvin": pvh, "lnvin": lnv, "wstin": wstT, "wssin": wss, "maskin": mask, "bsin": bs, "pwin": pw, "identin": ident,
        })
    return maps


def assemble(results):
    y_prompt = np.zeros((4, 2048, D), np.float32)
    y_sample = np.zeros((128, 4, D), np.float32)
    mk = np.zeros((NL, 4, 256, 4, 512), np.float32)
    mv = np.zeros((NL, 4, 256, 4, 512), np.float32)
    cp = np.zeros((NL, 4, 30, 1024), np.float32)
    pp = np.zeros((NL, 4, 15, 1024), np.float32)
    cs = np.zeros((NL, 128, 30, 1024), np.float32)
    pps = np.zeros((NL, 128, 15, 1024), np.float32)
    cv = np.zeros((NL, 128, 4, 1024), np.float32)
    for c, r in enumerate(results):
        b, half = c // 2, c % 2
        y_prompt[b, half * 1024:(half + 1) * 1024] = r["y_o"][:1024]
        y_sample[c * 16:(c + 1) * 16] = r["y_o"][1024:1088].reshape(16, 4, D)
        if half == 0:
            mk[:, b] = r["mk_o"].reshape(NL, 256, 4, 512)
            mv[:, b] = r["mv_o"].reshape(NL, 256, 4, 512)
        else:
            cp[:, b] = r["cp_o"]
            pp[:, b] = r["pp_o"]
        cs[:, c * 16:(c + 1) * 16] = r["cs_o"]
        pps[:, c * 16:(c + 1) * 16] = r["ps_o"]
        cv[:, c * 16:(c + 1) * 16] = r["cv_o"].reshape(NL, 16, 4, 1024)
    return (y_prompt, y_sample, mk, mv, cp, pp, cs, pps, cv)


def kernel(**inputs):
    maps = make_in_maps(inputs)
    nc = build()
    res = run_bass_kernel_spmd(nc, maps, core_ids=list(range(8)))
    return assemble(res.results)
```

```python
import numpy as np
from contextlib import ExitStack
import concourse.bass as bass
import concourse.mybir as mybir
from concourse.bass_utils import run_bass_kernel_spmd

F32 = mybir.dt.float32
BF16 = mybir.dt.bfloat16
AF = mybir.ActivationFunctionType
ALU = mybir.AluOpType
AX = mybir.AxisListType

D = 2048
KC = 16
NL = 2
NTM = 640
MAIN = 512
RMS_EPS = 1e-6
LN_EPS = 1e-5
SLOT = 8192
NSLOT = 3
WTOT = 655360
GELU_C = 0.044715
GELU_S = 1.5957691216057308
ATT_SCALE = 512 ** -0.5

WIN1_CHUNKS = (
    [("u", c) for c in range(8)] + [("v", c) for c in range(8)]
    + [x for c in range(8) for x in (("a", c), ("g", c))]
    + [("p", c) for c in range(8)]
)


def win_col(kind, c):
    base = {"u": 0, "v": 1024, "a": 2048, "g": 3072, "p": 4096}[kind]
    return base + c * 128


def stream_offsets():
    off = {}
    o = 0
    for b in range(10):
        off[("win1", b)] = o; o += 8192
    for d in range(16):
        off[("mg", d)] = o; o += 6144
        off[("mb", d)] = o; o += 3072
    for nm in ("wo", "xq", "xk", "xv", "xo"):
        for b in range(4):
            off[(nm, b)] = o; o += 8192
    for fb in range(4):
        for b in range(4):
            off[("up", fb, b)] = o; o += 8192
        for b in range(4):
            off[("dn", fb, b)] = o; o += 8192
    assert o == WTOT, o
    return off


WOFF = stream_offsets()


def pack_weights(w_in, w_branch, w_out, w_xq, w_xk, w_xv, w_xo, w_up, w_down):
    out = np.empty((128, WTOT), np.float32)

    def put(off, W, c0, ncols, width=None, col_off=0):
        K = W.shape[0]
        kc = K // 128
        width = width or ncols
        dst = out[:, off:off + kc * width].reshape(128, kc, width)[:, :, col_off:col_off + ncols]
        dst[...] = W[:, c0:c0 + ncols].reshape(kc, 128, ncols).transpose(1, 0, 2)

    for b in range(10):
        for j, (k, c) in enumerate(WIN1_CHUNKS[4 * b:4 * b + 4]):
            put(WOFF[("win1", b)], w_in, win_col(k, c), 128, width=512, col_off=j * 128)
    for d in range(16):
        for n in range(3):
            put(WOFF[("mg", d)] + n * 2048, w_in, 5120 + n * 2048 + d * 128, 128)
            put(WOFF[("mb", d)] + n * 1024, w_branch[n], d * 128, 128)
    for nm, W in (("wo", w_out), ("xq", w_xq), ("xk", w_xk), ("xv", w_xv), ("xo", w_xo)):
        for b in range(4):
            put(WOFF[(nm, b)], W, b * 512, 512)
    for fb in range(4):
        for b in range(4):
            put(WOFF[("up", fb, b)], w_up, fb * 2048 + b * 512, 512)
            put(WOFF[("dn", fb, b)], w_down[fb * 2048:(fb + 1) * 2048], b * 512, 512)
    return out


PV = {}
_o = 0
for _nm in ("g_mix", "g_xattn", "g_mem", "g_mlp"):
    for _l in range(NL):
        PV[(_nm, _l)] = _o; _o += 16
PV["g_final"] = _o; _o += 16
for _l in range(NL):
    PV[("conv_w", _l)] = _o; _o += 8 * 31
for _nm in ("conv_b", "ln_c_g", "ln_c_b", "pool_scale"):
    for _l in range(NL):
        PV[(_nm, _l)] = _o; _o += 8
PV["flag"] = _o; _o += 1
PV["icnt"] = _o; _o += 64
NPV = _o


class Op:
    __slots__ = ("eng", "fn", "dma", "deps", "signal", "pos", "sigval", "dsem", "dval", "name")


class Sched:
    ENGS = ("pe", "act", "dve", "pool", "sp")
    SELF_SYNC = {"pe": False, "act": True, "dve": True, "pool": True, "sp": False}
    NDSEM = 8

    def __init__(self):
        self.streams = {e: [] for e in self.ENGS}
        self.lastw = {}
        self.readers = {}
        self.dma_slot_prev = {}
        self.dma_count = {e: 0 for e in self.ENGS}
        self.out_dmas = []

    def add(self, eng, fn, reads=(), writes=(), dma=False, name=""):
        op = Op()
        op.eng, op.fn, op.dma, op.signal, op.name = eng, fn, dma, False, name
        op.pos = len(self.streams[eng])
        deps = set()
        for r in reads:
            w = self.lastw.get(r)
            if w is not None:
                deps.add(w)
        for w_ in writes:
            w = self.lastw.get(w_)
            if w is not None:
                deps.add(w)
            for rd in self.readers.get(w_, ()):
                deps.add(rd)
        if dma:
            k = self.dma_count[eng]
            self.dma_count[eng] += 1
            slot = (eng, k % self.NDSEM)
            op.dsem = slot
            op.dval = 16 * (k // self.NDSEM + 1)
            prev = self.dma_slot_prev.get(slot)
            if prev is not None:
                deps.add(prev)
            self.dma_slot_prev[slot] = op
        deps.discard(op)
        op.deps = deps
        for d in deps:
            if not d.dma:
                d.signal = True
        for r in reads:
            self.readers.setdefault(r, []).append(op)
        for w_ in writes:
            self.lastw[w_] = op
            self.readers[w_] = []
        self.streams[eng].append(op)
        return op

    def emit(self, nc, es):
        for e in self.ENGS:
            c = 0
            for op in self.streams[e]:
                if op.signal:
                    c += 1
                    op.sigval = c
        sems = {e: es.enter_context(nc.semaphore("s_" + e)) for e in self.ENGS}
        dsems = {}
        for e in self.ENGS:
            if self.dma_count[e]:
                for i in range(self.NDSEM):
                    dsems[(e, i)] = es.enter_context(nc.semaphore("d_%s%d" % (e, i)))
        block = es.enter_context(nc.Block())
        streams = self.streams
        SELF = self.SELF_SYNC

        def run(ename, eng):
            seen = {e: 0 for e in self.ENGS}
            seen_d = {}
            for op in streams[ename]:
                need = {}
                for d in op.deps:
                    if d.dma:
                        if seen_d.get(d.dsem, 0) < d.dval:
                            seen_d[d.dsem] = d.dval
                            eng.wait_ge(dsems[d.dsem], d.dval)
                    else:
                        if d.eng == ename and not SELF[ename]:
                            continue
                        if d.sigval > need.get(d.eng, 0):
                            need[d.eng] = d.sigval
                for e2, v in need.items():
                    if v > seen[e2]:
                        seen[e2] = v
                        eng.wait_ge(sems[e2], v)
                ins = op.fn(eng)
                if op.dma:
                    ins.then_inc(dsems[op.dsem], 16)
                elif op.signal:
                    ins.then_inc(sems[ename], 1)
            if ename == "sp":
                for d in self.out_dmas:
                    eng.wait_ge(dsems[d.dsem], d.dval)

        @block.tensor
        def _(e):
            run("pe", e)

        @block.scalar
        def _(e):
            run("act", e)

        @block.vector
        def _(e):
            run("dve", e)

        @block.gpsimd
        def _(e):
            run("pool", e)

        @block.sync
        def _(e):
            run("sp", e)


def build(debug=()):
    nc = bass.Bass("TRN2", target_bir_lowering=False)
    dram_in = lambda n, s: nc.dram_tensor(n, list(s), F32, kind="ExternalInput").ap()
    dram_out = lambda n, s, dt=F32: nc.dram_tensor(n, list(s), dt, kind="ExternalOutput").ap()
    wst = dram_in("wst", WST_SHAPE if WST_SHAPE else ((NL, 128, WTOT) if NLAYERS_RUN else (1, 128, 8)))
    xin = dram_in("xin", (1216, D))
    memin = dram_in("memin", (256, D))
    kcin = dram_in("kcin", (NL, 16, 256, D))
    vcin = dram_in("vcin", (NL, 16, 256, D))
    scin = dram_in("scin", (NL, 16, 30, 1024))
    spin = dram_in("spin", (NL, 16, 15, 1024))
    pvin = dram_in("pvin", (128, NPV))
    lnvin = dram_in("lnvin", (NL, 2, 1024))
    wstin = dram_in("wstin", (NL, 128, 4, 128))
    wssin = dram_in("wssin", (NL, 64, 4, 64))
    maskin = dram_in("maskin", (128, 192))
    bsin = dram_in("bsin", (1, NL * 4 * 192))
    pwin = dram_in("pwin", (NL, 128, 2048))
    identin = dram_in("identin", (128, 128))

    y_o = dram_out("y_o", (1088, D))
    mk_o = dram_out("mk_o", (NL, 256, D))
    mv_o = dram_out("mv_o", (NL, 256, D))
    cp_o = dram_out("cp_o", (NL, 30, 1024))
    pp_o = dram_out("pp_o", (NL, 15, 1024))
    cs_o = dram_out("cs_o", (NL, 16, 30, 1024))
    ps_o = dram_out("ps_o", (NL, 16, 15, 1024))
    cv_o = dram_out("cv_o", (NL, 64, 1024))
    dbg_o = {}
    for (nm, shp, dts) in debug:
        dbg_o[nm] = dram_out("dbg_" + nm, shp, BF16 if dts == "bf16" else F32)

    S = Sched()
    es = ExitStack()
    sb = lambda n, s, dt: es.enter_context(nc.sbuf_tensor(n, list(s), dt))
    xT = sb("xT", (128, KC, NTM), F32)
    B1 = sb("B1", (128, KC, NTM), BF16)
    B2 = sb("B2", (128, KC, NTM), BF16)
    Yb = sb("Yb", (128, 24, NTM), BF16)
    Wr = sb("Wr", (128, NSLOT, SLOT), BF16)
    Tt = sb("Tt", (128, 4, NTM), F32)
    PTb = sb("PTb", (128, 2, NTM), BF16)
    pv = sb("pv", (128, NPV), F32)
    ident = sb("ident", (128, 128), F32)
    identb = sb("identb", (128, 128), BF16)
    onesb = sb("onesb", (128, 128), BF16)
    ones1 = sb("ones1", (1, 128), F32)
    bsrow = sb("bsrow", (1, NL * 4 * 192), F32)
    maskt = sb("maskt", (128, 192), F32)
    wsm = sb("wsm", (128, 4, 192), BF16)
    wtmp = sb("wtmp", (128, 4, 192), F32)
    pwb = sb("pwb", (128, 2048), BF16)
    hconv = sb("hconv", (128, NL, 8, 30), F32)
    hpool = sb("hpool", (128, NL, 8, 15), F32)
    kst = sb("kst", (128, 2, 512), F32)
    ostg = sb("ostg", (128, 1024), F32)
    stt = sb("stt", (128, 16), F32)
    t3b = sb("t3b", (128, 64), F32)
    rss = sb("rss", (128, 2, 64), F32)
    pts = sb("pts", (128, 2, 32), BF16)
    ots = sb("ots", (128, 16, 64), BF16)
    epst = sb("epst", (128, 2), F32)
    ps = es.enter_context(nc.psum_tensor("ps", [128, 6, 512], F32))
    psb = es.enter_context(nc.psum_tensor("psb", [128, 2, 1024], BF16))

    B2f = B2[:].rearrange("p k n -> p (k n)")
    B2w = B2f.bitcast(F32)
    STG = B2w[:, 0:2048]
    VT0 = B2w[:, 0:1024]; VT1 = B2w[:, 1024:2048]; LNVG = B2w[:, 2048:3072]; LNVB = B2w[:, 3072:4096]
    VB = B2w[:, 4096:4608].bitcast(BF16)
    VSET = ["VT0", "VT1", "LNV", "VB"]
    GLm = B2w[:, 0:542]; GLs = B2w[:, 544:1088]; Pm = B2w[:, 1088:1615]; Psb_ = B2w[:, 1616:1920]
    PMc = B2w[:, 1920:2560].bitcast(BF16).rearrange("p (k n) -> p k n", n=NTM)
    ACC = B2w[:, 2560:3200]
    DG = B2w[:, 2560:4544].bitcast(BF16).rearrange("p (k n) -> p k n", n=128)
    ST1 = B2w[:, 4544:5056]
    GLmb = PTb[:, 0, 0:542]
    GLsb = PTb[:, 1, 0:544]
    GLsb3 = GLsb.rearrange("p (s n) -> p s n", n=34)
    CSET = ["GLm", "GLs", "Pm", "Ps", ("PMc", 0), ("PMc", 1), "ACC", "ST1", "DG"]
    GLs3 = GLs.rearrange("p (s n) -> p s n", n=34)
    Ps3 = Psb_.rearrange("p (s n) -> p s n", n=19)
    B1f = B1[:].rearrange("p k n -> p (k n)")
    Yf = Yb[:].rearrange("p k n -> p (k n)")
    MTB = Yf[:, 0:4096].rearrange("p (k n) -> p k n", n=256)
    KT = Yf[:, 4096:8192].rearrange("p (k n) -> p k n", n=256)
    VP = Yf[:, 8192:12288].rearrange("p (k n) -> p k n", n=2048)
    MEMSTG = B1f[:, 0:4096].bitcast(F32)
    KRAW0 = B1f[:, 0:2048]
    VS0 = B1f[:, 2048:6144].rearrange("p (k n) -> p k n", n=2048)
    KTS = B1f[:, 6144:10240].rearrange("p (k n) -> p k n", n=256)
    VS1 = Yf[:, 0:4096].rearrange("p (k n) -> p k n", n=2048)
    KRAW1 = Yf[:, 12288:14336]
    MEMK = ["KR0", "VS0"]
    ASET = ["MTB", "KT", "VP", "KR0", "VS0", "KR1", ("KTS", 0), ("KTS", 1)]

    XK = [("x", k) for k in range(KC)]
    B1K = [("B1", k) for k in range(KC)]
    B2K = [("B2", k) for k in range(KC)]
    YK = [("Y", k) for k in range(24)]

    def A(eng, meth, reads=(), writes=(), **kw):
        return S.add(eng, lambda e: getattr(e, meth)(**kw), reads=reads, writes=writes, name=meth)

    def AM(eng, calls, reads=(), writes=()):
        def fn(e):
            ins = None
            for (meth, kw) in calls:
                ins = getattr(e, meth)(**kw)
            return ins
        return S.add(eng, fn, reads=reads, writes=writes, name="multi")

    def CHAIN(eng, calls, reads=(), writes=()):
        grp_ = []
        for (meth, kw, short) in calls:
            if short:
                if grp_:
                    AM(eng, grp_, reads, writes)
                    grp_ = []
                AM(eng, [(meth, kw)], reads, writes)
            else:
                grp_.append((meth, kw))
        if grp_:
            AM(eng, grp_, reads, writes)

    def DMA(eng, out, in_, reads=(), writes=(), is_out=False):
        op = S.add(eng, lambda e: e.dma_start(out=out, in_=in_), reads=reads, writes=writes, dma=True, name="dma")
        if is_out:
            S.out_dmas.append(op)
        return op

    def MM(out, lhsT, rhs, start, stop):
        return ("matmul", dict(out=out, lhsT=lhsT, rhs=rhs, start=start, stop=stop))

    def TP(out, in_, idn):
        return ("transpose", dict(out=out, in_=in_, identity=idn))

    def ACT(out, in_, func, reads, writes, **kw):
        return A("act", "activation", reads, writes, out=out, in_=in_, func=func, **kw)

    def TT(out, in0, in1, op, reads, writes, eng="dve"):
        return A(eng, "tensor_tensor", reads, writes, out=out, in0=in0, in1=in1, op=op)

    def TS(out, in0, s1, s2, op0, op1, reads, writes, eng="dve"):
        if s2 is None:
            return A(eng, "tensor_scalar", reads, writes, out=out, in0=in0, scalar1=s1, scalar2=None, op0=op0)
        return A(eng, "tensor_scalar", reads, writes, out=out, in0=in0, scalar1=s1, scalar2=s2, op0=op0, op1=op1)

    def STT(out, in0, scalar, in1, op0, op1, reads, writes, eng="dve"):
        return A(eng, "scalar_tensor_tensor", reads, writes, out=out, in0=in0, scalar=scalar, in1=in1, op0=op0, op1=op1)

    def tkey(row, c0):
        return ("T", row, 0 if c0 == 0 else 1)

    TALL = [("T", r_, i_) for r_ in range(4) for i_ in range(2)] + ["T3"]

    def tbar():
        S.add("dve", lambda e: e.engine_nop(), writes=list(TALL), name="tbar")

    def retag(old, new):
        S.add("dve", lambda e: e.engine_nop(), writes=list(old) + list(new), name="retag")

    psrr = [0]

    def psbank(n=1):
        r = []
        for _ in range(n):
            r.append(psrr[0] % 6)
            psrr[0] += 1
        return r

    wcount = [0]

    preloaded = {}

    def wprefetch(layer, key, size):
        preloaded[(layer, key)] = wload(layer, key, size)

    def wload(layer, key, size):
        if (layer, key) in preloaded:
            return preloaded.pop((layer, key))
        slot = wcount[0] % NSLOT
        wcount[0] += 1
        off = WOFF[key]
        DMA("pool", Wr[:, slot, 0:size], wst[layer, :, off:off + size], writes=[("W", slot)])
        return slot

    def dbg(nm, src_ap, reads):
        if nm in dbg_o:
            DMA("sp", dbg_o[nm], src_ap, reads=reads, is_out=True)

    def pvc(key, i=0, n=1):
        return pv[:, PV[key] + i:PV[key] + i + n]

    DMA("sp", pv[:], pvin[:, :], writes=["pv"])
    DMA("sp", ident[:], identin[:, :], writes=["ident"])
    DMA("sp", bsrow[:], bsin[:, :], writes=["bsrow"])
    DMA("sp", maskt[:], maskin[:, :], writes=["maskt"])
    A("dve", "tensor_copy", ["ident"], ["identb"], out=identb[:], in_=ident[:])
    A("dve", "memset", [], ["onesb"], ap=onesb[:], constant=1.0)
    A("dve", "memset", [], ["ones1"], ap=ones1[:], constant=1.0)
    A("dve", "memset", [], ["epst"], ap=epst[:, 0:1], constant=RMS_EPS)
    A("dve", "memset", [], ["epst"], ap=epst[:, 1:2], constant=LN_EPS)

    def rsqrt_(out, in_, scale, eps_col, reads, writes, npart=128):
        ACT(out, in_, AF.Sqrt, list(reads) + ["epst"], writes, scale=scale, bias=epst[0:npart, eps_col:eps_col + 1])
        A("dve", "reciprocal", writes, writes, out=out, in_=out)

    def rms_stats(sqbuf, sqkeys, NT, grp):
        for k in range(KC):
            ACT(sqbuf[:, k, 0:NT], xT[:, k, 0:NT], AF.Square, [("x", k)], [sqkeys[k]])
        for (c0, n) in grp:
            (b,) = psbank(1)
            AM("pe", [MM(ps[:, b, 0:n], onesb[:], sqbuf[:, k, c0:c0 + n], k == 0, k == KC - 1) for k in range(KC)],
               reads=list(sqkeys) + ["onesb"], writes=[("ps", b)])
            rsqrt_(Tt[:, 0, c0:c0 + n], ps[:, b, 0:n], 1.0 / D, 0, [("ps", b)], [tkey(0, c0)])

    def rmsnorm_to(dst, dstkeys, gkey, NT, grp):
        rms_stats(dst, dstkeys, NT, grp)
        tk = [tkey(0, c0) for (c0, n) in grp]
        for k in range(KC):
            STT(dst[:, k, 0:NT], xT[:, k, 0:NT], pvc(gkey, k), Tt[:, 0, 0:NT], ALU.mult, ALU.mult,
                [("x", k), "pv"] + tk, [dstkeys[k]])

    def mm_chunk(lhs_fn, nkc, rhs_fn, rhskeys, wkeys, grp):
        banks = psbank(len(grp))
        res = []
        for (c0, n), b in zip(grp, banks):
            AM("pe", [MM(ps[:, b, 0:n], lhs_fn(k), rhs_fn(k, c0, n), k == 0, k == nkc - 1) for k in range(nkc)],
               reads=list(rhskeys) + list(wkeys), writes=[("ps", b)])
            res.append((b, c0, n))
        return res

    def gelu_ps(src, srckey, t, tk, out_ap, out_keys):
        ACT(t, src, AF.Square, [srckey], [tk])
        TS(t, t, GELU_C, 1.0, ALU.mult, ALU.add, [tk], [tk])
        TT(t, src, t, ALU.mult, [tk, srckey], [tk])
        ACT(t, t, AF.Sigmoid, [tk], [tk], scale=GELU_S)
        TT(out_ap, src, t, ALU.mult, [tk, srckey], out_keys)

    def wview(slot):
        return Wr[:, slot, :].rearrange("p (k n) -> p k n", n=512)

    def do_layer(pas, l, NT, ns, grp, tiles):
        DMA("pool", pwb[:], pwin[l, :, :], writes=["pwb"])
        DMA("sp", wtmp[:, :, 0:128], wstin[l, :, :, :], writes=["wtmp"])
        DMA("sp", wtmp[0:64, :, 128:192], wssin[l, :, :, :], writes=["wtmp"])
        for h in range(4):
            TT(wsm[:, h, 0:128], wtmp[:, h, 0:128], maskt[:, 0:128], ALU.mult, ["wtmp", "maskt"], ["wsm"])
            TT(wsm[0:64, h, 128:192], wtmp[0:64, h, 128:192], maskt[0:64, 128:192], ALU.mult, ["wtmp", "maskt"], ["wsm"])

        tbar()
        rmsnorm_to(B1, B1K, ("g_mix", l), NT, grp)
        dbg("h_%d_%d" % (pas, l), B1[:], B1K)

        retag(B2K, VSET)
        DMA("sp", LNVG, lnvin[l, 0:1, :].partition_broadcast(128), writes=["LNV"])
        DMA("sp", LNVB, lnvin[l, 1:2, :].partition_broadcast(128), writes=["LNV"])

        for blk in range(2):
            slot = wload(l, ("win1", blk), 8192)
            wv = wview(slot)
            for j in range(4):
                c = blk * 4 + j
                res = mm_chunk(lambda k: wv[:, k, j * 128:(j + 1) * 128], KC,
                               lambda k, c0, n: B1[:, k, c0:c0 + n], B1K, [("W", slot)], grp)
                for gi, (b, c0, n) in enumerate(res):
                    gelu_ps(ps[:, b, 0:n], ("ps", b), Tt[:, 1 + gi, 0:n], ("T", 1 + gi, 0), Yb[:, c, c0:c0 + n], [("Y", c)])

        slots_v = [wload(l, ("win1", 2), 8192), wload(l, ("win1", 3), 8192)]
        for ti, (c0, nt, r0) in enumerate(tiles):
            banks = psbank(2)
            for hf in range(2):
                wv = wview(slots_v[hf])
                AM("pe", [MM(ps[0:nt, banks[hf], :], B1[:, k, c0:c0 + nt], wv[:, k, :], k == 0, k == KC - 1) for k in range(KC)],
                   reads=B1K + [("W", slots_v[hf])], writes=[("ps", banks[hf])])
            for hf in range(2):
                b = banks[hf]
                sl = slice(hf * 512, (hf + 1) * 512)
                gelu_ps(ps[0:nt, b, :], ("ps", b), VT1[0:nt, sl], "VT1", VT0[0:nt, sl], ["VT0"])
            A("dve", "reduce_sum", ["VT0"], ["stt"], out=stt[0:nt, 0:1], in_=VT0[0:nt, :], axis=AX.X)
            TS(stt[0:nt, 1:2], stt[0:nt, 0:1], -1.0 / 1024, None, ALU.mult, None, ["stt"], ["stt"])
            ACT(VT1[0:nt, :], VT0[0:nt, :], AF.Square, ["VT0", "stt"], ["VT1", "stt"], bias=stt[0:nt, 1:2], accum_out=stt[0:nt, 2:3])
            rsqrt_(stt[0:nt, 3:4], stt[0:nt, 2:3], 1.0 / 1024, 1, ["stt"], ["stt"], npart=nt)
            TS(VT0[0:nt, :], VT0[0:nt, :], stt[0:nt, 1:2], stt[0:nt, 3:4], ALU.add, ALU.mult, ["VT0", "stt"], ["VT0"])
            TT(VT0[0:nt, :], VT0[0:nt, :], LNVG[0:nt, :], ALU.mult, ["VT0", "LNV"], ["VT0"])
            TT(VT0[0:nt, :], VT0[0:nt, :], LNVB[0:nt, :], ALU.add, ["VT0", "LNV"], ["VT0"])
            ACT(VB[0:nt, :], VT0[0:nt, :], AF.Copy, ["VT0"], ["VB"])
            sample_tile = (pas == 1 and ti == 4)
            if sample_tile:
                DMA("sp", cv_o[l, :, :], VT0[0:64, :], reads=["VT0"], is_out=True)
            banks2 = psbank(2)
            for half in range(2):
                b = banks2[half]
                calls = []
                for j in range(4):
                    fc = half * 4 + j
                    hh = fc // 2
                    w_ap = wsm[0:nt, hh, 128:128 + nt] if sample_tile else wsm[:, hh, 0:128]
                    bo = ((l * 4 + hh) * 192) + (128 if sample_tile else 0)
                    calls.append(MM(ps[:, b, j * 128:j * 128 + nt], VB[0:nt, fc * 128:(fc + 1) * 128], w_ap, True, False))
                    calls.append(MM(ps[:, b, j * 128:j * 128 + nt], ones1[0:1, :], bsrow[0:1, bo:bo + nt], False, True))
                AM("pe", calls, reads=["VB", "wsm", "bsrow", "ones1"], writes=[("ps", b)])
                yk = [("Y", half * 4 + j) for j in range(4)]
                TT(Yb[:, half * 4:half * 4 + 4, c0:c0 + nt], ps[:, b, :].rearrange("p (j n) -> p j n", n=128)[:, :, 0:nt],
                   Yb[:, half * 4:half * 4 + 4, c0:c0 + nt], ALU.mult, [("ps", b)] + yk, yk)
        dbg("ya_%d_%d" % (pas, l), Yb[:, 0:8, :], YK[0:8])
        if STOP == "v":
            retag(VSET, B2K)
            return

        retag(VSET, CSET)
        if pas == 0:
            A("dve", "memset", [], ["GLs"], ap=GLs[:, 0:30], constant=0.0)
            A("dve", "memset", [], ["Ps"], ap=Psb_[:, 0:15], constant=0.0)
        pending_conv = [None]
        for blk in range(4):
            slot = wload(l, ("win1", 4 + blk), 8192)
            wv = wview(slot)
            for pr in range(2):
                c = blk * 2 + pr
                ra = mm_chunk(lambda k: wv[:, k, (2 * pr) * 128:(2 * pr + 1) * 128], KC,
                              lambda k, c0, n: B1[:, k, c0:c0 + n], B1K, [("W", slot)], grp)
                rg = mm_chunk(lambda k: wv[:, k, (2 * pr + 1) * 128:(2 * pr + 2) * 128], KC,
                              lambda k, c0, n: B1[:, k, c0:c0 + n], B1K, [("W", slot)], grp)
                if pas == 1:
                    DMA("sp", ST1[0:120, 0:512].rearrange("p (g n) -> p g n", n=128),
                        scin[l, :, :, c * 128:(c + 1) * 128].rearrange("(g s) r n -> (s r) g n", g=4), writes=["ST1"])
                    (b,) = psbank(1)
                    AM("pe", [TP(ps[:, b, g * 120:(g + 1) * 120], ST1[0:120, g * 128:(g + 1) * 128], ident[0:120, 0:120]) for g in range(4)],
                       reads=["ST1", "ident"], writes=[("ps", b)])
                    ACT(GLs3[:, :, 0:30], ps[:, b, 0:480].rearrange("p (s r) -> p s r", r=30), AF.Copy, [("ps", b)], ["GLs"])
                for gi in range(2):
                    (ba, c0, n) = ra[gi]
                    (bg, _, _) = rg[gi]
                    tk = ("T", 1 + gi, 0)
                    t = Tt[:, 1 + gi, 0:n]
                    ACT(t, ps[:, bg, 0:n], AF.Sigmoid, [("ps", bg)], [tk])
                    in0 = ps[:, ba, 0:n]
                    if gi == 0:
                        o_ap = GLm[:, 30:542]; ok = "GLm"
                    elif pas == 0:
                        o_ap = GLs[:, 30:158]; ok = "GLs"
                    else:
                        o_ap = GLs3[:, :, 30:34]; ok = "GLs"
                        t = t.rearrange("p (s r) -> p s r", r=4)
                        in0 = in0.rearrange("p (s r) -> p s r", r=4)
                    TT(o_ap, in0, t, ALU.mult, [tk, ("ps", ba)], [ok])
                if pas == 0:
                    TS(GLm[:, 0:30], GLs[:, 128:158], pvc("flag"), None, ALU.mult, None, ["GLs", "pv"], ["GLm"])
                    A("dve", "tensor_copy", ["GLm"], ["hconv"], out=hconv[:, l, c, :], in_=GLm[:, 512:542])
                else:
                    A("dve", "tensor_copy", ["hconv"], ["GLm"], out=GLm[:, 0:30], in_=hconv[:, l, c, :])
                    A("dve", "tensor_copy", ["GLm"], ["hconv"], out=hconv[:, l, c, :], in_=GLm[:, 512:542])
                if pending_conv[0] is not None:
                    pending_conv[0]()
                cw = PV[("conv_w", l)] + c * 31
                ACT(GLmb, GLm[:, 0:542], AF.Copy, ["GLm"], [("PT", 0)])
                if pas == 0:
                    ACT(GLsb[:, 0:158], GLs[:, 0:158], AF.Copy, ["GLs"], [("PT", 1)])
                else:
                    ACT(GLsb, GLs, AF.Copy, ["GLs"], [("PT", 1)])
                AM("dve", [("tensor_scalar", dict(out=DG[:, k, :], in0=identb[:], scalar1=pv[:, cw + k:cw + k + 1], scalar2=None, op0=ALU.mult)) for k in range(31)],
                   reads=["identb", "pv"], writes=["DG", "ACC"])

                def conv_mm(c=c):
                    bm_, bs2_ = psbank(2)
                    AM("pe", [MM(ps[:, bm_, 0:512], DG[:, k, :], GLmb[:, k:k + 512], k == 0, k == 30) for k in range(31)],
                       reads=["DG", ("PT", 0)], writes=[("ps", bm_)])
                    if pas == 0:
                        AM("pe", [MM(ps[:, bs2_, 0:128], DG[:, k, :], GLsb[:, k:k + 128], k == 0, k == 30) for k in range(31)],
                           reads=["DG", ("PT", 1)], writes=[("ps", bs2_)])
                    else:
                        AM("pe", [MM(ps[:, bs2_, 0:64].rearrange("p (s r) -> p s r", r=4), DG[:, k, :], GLsb3[:, :, k:k + 4], k == 0, k == 30) for k in range(31)],
                           reads=["DG", ("PT", 1)], writes=[("ps", bs2_)])
                    cb = pvc(("conv_b", l), c)
                    ACT(Yb[:, 8 + c, 0:512], ps[:, bm_, 0:512], AF.Identity, [("ps", bm_), "pv"], [("Y", 8 + c)], bias=cb)
                    ACT(Yb[:, 8 + c, 512:NT], ps[:, bs2_, 0:NT - 512], AF.Identity, [("ps", bs2_), "pv"], [("Y", 8 + c)], bias=cb)
                pending_conv[0] = conv_mm
                if pas == 1:
                    (b,) = psbank(1)
                    A("dve", "tensor_copy", ["GLs"], ["t3b"], out=t3b[:].rearrange("p (s r) -> p s r", r=4), in_=GLs3[:, :, 30:34])
                    AM("pe", [TP(ps[0:64, b, 0:128], t3b[:], ident[:, :])], reads=["t3b", "ident"], writes=[("ps", b)])
                    A("dve", "tensor_copy", [("ps", b)], ["ostg"], out=ostg[0:64, c * 128:(c + 1) * 128], in_=ps[0:64, b, 0:128])
        pending_conv[0]()
        if pas == 1:
            for s in range(16):
                DMA("sp", cs_o[l, s, 26:30, :], ostg[s * 4:(s + 1) * 4, :], reads=["ostg"], is_out=True)
            DMA("sp", cs_o[l, :, 0:26, :], scin[l, :, 4:30, :], is_out=True)
            banks = psbank(2)
            for half in range(2):
                b = banks[half]
                AM("pe", [TP(ps[0:30, b, j * 128:(j + 1) * 128], hconv[:, l, half * 4 + j, :], ident[:, :]) for j in range(4)],
                   reads=["hconv", "ident"], writes=[("ps", b)])
                A("dve", "tensor_copy", [("ps", b)], [("kst", 0), ("kst", 1)], out=kst[0:30, half, :], in_=ps[0:30, b, :])
            DMA("sp", cp_o[l, :, :], kst[0:30, :, :].rearrange("p a n -> p (a n)"), reads=[("kst", 0), ("kst", 1)], is_out=True)

        for (c0, n) in grp:
            b1, b2 = psbank(2)
            AM("pe", [MM(ps[:, b1, 0:n], onesb[:], Yb[:, 8 + c, c0:c0 + n], c == 0, c == 7) for c in range(8)],
               reads=YK[8:16] + ["onesb"], writes=[("ps", b1)])
            for c in range(8):
                ACT(PTb[:, c % 2, 0:n], Yb[:, 8 + c, c0:c0 + n], AF.Square, [("Y", 8 + c)], [("PT", c % 2)])
                AM("pe", [MM(ps[:, b2, 0:n], onesb[:], PTb[:, c % 2, 0:n], c == 0, c == 7)],
                   reads=[("PT", c % 2), "onesb"], writes=[("ps", b2)])
            TS(Tt[:, 1, c0:c0 + n], ps[:, b1, 0:n], 1.0 / 1024, None, ALU.mult, None, [("ps", b1)], [tkey(1, c0)])
            TT(Tt[:, 2, c0:c0 + n], Tt[:, 1, c0:c0 + n], Tt[:, 1, c0:c0 + n], ALU.mult, [tkey(1, c0)], [tkey(2, c0)])
            STT(Tt[:, 2, c0:c0 + n], ps[:, b2, 0:n], 1.0 / 1024, Tt[:, 2, c0:c0 + n], ALU.mult, ALU.subtract,
                [("ps", b2), tkey(2, c0)], [tkey(2, c0)])
            rsqrt_(Tt[:, 2, c0:c0 + n], Tt[:, 2, c0:c0 + n], 1.0, 1, [tkey(2, c0)], [tkey(2, c0)])
        tks = [("T", 1, 0), ("T", 1, 1), ("T", 2, 0), ("T", 2, 1)]
        for c in range(8):
            TT(Tt[:, 3, 0:NT], Yb[:, 8 + c, 0:NT], Tt[:, 1, 0:NT], ALU.subtract, [("Y", 8 + c)] + tks, ["T3"])
            TT(Tt[:, 3, 0:NT], Tt[:, 3, 0:NT], Tt[:, 2, 0:NT], ALU.mult, ["T3"] + tks, ["T3"])
            ACT(Yb[:, 8 + c, 0:NT], Tt[:, 3, 0:NT], AF.Silu, ["T3", "pv"], [("Y", 8 + c)],
                scale=pvc(("ln_c_g", l), c), bias=pvc(("ln_c_b", l), c))
        dbg("yb_%d_%d" % (pas, l), Yb[:, 8:16, :], YK[8:16])
        if STOP == "conv":
            retag(CSET, B2K)
            return

        pw4 = pwb[:].rearrange("p (g k n) -> p g k n", g=4, k=2)
        for blk in range(2):
            slot = wload(l, ("win1", 8 + blk), 8192)
            wv = wview(slot)
            for j in range(4):
                c = blk * 4 + j
                gi_ = c // 2
                w = 2 << gi_
                res = mm_chunk(lambda k: wv[:, k, j * 128:(j + 1) * 128], KC,
                               lambda k, c0, n: B1[:, k, c0:c0 + n], B1K, [("W", slot)], grp)
                if pas == 1:
                    DMA("sp", ST1[0:120, 0:256].rearrange("p (g n) -> p g n", n=128),
                        spin[l, :, :, c * 128:(c + 1) * 128].rearrange("(g s) r n -> (s r) g n", g=2), writes=["ST1"])
                    (b,) = psbank(1)
                    AM("pe", [TP(ps[:, b, g * 120:(g + 1) * 120], ST1[0:120, g * 128:(g + 1) * 128], ident[0:120, 0:120]) for g in range(2)],
                       reads=["ST1", "ident"], writes=[("ps", b)])
                    ACT(Ps3[:, :, 0:15], ps[:, b, 0:240].rearrange("p (s r) -> p s r", r=15), AF.Copy, [("ps", b)], ["Ps"])
                (bm, _, _), (bs_, _, nsn) = res
                ACT(Pm[:, 15:527], ps[:, bm, 0:512], AF.Copy, [("ps", bm)], ["Pm"])
                if pas == 0:
                    ACT(Psb_[:, 15:143], ps[:, bs_, 0:128], AF.Copy, [("ps", bs_)], ["Ps"])
                    TS(Pm[:, 0:15], Psb_[:, 128:143], pvc("flag"), None, ALU.mult, None, ["Ps", "pv"], ["Pm"])
                    A("dve", "tensor_copy", ["Pm"], ["hpool"], out=hpool[:, l, c, :], in_=Pm[:, 512:527])
                else:
                    ACT(Ps3[:, :, 15:19], ps[:, bs_, 0:64].rearrange("p (s r) -> p s r", r=4), AF.Copy, [("ps", bs_)], ["Ps"])
                    A("dve", "tensor_copy", ["hpool"], ["Pm"], out=Pm[:, 0:15], in_=hpool[:, l, c, :])
                    A("dve", "tensor_copy", ["Pm"], ["hpool"], out=hpool[:, l, c, :], in_=Pm[:, 512:527])
                    (b,) = psbank(1)
                    A("dve", "tensor_copy", ["Ps"], ["t3b"], out=t3b[:].rearrange("p (s r) -> p s r", r=4), in_=Ps3[:, :, 15:19])
                    AM("pe", [TP(ps[0:64, b, 0:128], t3b[:], ident[:, :])], reads=["t3b", "ident"], writes=[("ps", b)])
                    A("dve", "tensor_copy", [("ps", b)], ["ostg"], out=ostg[0:64, c * 128:(c + 1) * 128], in_=ps[0:64, b, 0:128])
                calls = []
                for which in range(2):
                    if which == 0:
                        L = 527
                        src = lambda a, b_: Pm[:, a:b_]
                        t1 = lambda a, b_: Tt[:, 1, a:b_]
                        t2 = lambda a, b_: Tt[:, 2, a:b_]
                    elif pas == 0:
                        L = 143
                        src = lambda a, b_: Psb_[:, a:b_]
                        t1 = lambda a, b_: ACC[:, a:b_]
                        t2 = lambda a, b_: ACC[:, 160 + a:160 + b_]
                    else:
                        L = 19
                        src = lambda a, b_: Ps3[:, :, a:b_]
                        A3 = ACC[:, 0:304].rearrange("p (s n) -> p s n", n=19)
                        B3 = ACC[:, 320:624].rearrange("p (s n) -> p s n", n=19)
                        t1 = lambda a, b_, A3=A3: A3[:, :, a:b_]
                        t2 = lambda a, b_, B3=B3: B3[:, :, a:b_]
                    cur = src
                    sh = 1
                    bufs = [t1, t2]
                    bi = 0
                    lo = 0
                    while sh < w:
                        lo2 = lo + sh
                        nxt = bufs[bi]
                        calls.append(("tensor_tensor", dict(out=nxt(lo2, L), in0=cur(lo2, L), in1=cur(lo2 - sh, L - sh), op=ALU.add), which == 1 and pas == 1))
                        cur = nxt
                        bi ^= 1
                        lo = lo2
                        sh *= 2
                    hist = 15
                    if which == 0:
                        calls.append(("scalar_tensor_tensor", dict(out=PMc[:, c % 2, 0:512], in0=cur(hist, L), scalar=1.0 / w, in1=src(hist, L),
                                                                   op0=ALU.mult, op1=ALU.subtract), False))
                        if pas == 0:
                            ic = PV["icnt"] + (c // 2) * 16
                            calls.append(("tensor_tensor", dict(out=Tt[:, 3, 0:16], in0=cur(hist, hist + 16), in1=pv[:, ic:ic + 16], op=ALU.mult), True))
                            calls.append(("tensor_tensor", dict(out=PMc[:, c % 2, 0:16], in0=Tt[:, 3, 0:16], in1=src(hist, hist + 16), op=ALU.subtract), True))
                    else:
                        if pas == 0:
                            o = PMc[:, c % 2, 512:640]
                        else:
                            o = PMc[:, c % 2, 512:576].rearrange("p (s r) -> p s r", r=4)
                        calls.append(("scalar_tensor_tensor", dict(out=o, in0=cur(hist, L), scalar=1.0 / w, in1=src(hist, L), op0=ALU.mult, op1=ALU.subtract), pas == 1))
                CHAIN("dve", calls, reads=["Pm", "Ps", "pv"],
                   writes=[("PMc", c % 2), ("T", 1, 0), ("T", 2, 0), ("T", 1, 1), ("T", 2, 1), "ACC", "T3"])
                if c % 2 == 1:
                    for oc in range(2):
                        res2 = mm_chunk(lambda k: pw4[:, gi_, k, oc * 128:(oc + 1) * 128], 2,
                                        lambda k, c0, n: PMc[:, k, c0:c0 + n], [("PMc", 0), ("PMc", 1)], ["pwb"], grp)
                        for (b, c0, n) in res2:
                            ACT(Yb[:, 16 + 2 * gi_ + oc, c0:c0 + n], ps[:, b, 0:n], AF.Copy, [("ps", b), "pv"], [("Y", 16 + 2 * gi_ + oc)],
                                scale=pvc(("pool_scale", l), 2 * gi_ + oc))
        if pas == 1:
            for s in range(16):
                DMA("sp", ps_o[l, s, 11:15, :], ostg[s * 4:(s + 1) * 4, :], reads=["ostg"], is_out=True)
            DMA("sp", ps_o[l, :, 0:11, :], spin[l, :, 4:15, :], is_out=True)
            banks = psbank(2)
            for half in range(2):
                b = banks[half]
                AM("pe", [TP(ps[0:15, b, j * 128:(j + 1) * 128], hpool[:, l, half * 4 + j, :], ident[:, :]) for j in range(4)],
                   reads=["hpool", "ident"], writes=[("ps", b)])
                A("dve", "tensor_copy", [("ps", b)], [("kst", 0), ("kst", 1)], out=kst[0:15, half, :], in_=ps[0:15, b, :])
            DMA("sp", pp_o[l, :, :], kst[0:15, :, :].rearrange("p a n -> p (a n)"), reads=[("kst", 0), ("kst", 1)], is_out=True)
        dbg("yc_%d_%d" % (pas, l), Yb[:, 16:24, :], YK[16:24])
        retag(CSET, B2K)
        if STOP == "pool":
            return

        if pas == 0 and l == NL - 1:
            grp = [(0, MAIN)]
            NT = MAIN
        else:
            hN = NT // 2
            grp = [(0, hN), (hN, NT - hN)]
        tbar()
        for d in range(16):
            sg = wload(l, ("mg", d), 6144)
            sbr = wload(l, ("mb", d), 3072)
            gv = Wr[:, sg, 0:6144].rearrange("p (n k c) -> p n k c", n=3, k=16)
            bv = Wr[:, sbr, 0:3072].rearrange("p (n k c) -> p n k c", n=3, k=8)
            for n_ in range(3):
                rg = mm_chunk(lambda k: gv[:, n_, k, :], KC, lambda k, c0, n: B1[:, k, c0:c0 + n], B1K, [("W", sg)], grp)
                rb = mm_chunk(lambda k: bv[:, n_, k, :], 8, lambda k, c0, n: Yb[:, n_ * 8 + k, c0:c0 + n],
                              YK[n_ * 8:n_ * 8 + 8], [("W", sbr)], grp)
                for gi in range(len(grp)):
                    (bg, c0, n) = rg[gi]
                    (bb, _, _) = rb[gi]
                    tk = tkey(1, c0)
                    ak = tkey(2, c0)
                    ACT(Tt[:, 1, c0:c0 + n], ps[:, bg, 0:n], AF.Sigmoid, [("ps", bg)], [tk])
                    if n_ == 0:
                        TT(Tt[:, 2, c0:c0 + n], ps[:, bb, 0:n], Tt[:, 1, c0:c0 + n], ALU.mult, [("ps", bb), tk], [ak])
                    else:
                        TT(Tt[:, 1, c0:c0 + n], ps[:, bb, 0:n], Tt[:, 1, c0:c0 + n], ALU.mult, [("ps", bb), tk], [tk])
                        if n_ == 1:
                            TT(Tt[:, 2, c0:c0 + n], Tt[:, 2, c0:c0 + n], Tt[:, 1, c0:c0 + n], ALU.add, [tk, ak], [ak])
                        else:
                            TT(B2[:, d, c0:c0 + n], Tt[:, 2, c0:c0 + n], Tt[:, 1, c0:c0 + n], ALU.add, [tk, ak], [("B2", d)])
        dbg("mg_%d_%d" % (pas, l), B2[:], B2K)
        if STOP == "merge":
            return

        def proj_residual(key, src, srckeys):
            for blk in range(4):
                slot = wload(l, (key, blk), 8192)
                wv = wview(slot)
                for j in range(4):
                    e_ = blk * 4 + j
                    res = mm_chunk(lambda k: wv[:, k, j * 128:(j + 1) * 128], KC,
                                   lambda k, c0, n: src[:, k, c0:c0 + n], srckeys, [("W", slot)], grp)
                    for (b, c0, n) in res:
                        TT(xT[:, e_, c0:c0 + n], ps[:, b, 0:n], xT[:, e_, c0:c0 + n], ALU.add, [("ps", b), ("x", e_)], [("x", e_)])

        proj_residual("wo", B2, B2K)
        dbg("x1_%d_%d" % (pas, l), xT[:], XK)
        if STOP == "wo":
            return

        rmsnorm_to(B1, B1K, ("g_xattn", l), NT, grp)
        for blk in range(4):
            slot = wload(l, ("xq", blk), 8192)
            wv = wview(slot)
            for j in range(4):
                e_ = blk * 4 + j
                res = mm_chunk(lambda k: wv[:, k, j * 128:(j + 1) * 128], KC,
                               lambda k, c0, n: B1[:, k, c0:c0 + n], B1K, [("W", slot)], grp)
                for (b, c0, n) in res:
                    ACT(B2[:, e_, c0:c0 + n], ps[:, b, 0:n], AF.Copy, [("ps", b)], [("B2", e_)])
        if STOP == "attn_q":
            return
        tbar()
        retag(YK + B1K, ASET)
        for tkc in range(2):
            DMA("sp", MEMSTG, memin[tkc * 128:(tkc + 1) * 128, :], writes=MEMK)
            for q in range(4):
                ACT(Tt[:, 1, 0:512], MEMSTG[:, q * 512:(q + 1) * 512], AF.Square, MEMK, [("T", 1, 0), "stt"], accum_out=stt[:, 8 + q:9 + q])
            A("dve", "reduce_sum", ["stt"], ["stt"], out=stt[:, 12:13], in_=stt[:, 8:12], axis=AX.X)
            rsqrt_(stt[:, 13:14], stt[:, 12:13], 1.0 / D, 0, ["stt"], ["stt"])
            TS(MEMSTG, MEMSTG, stt[:, 13:14], None, ALU.mult, None, ["stt"] + MEMK, MEMK)
            for q in range(4):
                (b,) = psbank(1)
                AM("pe", [TP(ps[:, b, j * 128:(j + 1) * 128], MEMSTG[:, (q * 4 + j) * 128:(q * 4 + j + 1) * 128], ident[:, :]) for j in range(4)],
                   reads=MEMK + ["ident"], writes=[("ps", b)])
                for j in range(4):
                    k = q * 4 + j
                    ACT(MTB[:, k, tkc * 128:(tkc + 1) * 128], ps[:, b, j * 128:(j + 1) * 128], AF.Copy, [("ps", b), "pv"], ["MTB"],
                        scale=pvc(("g_mem", l), k))
        if STOP == "attn_mem":
            retag(ASET, YK + B1K)
            return
        for which, key, out_d in ((0, "xk", mk_o), (1, "xv", mv_o)):
            for blk in range(4):
                slot = wload(l, (key, blk), 8192)
                wv = wview(slot)
                for tkc in range(2):
                    (b,) = psbank(1)
                    AM("pe", [MM(ps[:, b, :], MTB[:, k, tkc * 128:(tkc + 1) * 128], wv[:, k, :], k == 0, k == KC - 1) for k in range(KC)],
                       reads=["MTB", ("W", slot)], writes=[("ps", b)])
                    if pas == 0 and "kvout" not in SKIP:
                        if True:
                            A("dve", "tensor_copy", [("ps", b)], [("kst", tkc)], out=kst[:, tkc, :], in_=ps[:, b, :])
                        else:
                            ACT(kst[:, tkc, :], ps[:, b, :], AF.Copy, [("ps", b)], [("kst", tkc)])
                        DMA("sp", out_d[l, tkc * 128:(tkc + 1) * 128, blk * 512:(blk + 1) * 512], kst[:, tkc, :], reads=[("kst", tkc)], is_out=True)
                    if which == 1:
                        A("dve", "tensor_copy", [("ps", b)], ["VP"], out=VP[:, tkc, blk * 512:(blk + 1) * 512], in_=ps[:, b, :])
                if which == 0 and "kt" not in SKIP:
                    for j in range(4):
                        (b,) = psbank(1)
                        AM("pe", [MM(ps[:, b, 0:256], wv[:, k, j * 128:(j + 1) * 128], MTB[:, k, :], k == 0, k == KC - 1) for k in range(KC)],
                           reads=["MTB", ("W", slot)], writes=[("ps", b)])
                        ACT(KT[:, blk * 4 + j, :], ps[:, b, 0:256], AF.Copy, [("ps", b)], ["KT"])
        if STOP == "attn_kv":
            retag(ASET, YK + B1K)
            return
        for blk in range(3):
            wprefetch(l, ("xo", blk), 8192)
        pgrp = [(0, MAIN)] + ([(MAIN, 128)] if (pas == 0 and len(grp) > 1) else [])
        for hd in range(4):
            qk = [("B2", hd * 4 + dc) for dc in range(4)]
            for (c0, n) in pgrp:
                for kcx in range(2):
                    (b,) = psbank(1)
                    AM("pe", [MM(ps[:, b, 0:n], KT[:, hd * 4 + dc, kcx * 128:(kcx + 1) * 128], B2[:, hd * 4 + dc, c0:c0 + n], dc == 0, dc == 3) for dc in range(4)],
                       reads=["KT"] + qk, writes=[("ps", b)])
                    ACT(PTb[:, kcx, c0:c0 + n], ps[:, b, 0:n], AF.Exp, [("ps", b)], [("PT", kcx)], scale=ATT_SCALE)
                (b,) = psbank(1)
                AM("pe", [MM(ps[:, b, 0:n], onesb[:], PTb[:, 0, c0:c0 + n], True, False), MM(ps[:, b, 0:n], onesb[:], PTb[:, 1, c0:c0 + n], False, True)],
                   reads=[("PT", 0), ("PT", 1), "onesb"], writes=[("ps", b)])
                A("dve", "reciprocal", [("ps", b)], [tkey(1, c0)], out=Tt[:, 1, c0:c0 + n], in_=ps[:, b, 0:n])
                for dc in range(4):
                    (b,) = psbank(1)
                    ch = hd * 4 + dc
                    AM("pe", [MM(ps[:, b, 0:n], VP[:, 0, ch * 128:(ch + 1) * 128], PTb[:, 0, c0:c0 + n], True, False),
                              MM(ps[:, b, 0:n], VP[:, 1, ch * 128:(ch + 1) * 128], PTb[:, 1, c0:c0 + n], False, True)],
                       reads=["VP", ("PT", 0), ("PT", 1)], writes=[("ps", b)])
                    TT(B2[:, ch, c0:c0 + n], ps[:, b, 0:n], Tt[:, 1, c0:c0 + n], ALU.mult, [("ps", b), tkey(1, c0)], [("B2", ch)])
        if pas == 1:
            KR = [KRAW0, KRAW1]; KRK = ["KR0", "KR1"]
            VSB = [VS0, VS1]; VSK = ["VS0", "MTB"]
            for s in range(16):
                vb = s % 2
                pt4 = pts[:, vb, :].rearrange("p (k h t) -> p k h t", k=2, h=4)
                DMA("pool", VSB[vb], vcin[l, s, :, :].rearrange("(k p) n -> p k n", p=128), writes=[VSK[vb]])
                for kcx in range(2):
                    DMA("pool", KR[kcx], kcin[l, s, kcx * 128:(kcx + 1) * 128, :], writes=[KRK[kcx]])
                    for q in range(2):
                        AM("pe", [TP(psb[:, q, j * 128:(j + 1) * 128], KR[kcx][:, (q * 8 + j) * 128:(q * 8 + j + 1) * 128], identb[:, :]) for j in range(8)],
                           reads=[KRK[kcx], "identb"], writes=[("psb", q)])
                        o_ = KTS[:, q * 8:(q + 1) * 8, kcx * 128:(kcx + 1) * 128]
                        i_ = psb[:, q, :].rearrange("p (j n) -> p j n", n=128)
                        if q == 0:
                            ACT(o_, i_, AF.Copy, [("psb", q)], [("KTS", kcx)])
                        else:
                            A("dve", "tensor_copy", [("psb", q)], [("KTS", kcx)], out=o_, in_=i_)
                (b,) = psbank(1)
                cs0 = MAIN + s * 4
                calls = []
                for kcx in range(2):
                    for hd in range(4):
                        for dc in range(4):
                            calls.append(MM(ps[:, b, (kcx * 4 + hd) * 4:(kcx * 4 + hd) * 4 + 4], KTS[:, hd * 4 + dc, kcx * 128:(kcx + 1) * 128],
                                            B2[:, hd * 4 + dc, cs0:cs0 + 4], dc == 0, dc == 3))
                AM("pe", calls, reads=[("KTS", 0), ("KTS", 1)] + B2K, writes=[("ps", b)])
                ACT(pts[:, vb, :], ps[:, b, 0:32], AF.Exp, [("ps", b)], [("pts", vb)], scale=ATT_SCALE)
                (b2_,) = psbank(1)
                calls = []
                for hd in range(4):
                    for r in range(4):
                        for kcx in range(2):
                            calls.append(MM(ps[:, b2_, 64 + hd * 16 + r * 4:64 + hd * 16 + r * 4 + 4], onesb[:], pt4[:, kcx, hd, :], kcx == 0, kcx == 1))
                for dc16 in range(16):
                    hd = dc16 // 4
                    for kcx in range(2):
                        calls.append(MM(ps[:, b2_, dc16 * 4:dc16 * 4 + 4], VSB[vb][:, kcx, dc16 * 128:(dc16 + 1) * 128], pt4[:, kcx, hd, :], kcx == 0, kcx == 1))
                AM("pe", calls, reads=[("pts", vb), VSK[vb], "onesb"], writes=[("ps", b2_)])
                A("dve", "reciprocal", [("ps", b2_)], [("rss", vb)], out=rss[:, vb, :], in_=ps[:, b2_, 64:128])
                TT(ots[:, :, s * 4:(s + 1) * 4], ps[:, b2_, 0:64].rearrange("p (d t) -> p d t", t=4), rss[:, vb, :].rearrange("p (d t) -> p d t", t=4), ALU.mult,
                   [("ps", b2_), ("rss", vb)], ["ots"])
            A("dve", "tensor_copy", ["ots"] + B2K, B2K, out=B2[:, :, MAIN:MAIN + 64], in_=ots[:])
        retag(ASET, YK + B1K)
        tbar()
        dbg("ot_%d_%d" % (pas, l), B2[:], B2K)
        proj_residual("xo", B2, B2K)
        dbg("x2_%d_%d" % (pas, l), xT[:], XK)
        if STOP == "attn":
            return

        rmsnorm_to(B1, B1K, ("g_mlp", l), NT, grp)
        for fb in range(4):
            for blk in range(4):
                slot = wload(l, ("up", fb, blk), 8192)
                wv = wview(slot)
                for j in range(4):
                    f_ = blk * 4 + j
                    res = mm_chunk(lambda k: wv[:, k, j * 128:(j + 1) * 128], KC,
                                   lambda k, c0, n: B1[:, k, c0:c0 + n], B1K, [("W", slot)], grp)
                    for gi, (b, c0, n) in enumerate(res):
                        tk = ("T", 1 + gi, 0)
                        ACT(Tt[:, 1 + gi, 0:n], ps[:, b, 0:n], AF.Relu, [("ps", b)], [tk])
                        TT(B2[:, f_, c0:c0 + n], Tt[:, 1 + gi, 0:n], Tt[:, 1 + gi, 0:n], ALU.mult, [tk], [("B2", f_)])
            for blk in range(4):
                slot = wload(l, ("dn", fb, blk), 8192)
                wv = wview(slot)
                for j in range(4):
                    e_ = blk * 4 + j
                    res = mm_chunk(lambda k: wv[:, k, j * 128:(j + 1) * 128], KC,
                                   lambda k, c0, n: B2[:, k, c0:c0 + n], B2K, [("W", slot)], grp)
                    for (b, c0, n) in res:
                        TT(xT[:, e_, c0:c0 + n], ps[:, b, 0:n], xT[:, e_, c0:c0 + n], ALU.add, [("ps", b), ("x", e_)], [("x", e_)])
        dbg("x3_%d_%d" % (pas, l), xT[:], XK)

    def do_pass(pas):
        ns = 128 if pas == 0 else 64
        NT = MAIN + ns
        grp = [(0, MAIN), (MAIN, ns)]
        if pas == 0:
            tiles = [(i * 128, 128, 128 + i * 128) for i in range(4)] + [(MAIN, 128, 0)]
        else:
            tiles = [(i * 128, 128, 128 + 512 + i * 128) for i in range(4)] + [(MAIN, 64, 1152)]
        retag(B2K, ["STG"])
        for (c0, nt, r0) in tiles:
            DMA("sp", STG[0:nt, :], xin[r0:r0 + nt, :], writes=["STG"])
            for q in range(4):
                (b,) = psbank(1)
                AM("pe", [TP(ps[:, b, j * 128:j * 128 + nt], STG[0:nt, (q * 4 + j) * 128:(q * 4 + j + 1) * 128], ident[0:nt, 0:nt]) for j in range(4)],
                   reads=["STG", "ident"], writes=[("ps", b)])
                ACT(xT[:, q * 4:q * 4 + 4, c0:c0 + nt], ps[:, b, :].rearrange("p (j n) -> p j n", n=128)[:, :, 0:nt], AF.Copy,
                    [("ps", b)], [("x", q * 4 + j) for j in range(4)])
        retag(["STG"], B2K)
        for l in range(NL):
            if l < NLAYERS_RUN:
                do_layer(pas, l, NT, ns, grp, tiles)
        tbar()
        rms_stats(B1, B1K, NT, grp)
        retag(B2K, ["STG"])
        out_tiles = [(i * 128, 128, (0 if pas == 0 else 512) + i * 128) for i in range(4)]
        if pas == 1:
            out_tiles.append((MAIN, 64, 1024))
        for (c0, nt, orow) in out_tiles:
            for k in range(KC):
                STT(xT[:, k, c0:c0 + nt], xT[:, k, c0:c0 + nt], pvc("g_final", k), Tt[:, 0, c0:c0 + nt], ALU.mult, ALU.mult,
                    [("x", k), "pv", ("T", 0, 0), ("T", 0, 1)], [("x", k)])
            for q in range(4):
                (b,) = psbank(1)
                AM("pe", [TP(ps[0:nt, b, j * 128:(j + 1) * 128], xT[:, q * 4 + j, c0:c0 + nt], ident[:, :]) for j in range(4)],
                   reads=XK[q * 4:q * 4 + 4] + ["ident"], writes=[("ps", b)])
                ACT(STG[0:nt, q * 512:(q + 1) * 512], ps[0:nt, b, :], AF.Copy, [("ps", b)], ["STG"])
            DMA("sp", y_o[orow:orow + nt, :], STG[0:nt, :], reads=["STG"], is_out=True)
        retag(["STG"], B2K)

    for pas in PASSES_RUN:
        do_pass(pas)

    S.emit(nc, es)
    es.close()
    return nc


NLAYERS_RUN = NL
PASSES_RUN = (0, 1)
STOP = None
WST_SHAPE = None
SKIP = ()


def make_in_maps(inp):
    f = lambda a: np.ascontiguousarray(np.asarray(a, dtype=np.float32))
    I = {k: np.asarray(v) for k, v in inp.items()}
    wstream = np.stack([
        pack_weights(I["w_in"][l], I["w_branch"][l], I["w_out"][l], I["w_xq"][l], I["w_xk"][l], I["w_xv"][l],
                     I["w_xo"][l], I["w_up"][l], I["w_down"][l]) for l in range(NL)])
    lnv = f(np.stack([np.stack([I["ln_v_g"][l], I["ln_v_b"][l]]) for l in range(NL)]))
    wstT = f(np.transpose(I["w_s"], (0, 3, 1, 2)))
    wss = f(np.tile(np.transpose(I["w_s"][:, :, :4, :4], (0, 3, 1, 2)), (1, 16, 1, 16)))
    mask = np.zeros((128, 192), np.float32)
    mask[:, :128] = np.triu(np.ones((128, 128), np.float32))
    blk = np.kron(np.eye(16, dtype=np.float32), np.triu(np.ones((4, 4), np.float32)))
    mask[:64, 128:192] = blk
    bs = np.zeros((NL, 4, 192), np.float32)
    bs[:, :, :128] = I["b_s"]
    bs[:, :, 128:192] = np.tile(I["b_s"][:, :, :4], (1, 1, 16))
    bs = f(bs.reshape(1, -1))
    pw = f(np.stack([np.transpose(I["pool_w"][l].reshape(4, 2, 128, 256), (2, 0, 1, 3)).reshape(128, 2048) for l in range(NL)]))
    ident = np.eye(128, dtype=np.float32)

    def pvec(v):
        return np.asarray(v, np.float32).reshape(-1, 128).T

    maps = []
    for c in range(8):
        b, half = c // 2, c % 2
        pvh = np.zeros((128, NPV), np.float32)
        for nm in ("g_mix", "g_xattn", "g_mem", "g_mlp"):
            for l in range(NL):
                pvh[:, PV[(nm, l)]:PV[(nm, l)] + 16] = pvec(I[nm][l])
        pvh[:, PV["g_final"]:PV["g_final"] + 16] = pvec(I["g_final"])
        for l in range(NL):
            cw = np.asarray(I["conv_w"][l], np.float32)
            pvh[:, PV[("conv_w", l)]:PV[("conv_w", l)] + 248] = cw.T.reshape(8, 128, 31).transpose(1, 0, 2).reshape(128, 248)
            for nm in ("conv_b", "ln_c_g", "ln_c_b", "pool_scale"):
                pvh[:, PV[(nm, l)]:PV[(nm, l)] + 8] = pvec(I[nm][l])
        pvh[:, PV["flag"]] = float(half)
        ic = np.zeros((4, 16), np.float32)
        for gi, w in enumerate((2, 4, 8, 16)):
            for t in range(16):
                ic[gi, t] = 1.0 / (min(w, t + 1) if half == 0 else w)
        pvh[:, PV["icnt"]:PV["icnt"] + 64] = ic.reshape(1, 64)
        xp = I["x_prompt"][b]
        own = xp[half * 1024:(half + 1) * 1024]
        halo = xp[896:1024] if half == 1 else np.zeros((128, D), np.float32)
        xs = I["x_sample"][c * 16:(c + 1) * 16].reshape(64, D)
        xin = f(np.concatenate([halo, own, xs], axis=0))
        maps.append({
            "wst": wstream, "xin": xin, "memin": f(I["mem_prompt"][b]),
            "kcin": f(I["cache_mem_k"][:, c * 16:(c + 1) * 16].reshape(NL, 16, 256, D)),
            "vcin": f(I["cache_mem_v"][:, c * 16:(c + 1) * 16].reshape(NL, 16, 256, D)),
            "scin": f(I["state_conv"][:, c * 16:(c + 1) * 16]), "spin": f(I["state_pool"][:, c * 16:(c + 1) * 16]),
            "pvin": pvh, "lnvin": lnv, "wstin": wstT, "wssin": wss, "maskin": mask, "bsin": bs, "pwin": pw, "identin": ident,
        })
    return maps


def assemble(results):
    y_prompt = np.zeros((4, 2048, D), np.float32)
    y_sample = np.zeros((128, 4, D), np.float32)
    mk = np.zeros((NL, 4, 256, 4, 512), np.float32)
    mv = np.zeros((NL, 4, 256, 4, 512), np.float32)
    cp = np.zeros((NL, 4, 30, 1024), np.float32)
    pp = np.zeros((NL, 4, 15, 1024), np.float32)
    cs = np.zeros((NL, 128, 30, 1024), np.float32)
    pps = np.zeros((NL, 128, 15, 1024), np.float32)
    cv = np.zeros((NL, 128, 4, 1024), np.float32)
    for c, r in enumerate(results):
        b, half = c // 2, c % 2
        y_prompt[b, half * 1024:(half + 1) * 1024] = r["y_o"][:1024]
        y_sample[c * 16:(c + 1) * 16] = r["y_o"][1024:1088].reshape(16, 4, D)
        if half == 0:
            mk[:, b] = r["mk_o"].reshape(NL, 256, 4, 512)
            mv[:, b] = r["mv_o"].reshape(NL, 256, 4, 512)
        else:
            cp[:, b] = r["cp_o"]
            pp[:, b] = r["pp_o"]
        cs[:, c * 16:(c + 1) * 16] = r["cs_o"]
        pps[:, c * 16:(c + 1) * 16] = r["ps_o"]
        cv[:, c * 16:(c + 1) * 16] = r["cv_o"].reshape(NL, 16, 4, 1024)
    return (y_prompt, y_sample, mk, mv, cp, pp, cs, pps, cv)


def kernel(**inputs):
    maps = make_in_maps(inputs)
    nc = build()
    res = run_bass_kernel_spmd(nc, maps, core_ids=list(range(8)))
    return assemble(res.results)
```
